# Optimizing a Trainium2 kernel written in Bass

```python
import jax, jax.numpy as jnp
from jax import lax
import numpy as np

D_MODEL = 1024
BATCH = 2
SEQ = 16384
DEPTH = 4

N_MIXERS = 2
D_MIX = D_MODEL
MEM_LEN = 256
MEM_HEADS = 4
MEM_HEAD_DIM = 64
MEM_WIDTH = MEM_HEADS * MEM_HEAD_DIM
TOK_WIDTH = D_MIX - MEM_WIDTH
LRU_WIDTH = TOK_WIDTH
LRU_BLOCKS = 8
LRU_BLOCK_DIM = LRU_WIDTH // LRU_BLOCKS
CONV_WIDTH = 4
CONV_PAD = (2, 1)
LRU_C = 8.0
MLA_HEADS = 12
QK_NOPE_DIM = 64
QK_ROPE_DIM = 32
QK_DIM = QK_NOPE_DIM + QK_ROPE_DIM
V_HEAD_DIM = TOK_WIDTH // MLA_HEADS
Q_LORA_RANK = 384
KV_LORA_RANK = 256
ROPE_THETA = 10000.0
Q_BLOCK = 128
D_FF = 2816
EPS = 1e-6
N_LRU_LAYERS = (DEPTH + 1) // 2
N_MLA_LAYERS = DEPTH // 2
LRU_IN_WIDTH = 2 * LRU_WIDTH + MEM_WIDTH
MLA_TOK_IN_WIDTH = Q_LORA_RANK + KV_LORA_RANK + QK_ROPE_DIM
MLA_IN_WIDTH = MLA_TOK_IN_WIDTH + MEM_WIDTH

kernel_name = 'hybrid_rglru_mla_macaron_encoder'


def rms_norm(x, g):
    xf = x.astype(jnp.float32)
    y = xf * lax.rsqrt(jnp.mean(xf * xf, axis=-1, keepdims=True) + EPS)
    return (y * g.astype(jnp.float32)).astype(x.dtype)


def swiglu_ffn(x, w_gate_up, w_down):
    g, u = jnp.split(x @ w_gate_up, [D_FF], axis=-1)
    return (jax.nn.silu(g) * u) @ w_down


def rope_tables(positions):
    half = QK_ROPE_DIM // 2
    inv_freq = ROPE_THETA ** (-jnp.arange(half, dtype=jnp.float32) * (2.0 / QK_ROPE_DIM))
    ang = positions.astype(jnp.float32)[..., None] * inv_freq
    return jnp.cos(ang)[:, :, None, :], jnp.sin(ang)[:, :, None, :]


def apply_rope(x, cos, sin):
    half = QK_ROPE_DIM // 2
    xf = x.astype(jnp.float32)
    x1, x2 = xf[..., :half], xf[..., half:]
    return jnp.concatenate([x1 * cos - x2 * sin, x2 * cos + x1 * sin], axis=-1).astype(x.dtype)


def memory_attention(q_in, mem_n, w_mem_kv, q_gain, k_gain):
    B, S, _ = q_in.shape
    q = rms_norm(q_in.reshape(B, S, MEM_HEADS, MEM_HEAD_DIM), q_gain)
    k, v = jnp.split(mem_n @ w_mem_kv, [MEM_WIDTH], axis=-1)
    k = rms_norm(k.reshape(B, -1, MEM_HEADS, MEM_HEAD_DIM), k_gain)
    v = v.reshape(B, -1, MEM_HEADS, MEM_HEAD_DIM)
    s = jnp.einsum('bshd,bmhd->bhsm', q.astype(jnp.float32), k.astype(jnp.float32)) * (MEM_HEAD_DIM ** -0.5)
    p = jax.nn.softmax(s, axis=-1).astype(v.dtype)
    return jnp.einsum('bhsm,bmhd->bshd', p, v).reshape(B, S, MEM_WIDTH)


def _linear_combine(left, right):
    a1, b1 = left
    a2, b2 = right
    return a1 * a2, a2 * b1 + b2


def rglru_direction(xc, gate_w, gate_b, lam, reverse):
    B, S, _ = xc.shape
    xb = xc.reshape(B, S, LRU_BLOCKS, LRU_BLOCK_DIM)
    gates = jnp.einsum('bsnk,gnkj->gbsnj', xb, gate_w).reshape(2, B, S, LRU_WIDTH)
    gates = gates.astype(jnp.float32) + gate_b.astype(jnp.float32)[:, None, None, :]
    r_gate = jax.nn.sigmoid(gates[0])
    i_gate = jax.nn.sigmoid(gates[1])
    log_a = -LRU_C * r_gate * jax.nn.softplus(-lam.astype(jnp.float32))
    a = jnp.exp(log_a)
    b = jnp.sqrt(-jnp.expm1(2.0 * log_a)) * (i_gate * xc.astype(jnp.float32))
    _, h = lax.associative_scan(_linear_combine, (a, b), reverse=reverse, axis=1)
    return h


def rglru_mixer(u, conv_w, conv_b, gate_w, gate_b, lam):
    gate_branch, xr = jnp.split(u, [LRU_WIDTH], axis=-1)
    xc = lax.conv_general_dilated(
        xr, conv_w[:, None, :].astype(xr.dtype), window_strides=(1,), padding=[CONV_PAD],
        dimension_numbers=('NWC', 'WIO', 'NWC'), feature_group_count=LRU_WIDTH) + conv_b
    h = (rglru_direction(xc, gate_w[0], gate_b[0], lam[0], False)
         + rglru_direction(xc, gate_w[1], gate_b[1], lam[1], True))
    return h.astype(u.dtype) * jax.nn.gelu(gate_branch)


def dense_attention(q, k, v):
    B, S, H, D = q.shape
    nb = S // Q_BLOCK
    qb = q.reshape(B, nb, Q_BLOCK, H, D).transpose(1, 0, 2, 3, 4)
    kf = k.astype(jnp.float32)
    scale = QK_DIM ** -0.5

    def one_block(q_blk):
        s = jnp.einsum('bqhd,bkhd->bhqk', q_blk.astype(jnp.float32), kf) * scale
        p = jax.nn.softmax(s, axis=-1).astype(v.dtype)
        return jnp.einsum('bhqk,bkhv->bqhv', p, v)

    o = lax.map(one_block, qb)
    return o.transpose(1, 0, 2, 3, 4).reshape(B, S, H * V_HEAD_DIM)


def mla_mixer(u, q_a_norm, w_uq, kv_a_norm, w_ukv, q_norm, k_norm, cos, sin):
    B, S, _ = u.shape
    c_q, c_kv, k_rope = jnp.split(u, [Q_LORA_RANK, Q_LORA_RANK + KV_LORA_RANK], axis=-1)
    q = (rms_norm(c_q, q_a_norm) @ w_uq).reshape(B, S, MLA_HEADS, QK_DIM)
    kv = (rms_norm(c_kv, kv_a_norm) @ w_ukv).reshape(B, S, MLA_HEADS, QK_NOPE_DIM + V_HEAD_DIM)
    k_nope, v = jnp.split(kv, [QK_NOPE_DIM], axis=-1)
    k_rope = jnp.broadcast_to(k_rope[:, :, None, :], (B, S, MLA_HEADS, QK_ROPE_DIM))
    k = jnp.concatenate([k_nope, k_rope], axis=-1)
    q = rms_norm(q, q_norm)
    k = rms_norm(k, k_norm)
    q = jnp.concatenate([q[..., :QK_NOPE_DIM], apply_rope(q[..., QK_NOPE_DIM:], cos, sin)], axis=-1)
    k = jnp.concatenate([k[..., :QK_NOPE_DIM], apply_rope(k[..., QK_NOPE_DIM:], cos, sin)], axis=-1)
    return dense_attention(q, k, v)


def setup_inputs(seed: int = 0) -> dict:
    key = jax.random.key(seed)
    ks = iter(jax.random.split(key, 40))

    def w(shape, fan_in):
        return jax.random.normal(next(ks), shape, jnp.float32) * (fan_in ** -0.5)

    def gain(shape):
        return 1.0 + 0.02 * jax.random.normal(next(ks), shape, jnp.float32)

    def bias(shape):
        return 0.01 * jax.random.normal(next(ks), shape, jnp.float32)

    x = jax.random.normal(next(ks), (BATCH, SEQ, D_MODEL), jnp.float32)
    mem = jax.random.normal(next(ks), (BATCH, MEM_LEN, D_MODEL), jnp.float32)
    positions = jnp.broadcast_to(jnp.arange(SEQ, dtype=jnp.int32), (BATCH, SEQ))
    a_c = jax.random.uniform(next(ks), (N_LRU_LAYERS, 2, LRU_WIDTH), jnp.float32, 0.9, 0.999)
    a0 = a_c ** (1.0 / LRU_C)
    lru_lambda = jnp.log(a0) - jnp.log1p(-a0)
    return {
        'x': x,
        'mem': mem,
        'positions': positions,
        'ffn1_norm': gain((DEPTH, D_MODEL)),
        'ffn1_w_gate_up': w((DEPTH, D_MODEL, 2 * D_FF), D_MODEL),
        'ffn1_w_down': w((DEPTH, D_FF, D_MODEL), D_FF),
        'mix_norm': gain((DEPTH, D_MODEL)),
        'mem_norm': gain((DEPTH, D_MODEL)),
        'w_mem_kv': w((DEPTH, D_MODEL, 2 * MEM_WIDTH), D_MODEL),
        'mem_q_norm': gain((DEPTH, MEM_HEAD_DIM)),
        'mem_k_norm': gain((DEPTH, MEM_HEAD_DIM)),
        'w_out': w((DEPTH, D_MIX, D_MODEL), D_MIX),
        'ffn2_norm': gain((DEPTH, D_MODEL)),
        'ffn2_w_gate_up': w((DEPTH, D_MODEL, 2 * D_FF), D_MODEL),
        'ffn2_w_down': w((DEPTH, D_FF, D_MODEL), D_FF),
        'lru_w_in': w((N_LRU_LAYERS, D_MODEL, LRU_IN_WIDTH), D_MODEL),
        'lru_conv_w': w((N_LRU_LAYERS, CONV_WIDTH, LRU_WIDTH), CONV_WIDTH),
        'lru_conv_b': bias((N_LRU_LAYERS, LRU_WIDTH)),
        'lru_gate_w': w((N_LRU_LAYERS, 2, 2, LRU_BLOCKS, LRU_BLOCK_DIM, LRU_BLOCK_DIM), LRU_BLOCK_DIM),
        'lru_gate_b': bias((N_LRU_LAYERS, 2, 2, LRU_WIDTH)),
        'lru_lambda': lru_lambda,
        'mla_w_in': w((N_MLA_LAYERS, D_MODEL, MLA_IN_WIDTH), D_MODEL),
        'mla_q_a_norm': gain((N_MLA_LAYERS, Q_LORA_RANK)),
        'mla_w_uq': w((N_MLA_LAYERS, Q_LORA_RANK, MLA_HEADS * QK_DIM), Q_LORA_RANK),
        'mla_kv_a_norm': gain((N_MLA_LAYERS, KV_LORA_RANK)),
        'mla_w_ukv': w((N_MLA_LAYERS, KV_LORA_RANK, MLA_HEADS * (QK_NOPE_DIM + V_HEAD_DIM)), KV_LORA_RANK),
        'mla_q_norm': gain((N_MLA_LAYERS, QK_DIM)),
        'mla_k_norm': gain((N_MLA_LAYERS, QK_DIM)),
    }


def reference(x, mem, positions, ffn1_norm, ffn1_w_gate_up, ffn1_w_down, mix_norm, mem_norm,
              w_mem_kv, mem_q_norm, mem_k_norm, w_out, ffn2_norm, ffn2_w_gate_up, ffn2_w_down,
              lru_w_in, lru_conv_w, lru_conv_b, lru_gate_w, lru_gate_b, lru_lambda,
              mla_w_in, mla_q_a_norm, mla_w_uq, mla_kv_a_norm, mla_w_ukv, mla_q_norm, mla_k_norm):
    cos, sin = rope_tables(positions)
    for layer in range(DEPTH):
        x = x + 0.5 * swiglu_ffn(rms_norm(x, ffn1_norm[layer]), ffn1_w_gate_up[layer], ffn1_w_down[layer])
        h = rms_norm(x, mix_norm[layer])
        mem_n = rms_norm(mem, mem_norm[layer])
        j = layer // N_MIXERS
        if layer % N_MIXERS == 0:
            u = h @ lru_w_in[j]
            u_tok, u_mem = jnp.split(u, [2 * LRU_WIDTH], axis=-1)
            tok = rglru_mixer(u_tok, lru_conv_w[j], lru_conv_b[j], lru_gate_w[j],
                              lru_gate_b[j], lru_lambda[j])
        else:
            u = h @ mla_w_in[j]
            u_tok, u_mem = jnp.split(u, [MLA_TOK_IN_WIDTH], axis=-1)
            tok = mla_mixer(u_tok, mla_q_a_norm[j], mla_w_uq[j], mla_kv_a_norm[j], mla_w_ukv[j],
                            mla_q_norm[j], mla_k_norm[j], cos, sin)
        mem_out = memory_attention(u_mem, mem_n, w_mem_kv[layer], mem_q_norm[layer], mem_k_norm[layer])
        x = x + jnp.concatenate([tok, mem_out], axis=-1) @ w_out[layer]
        x = x + 0.5 * swiglu_ffn(rms_norm(x, ffn2_norm[layer]), ffn2_w_gate_up[layer], ffn2_w_down[layer])
    return x
```

```python
import numpy as np
import ml_dtypes
from contextlib import ExitStack
import concourse.bass as bass
import concourse.mybir as mybir
from concourse.bass_utils import run_bass_kernel_spmd

F32, BF16, I32 = mybir.dt.float32, mybir.dt.bfloat16, mybir.dt.int32
ALU = mybir.AluOpType
AF = mybir.ActivationFunctionType
NPBF = ml_dtypes.bfloat16

D = 1024
DFF = 2816
NJ = DFF // 128
EPS = 1e-6
NCORES = 8
PI = float(np.pi)


class Reg:
    __slots__ = ("name", "w", "rs", "guard", "sem")

    def __init__(self, name):
        self.name = name
        self.w = None
        self.rs = {}
        self.guard = []
        self.sem = None


class SemSlot:
    __slots__ = ("h", "count", "sw")

    def __init__(self, h, sw=False):
        self.h = h
        self.count = 0
        self.sw = sw


class Op:
    __slots__ = ("eng", "fn", "waits", "signal", "track", "idx", "ticket")


class Prog:
    ENGS = ["pe", "act", "dve", "pool", "sp"]

    def __init__(self):
        self.nc = bass.Bass("TRN2", target_bir_lowering=False)
        self.es = ExitStack()
        self.pes = None
        self.ops = {e: [] for e in self.ENGS}
        self.seen = {e: {} for e in self.ENGS}
        self.nreg = 0
        self.esem = {}
        self.all_regs = []
        self.free_slots = []
        self.phase_slots = []
        self.nslots = 0
        self.emitted = {e: 0 for e in self.ENGS}
        self.tbase = {e: 0 for e in self.ENGS}
        self.io = {}
        for e in ["pe", "act", "dve", "pool"]:
            self.esem[e] = self.es.enter_context(self.nc.semaphore(f"s_{e}"))

    def reg(self, name=None):
        self.nreg += 1
        r = Reg(name or f"r{self.nreg}")
        self.all_regs.append(r)
        return r

    def regs(self, n, name="r"):
        return [self.reg(f"{name}{i}") for i in range(n)]

    def sb(self, name, shape, dt):
        self.nreg += 1
        return self.pes.enter_context(self.nc.sbuf_tensor(f"s{self.nreg}_" + name, list(shape), dt))

    def ps(self, name, shape, dt=F32):
        self.nreg += 1
        return self.pes.enter_context(self.nc.psum_tensor(f"p{self.nreg}_" + name, list(shape), dt))

    def dram_in(self, name, shape, dt):
        if name in self.io:
            ap = self.io[name]
            assert list(ap.shape) == list(shape) and ap.dtype == dt, (name, ap.shape, shape)
            return ap
        return self.nc.dram_tensor(name, list(shape), dt, kind="ExternalInput").ap()

    def dram_out(self, name, shape, dt):
        if name in self.io:
            ap = self.io[name]
            assert list(ap.shape) == list(shape) and ap.dtype == dt, (name, ap.shape, shape)
            return ap
        return self.nc.dram_tensor(name, list(shape), dt, kind="ExternalOutput").ap()

    def ext_in(self, name, shape, dt):
        return self.nc.dram_tensor(name, list(shape), dt, kind="ExternalInput").ap()

    def ext_out(self, name, shape, dt):
        return self.nc.dram_tensor(name, list(shape), dt, kind="ExternalOutput").ap()

    def dram_tmp(self, name, shape, dt):
        if name in self.io:
            return self.io[name]
        return self.nc.dram_tensor(name, list(shape), dt).ap()

    def _slot(self, reg, q):
        if reg.sem is None:
            if q != "pool" and self.free_slots:
                reg.sem = self.free_slots.pop()
            else:
                self.nslots += 1
                reg.sem = SemSlot(self.es.enter_context(self.nc.semaphore(f"d{self.nslots}")), sw=(q == "pool"))
            self.phase_slots.append(reg.sem)
        assert q != "pool" or reg.sem.sw, f"software DMA onto a recycled semaphore ({reg.name})"
        return reg.sem

    def _add(self, eng, fn, reads, writes, track=None):
        op = Op()
        op.eng, op.fn, op.signal, op.track = eng, fn, False, track
        lst = self.ops[eng]
        op.idx = len(lst)
        waits = []
        for r in reads:
            if r.w is not None:
                waits.append(r.w)
        for r in writes:
            joining = (track is not None and r.w is not None and r.w[0] == "d"
                       and r.w[1] is track.sem and not r.rs)
            if joining:
                waits.extend(r.guard)
            else:
                g = ([r.w] if r.w is not None else []) + list(r.rs.values())
                r.guard = g
                waits.extend(g)
        if track is not None:
            slot = self._slot(track, eng)
            op.track = slot
            slot.count += 1
            tok = ("d", slot, slot.count)
            key = ("d", id(slot))
        else:
            tok = ("c", eng, op.idx)
            key = ("c", eng)
        for r in reads:
            r.rs[key] = tok
        for r in writes:
            r.w = tok
            r.rs = {}
        op.waits = self._filter_waits(eng, waits)
        lst.append(op)
        return op

    def _filter_waits(self, eng, waits):
        final = []
        seen = self.seen[eng]
        for t in waits:
            if t[0] == "c":
                if t[1] == eng and eng == "pe":
                    continue
                k = t[1]
            else:
                k = id(t[1])
            if seen.get(k, -1) >= t[2]:
                continue
            seen[k] = t[2]
            final.append(t)
            if t[0] == "c":
                self.ops[t[1]][t[2]].signal = True
        return final

    def dma(self, q, out, in_, reads, writes, track=None, **kw):
        tr = track if track is not None else writes[0]
        return self._add(q, lambda e: e.dma_start(out=out, in_=in_, **kw), reads, writes, track=tr)

    def mm(self, out, lhsT, rhs, start, stop, reads, writes):
        return self._add("pe", lambda e: e.matmul(out, lhsT=lhsT, rhs=rhs, start=start, stop=stop), reads, writes)

    def act(self, out, in_, func, reads, writes, bias=None, scale=None, eng="act"):
        kw = {}
        if bias is not None:
            kw["bias"] = bias
        if scale is not None:
            kw["scale"] = scale
        return self._add(eng, lambda e: e.activation(out=out, in_=in_, func=func, **kw), reads, writes)

    def tt(self, eng, out, in0, in1, op, reads, writes):
        return self._add(eng, lambda e: e.tensor_tensor(out=out, in0=in0, in1=in1, op=op), reads, writes)

    def ts(self, eng, out, in0, s1, s2, op0, op1, reads, writes):
        if op1 is None:
            return self._add(eng, lambda e: e.tensor_scalar(out=out, in0=in0, scalar1=s1, scalar2=None, op0=op0), reads, writes)
        return self._add(eng, lambda e: e.tensor_scalar(out=out, in0=in0, scalar1=s1, scalar2=s2, op0=op0, op1=op1), reads, writes)

    def stt(self, eng, out, in0, scalar, in1, op0, op1, reads, writes):
        return self._add(eng, lambda e: e.scalar_tensor_tensor(out=out, in0=in0, scalar=scalar, in1=in1, op0=op0, op1=op1), reads, writes)

    def copy(self, eng, out, in_, reads, writes):
        if eng == "act":
            return self._add(eng, lambda e: e.copy(out=out, in_=in_), reads, writes)
        return self._add(eng, lambda e: e.tensor_copy(out=out, in_=in_), reads, writes)

    def memset(self, eng, ap, val, writes):
        return self._add(eng, lambda e: e.memset(ap, val), [], writes)

    def scan(self, out, d0, d1, initial, reads, writes):
        return self._add("dve", lambda e: e.tensor_tensor_scan(out=out, data0=d0, data1=d1, initial=initial,
                                                               op0=ALU.mult, op1=ALU.add), reads, writes)

    def recip(self, out, in_, reads, writes):
        return self._add("dve", lambda e: e.reciprocal(out=out, in_=in_), reads, writes)

    def begin_phase(self):
        self.pes = ExitStack()

    def end_phase(self):
        toks = []
        for e in ["pe", "act", "dve", "pool"]:
            lst = self.ops[e]
            for i in range(len(lst) - 1, self.emitted[e] - 1, -1):
                if lst[i].track is None and lst[i].fn is not None:
                    toks.append(("c", e, i))
                    break
        for sl in self.phase_slots:
            toks.append(("d", sl, sl.count))
        for e in self.ENGS:
            op = Op()
            op.eng, op.fn, op.signal, op.track = e, None, False, None
            op.idx = len(self.ops[e])
            op.waits = self._filter_waits(e, [t for t in toks if not (t[0] == "c" and t[1] == e)])
            self.ops[e].append(op)
        self._emit_slice()
        self.pes.close()
        self.pes = None
        for r in self.all_regs:
            r.w, r.rs, r.guard = None, {}, []
            r.sem = None
        self.all_regs = [r for r in self.all_regs if getattr(r, "name", "").startswith("DR_")]
        self.free_slots.extend(sl for sl in self.phase_slots if not sl.sw)
        self.phase_slots = []

    def _emit_slice(self):
        nc = self.nc
        for e in ["pe", "act", "dve", "pool"]:
            t = self.tbase[e]
            for op in self.ops[e][self.emitted[e]:]:
                if op.signal:
                    t += 1
                op.ticket = t
            self.tbase[e] = t
        prog = self
        start = dict(self.emitted)

        def run(name, engine):
            for op in prog.ops[name][start[name]:]:
                for t in op.waits:
                    if t[0] == "c":
                        engine.wait_ge(prog.esem[t[1]], prog.ops[t[1]][t[2]].ticket)
                    else:
                        engine.wait_ge(t[1].h, 16 * t[2])
                if op.fn is None:
                    continue
                ins = op.fn(engine)
                if ins is None:
                    continue
                if op.track is not None:
                    ins.then_inc(op.track.h, 16)
                elif op.signal:
                    ins.then_inc(prog.esem[name], 1)

        with nc.Block() as block:
            @block.tensor
            def _(e):
                run("pe", e)

            @block.scalar
            def _(e):
                run("act", e)

            @block.vector
            def _(e):
                run("dve", e)

            @block.gpsimd
            def _(e):
                run("pool", e)

            @block.sync
            def _(e):
                run("sp", e)
        for e in self.ENGS:
            self.emitted[e] = len(self.ops[e])

    def finish(self):
        self.es.close()
        return self.nc


class Ctx:
    pass


def setup_common(p, TOK, SBK):
    c = Ctx()
    c.TOK, c.SBK, c.NSB = TOK, SBK, TOK // SBK
    c.NH = SBK // 512
    c.ones = p.sb("ones", [128, 128], BF16)
    c.r_ones = p.reg("ones")
    p.memset("dve", c.ones[:], 1.0, [c.r_ones])
    c.eps = p.sb("eps", [128, 1], F32)
    c.r_eps = p.reg("eps")
    p.memset("dve", c.eps[:], EPS, [c.r_eps])
    c.x = p.sb("x_sb", [128, 8, SBK], F32)
    c.r_x = p.regs(8, "x")
    c.h = p.sb("h_sb", [128, 8, SBK], BF16)
    c.r_h = p.reg("h")
    c.sq = p.sb("sq_sb", [128, 8, 512], BF16)
    c.r_sq = p.reg("sq")
    c.rstd = p.sb("rstd_sb", [128, 512], F32)
    c.r_rstd = p.reg("rstd")
    c.ps_ss = p.ps("ps_ss", [128, 512])
    c.r_ps_ss = p.reg("ps_ss")
    return c


def load_x(p, c, xT, sbi, q="sp"):
    SBK = c.SBK
    for ch in range(8):
        p.dma(q, c.x[:, ch, :], xT[ch * 128:(ch + 1) * 128, sbi * SBK:(sbi + 1) * SBK], [], [c.r_x[ch]])


def store_x(p, c, xTo, r_out, sbi, q="sp"):
    SBK = c.SBK
    for ch in range(8):
        p.dma(q, xTo[ch * 128:(ch + 1) * 128, sbi * SBK:(sbi + 1) * SBK], c.x[:, ch, :], [c.r_x[ch]], [], track=c.r_x[ch])


def rstd_from_ps(p, c, out, ps, dim, reads, writes):
    p.act(out, ps, AF.Sqrt, reads + [c.r_eps], writes, bias=c.eps[0:out.shape[0], :], scale=1.0 / dim)
    p.recip(out, out, writes, writes)


def rmsnorm_x(p, c, g_sb, r_g):
    for hf in range(c.NH):
        sl = slice(hf * 512, (hf + 1) * 512)
        for ch in range(8):
            p.act(c.sq[:, ch, :], c.x[:, ch, sl], AF.Square, [c.r_x[ch]], [c.r_sq])
        for ch in range(8):
            p.mm(c.ps_ss[:], c.ones[:], c.sq[:, ch, :], ch == 0, ch == 7, [c.r_ones, c.r_sq], [c.r_ps_ss])
        rstd_from_ps(p, c, c.rstd[:], c.ps_ss[:], D, [c.r_ps_ss], [c.r_rstd])
        for ch in range(8):
            eng = "dve"
            p.stt(eng, c.h[:, ch, sl], c.x[:, ch, sl], g_sb[:, ch:ch + 1], c.rstd[:], ALU.mult, ALU.mult,
                  [c.r_x[ch], r_g, c.r_rstd], [c.r_h])


def load_gain(p, name, src, ncol, mult):
    t = p.sb(name, [128, ncol], F32)
    r = p.reg(name)
    p.dma("sp", t[:], src, [], [r])
    if mult != 1.0:
        p.ts("dve", t[:], t[:], float(mult), None, ALU.mult, None, [r], [r])
    return t, r


class FFN:
    def __init__(self, p, c, tag, w_gu, w_down, g_dram):
        self.p, self.c = p, c
        SBK = c.SBK
        self.wg_s = p.dram_tmp(f"wg_s{tag}", [11, 128, 8, 256], BF16)
        self.wu_s = p.dram_tmp(f"wu_s{tag}", [11, 128, 8, 256], BF16)
        r_cast = p.reg(f"wcast{tag}")
        self.r_wgs = [r_cast] * 11
        self.r_wus = [r_cast] * 11
        wv = w_gu.rearrange("(c p) n -> p c n", p=128)
        for s in range(11):
            p.dma("pool", self.wg_s[s], wv[:, :, s * 256:(s + 1) * 256], [], [self.r_wgs[s]])
            p.dma("pool", self.wu_s[s], wv[:, :, DFF + s * 256:DFF + (s + 1) * 256], [], [self.r_wus[s]])
        self.g, self.r_g = load_gain(p, f"g{tag}", g_dram, 8, 1.0)
        self.wd = p.sb(f"wd{tag}", [128, NJ, D], BF16)
        self.r_wd = p.reg(f"wd{tag}")
        wdv = w_down.rearrange("(j p) n -> p j n", p=128)
        for j0 in range(0, NJ, 2):
            p.dma("pool", self.wd[:, j0:j0 + 2, :], wdv[:, j0:j0 + 2, :], [], [self.r_wd])

    @staticmethod
    def alloc_shared(p, c):
        s = Ctx()
        s.wg = [p.sb(f"wg{i}", [128, 8, 256], BF16) for i in range(2)]
        s.wu = [p.sb(f"wu{i}", [128, 8, 256], BF16) for i in range(2)]
        s.r_wg = p.regs(2, "wg")
        s.r_wu = p.regs(2, "wu")
        s.actT = p.sb("actT", [128, NJ, c.SBK], BF16)
        s.r_act = p.regs(NJ, "act")
        s.ps_g = [p.ps(f"ps_g{i}", [128, 512]) for i in range(2)]
        s.ps_u = [p.ps(f"ps_u{i}", [128, 512]) for i in range(2)]
        s.r_psg = p.regs(2, "psg")
        s.r_psu = p.regs(2, "psu")
        s.sg = [p.sb(f"sg{i}", [128, 512], F32) for i in range(2)]
        s.r_sg = p.regs(2, "sg")
        s.ps_y = [p.ps(f"ps_y{i}", [128, 512]) for i in range(2)]
        s.r_psy = p.regs(2, "psy")
        s.k = 0
        s.ky = 0
        return s

    def run(self, s, extra=None):
        p, c = self.p, self.c
        rmsnorm_x(p, c, self.g, self.r_g)
        for sl_i in range(11):
            b = sl_i % 2
            xg, xu = (extra if (extra is not None and sl_i == 0) else ([], []))
            p.dma("sp", s.wg[b][:], self.wg_s[sl_i], [self.r_wgs[sl_i]], [s.r_wg[b]] + xg)
            p.dma("sp", s.wu[b][:], self.wu_s[sl_i], [self.r_wus[sl_i]], [s.r_wu[b]] + xu)
            for jj in range(2):
                j = sl_i * 2 + jj
                for hf in range(c.NH):
                    sl = slice(hf * 512, (hf + 1) * 512)
                    k = s.k % 2
                    s.k += 1
                    for ch in range(8):
                        p.mm(s.ps_g[k][:], s.wg[b][:, ch, jj * 128:(jj + 1) * 128], c.h[:, ch, sl], ch == 0, ch == 7,
                             [s.r_wg[b], c.r_h], [s.r_psg[k]])
                    for ch in range(8):
                        p.mm(s.ps_u[k][:], s.wu[b][:, ch, jj * 128:(jj + 1) * 128], c.h[:, ch, sl], ch == 0, ch == 7,
                             [s.r_wu[b], c.r_h], [s.r_psu[k]])
                    p.act(s.sg[k][:], s.ps_g[k][:], AF.Silu, [s.r_psg[k]], [s.r_sg[k]])
                    p.tt("dve", s.actT[:, j, sl], s.sg[k][:], s.ps_u[k][:], ALU.mult, [s.r_sg[k], s.r_psu[k]], [s.r_act[j]])
        for o in range(8):
            for hf in range(c.NH):
                sl = slice(hf * 512, (hf + 1) * 512)
                k = s.ky % 2
                s.ky += 1
                for j in range(NJ):
                    p.mm(s.ps_y[k][:], self.wd[:, j, o * 128:(o + 1) * 128], s.actT[:, j, sl], j == 0, j == NJ - 1,
                         [self.r_wd, s.r_act[j]], [s.r_psy[k]])
                p.stt("dve", c.x[:, o, sl], s.ps_y[k][:], 0.5, c.x[:, o, sl], ALU.mult, ALU.add,
                      [s.r_psy[k], c.r_x[o]], [c.r_x[o]])


def load_w_resident(p, name, w, K, N, kc=128):
    nk = K // kc
    t = p.sb(name, [kc, nk, N], BF16)
    r = p.reg(name)
    wv = w.rearrange("(c p) n -> p c n", p=kc)
    step = max(1, 4096 // N)
    for c0 in range(0, nk, step):
        c1 = min(nk, c0 + step)
        p.dma("pool", t[:, c0:c1, :], wv[:, c0:c1, :], [], [r])
    return t, r


def build_a(p, TOK, kind, tag):
    SBK = min(1024, TOK)
    WIN = 1792 if kind == "lru" else 928
    NTC, TCP = (16, 96) if kind == "lru" else (6, 128)
    xT = p.dram_in("xT", [D, TOK], F32)
    g1 = p.dram_in("g1", [128, 8], F32)
    gm = p.dram_in("gm", [128, 8], F32)
    w_gu = p.dram_in("w_gu", [D, 2 * DFF], F32)
    w_dn = p.dram_in("w_dn", [DFF, D], F32)
    w_in = p.dram_in("w_in", [D, WIN], F32)
    xo = p.dram_out("xT_out", [D, TOK], F32)
    uo = p.dram_out("u_tok", [NTC, TCP, TOK], F32)
    umo = p.dram_out("u_mem", [2, 128, TOK], F32)
    r_xo, r_uo, r_umo = p.reg("xo"), p.reg("uo"), p.reg("umo")
    c = setup_common(p, TOK, SBK)
    ffn = FFN(p, c, "1" + tag, w_gu, w_dn, g1)
    fs = FFN.alloc_shared(p, c)
    gmt, r_gm = load_gain(p, "gm", gm, 8, 1.0)
    win, r_win = load_w_resident(p, "win", w_in, D, WIN)
    ps_p = [p.ps(f"ps_p{i}", [128, 512]) for i in range(1)]
    r_psp = p.regs(1, "psp")
    ust = [p.sb(f"ust{i}", [128, 512], F32) for i in range(2)]
    r_ust = p.regs(2, "ust")
    if kind == "lru":
        chunks = [(i * 96, 96) for i in range(16)] + [(1536, 128), (1664, 128)]
    else:
        chunks = [(i * 128, 128) for i in range(5)] + [(576, 96), (672, 128), (800, 128)]
    kk = 0
    for sbi in range(c.NSB):
        load_x(p, c, xT, sbi)
        ffn.run(fs)
        store_x(p, c, xo, r_xo, sbi)
        rmsnorm_x(p, c, gmt, r_gm)
        for ci, (cs, M) in enumerate(chunks):
            for hf in range(c.NH):
                ub = kk % 2
                kk += 1
                sl = slice(hf * 512, (hf + 1) * 512)
                for ch in range(8):
                    p.mm(ps_p[0][0:M, :], win[:, ch, cs:cs + M], c.h[:, ch, sl], ch == 0, ch == 7, [r_win, c.r_h], [r_psp[0]])
                p.copy("act", ust[ub][0:M, :], ps_p[0][0:M, :], [r_psp[0]], [r_ust[ub]])
                tsl = slice(sbi * SBK + hf * 512, sbi * SBK + (hf + 1) * 512)
                if ci < NTC:
                    p.dma("sp", uo[ci, 0:M, tsl], ust[ub][0:M, :], [r_ust[ub]], [], track=r_ust[ub])
                else:
                    p.dma("sp", umo[ci - NTC, :, tsl], ust[ub][0:128, :], [r_ust[ub]], [], track=r_ust[ub])


def build_lru_b(p, S, NB):
    CH = min(1024, S)
    NCH = S // CH
    NHF = CH // 512
    xr = p.dram_in("xr", [NB, 96, S], F32)
    gt = p.dram_in("gate", [NB, 96, S], F32)
    cw = p.dram_in("cw", [96, NB, 4], F32)
    cb = p.dram_in("cb", [96, NB], F32)
    gw = p.dram_in("gw", [96, NB, 4, 96], F32)
    gb = p.dram_in("gb", [96, NB, 4], F32)
    lam = p.dram_in("lam", [96, 2 * NB], F32)
    out = p.dram_out("tok", [NB, 96, S], BF16)
    r_out = p.reg("out")

    def small(name, src, shape, q="sp"):
        t = p.sb(name, shape, F32)
        r = p.reg(name)
        p.dma(q, t[:], src, [], [r])
        return t, r
    cw_t, r_cw = small("cw", cw, [96, NB, 4])
    cb_t, r_cb = small("cb", cb, [96, NB])
    gb_t, r_gb = small("gb", gb, [96, NB, 4])
    lam_t, r_lam = small("lam", lam, [96, 2 * NB])
    gw_t = p.sb("gw", [96, NB, 4, 96], BF16)
    r_gw = p.reg("gw")
    p.dma("pool", gw_t[:], gw, [], [r_gw])
    one = p.sb("one", [128, 1], F32)
    r_one = p.reg("one")
    p.memset("dve", one[:], 1.0, [r_one])
    m8 = p.sb("m8", [96, 2 * NB], F32)
    r_m8 = p.reg("m8")
    p.act(m8[:], lam_t[:], AF.Exp, [r_lam], [r_m8], scale=-1.0)
    p.act(m8[:], m8[:], AF.Ln, [r_m8, r_one], [r_m8], bias=one[0:96, :])
    p.ts("dve", m8[:], m8[:], -8.0, None, ALU.mult, None, [r_m8], [r_m8])

    xc_full = p.sb("xc_full", [96, S], F32)
    hf_full = p.sb("hf_full", [96, S], F32)
    r_xc = p.regs(NCH, "xc")
    r_hf = p.regs(NCH, "hf")
    xr_t = [p.sb(f"xr_t{i}", [96, CH + 3], F32) for i in range(2)]
    r_xr = p.regs(2, "xr_t")
    g_t = [p.sb(f"g_t{i}", [96, CH], F32) for i in range(2)]
    r_gt = p.regs(2, "g_t")
    xcb = [p.sb(f"xcb{i}", [96, CH], BF16) for i in range(2)]
    r_xcb = p.regs(2, "xcb")
    ps_r = [p.ps(f"ps_r{i}", [128, CH]) for i in range(1)]
    ps_i = [p.ps(f"ps_i{i}", [128, CH]) for i in range(1)]
    r_psr = p.regs(1, "psr")
    r_psi = p.regs(1, "psi")

    def wt(name, dt=F32):
        return p.sb(name, [96, CH], dt), p.reg(name)
    rr, r_rr = wt("rr")
    ii, r_ii = wt("ii")
    aa, r_aa = wt("aa")
    qq, r_qq = wt("qq")
    hr = [wt(f"hr{i}") for i in range(2)]
    g2, r_g2 = wt("g2")
    ot = [wt(f"ot{i}", BF16) for i in range(2)]

    def gates(blk, d, xcb_t, r_xcb_t, xc_ap, r_xc_c):
        for hf in range(NHF):
            sl = slice(hf * 512, (hf + 1) * 512)
            p.mm(ps_r[0][0:96, sl], gw_t[:, blk, d * 2 + 0, :], xcb_t[:, sl], True, True, [r_gw, r_xcb_t], [r_psr[0]])
            p.mm(ps_i[0][0:96, sl], gw_t[:, blk, d * 2 + 1, :], xcb_t[:, sl], True, True, [r_gw, r_xcb_t], [r_psi[0]])
        p.act(rr[:], ps_r[0][0:96, :], AF.Sigmoid, [r_psr[0], r_gb], [r_rr], bias=gb_t[:, blk, d * 2:d * 2 + 1])
        p.act(ii[:], ps_i[0][0:96, :], AF.Sigmoid, [r_psi[0], r_gb], [r_ii], bias=gb_t[:, blk, d * 2 + 1:d * 2 + 2])
        p.act(aa[:], rr[:], AF.Exp, [r_rr, r_m8], [r_aa], scale=m8[:, blk * 2 + d:blk * 2 + d + 1])
        p.act(qq[:], aa[:], AF.Square, [r_aa], [r_qq])
        p.act(qq[:], qq[:], AF.Sqrt, [r_qq, r_one], [r_qq], bias=one[0:96, :], scale=-1.0)
        p.tt("dve", ii[:], ii[:], xc_ap, ALU.mult, [r_ii, r_xc_c], [r_ii])
        p.tt("dve", qq[:], qq[:], ii[:], ALU.mult, [r_qq, r_ii], [r_qq])

    k = 0
    for blk in range(NB):
        for ci in range(NCH):
            b = k % 2
            k += 1
            t0 = ci * CH
            lo = max(0, t0 - 2)
            hi = min(S, t0 + CH + 1)
            if ci == 0 or ci == NCH - 1:
                p.memset("pool", xr_t[b][:], 0.0, [r_xr[b]])
            p.dma("sp", xr_t[b][:, lo - (t0 - 2):hi - (t0 - 2)], xr[blk, :, lo:hi], [], [r_xr[b]])
            xc = xc_full[:, t0:t0 + CH]
            p.ts("dve", xc, xr_t[b][:, 0:CH], cw_t[:, blk, 0:1], cb_t[:, blk:blk + 1], ALU.mult, ALU.add,
                 [r_xr[b], r_cw, r_cb], [r_xc[ci]])
            for kk in range(1, 4):
                p.stt("dve", xc, xr_t[b][:, kk:kk + CH], cw_t[:, blk, kk:kk + 1], xc, ALU.mult, ALU.add,
                      [r_xr[b], r_cw, r_xc[ci]], [r_xc[ci]])
            p.copy("pool", xcb[b][:], xc, [r_xc[ci]], [r_xcb[b]])
            gates(blk, 0, xcb[b], r_xcb[b], xc, r_xc[ci])
            init = 0.0 if ci == 0 else hf_full[:, t0 - 1:t0]
            rd = [r_aa, r_qq] + ([] if ci == 0 else [r_hf[ci - 1]])
            p.scan(hf_full[:, t0:t0 + CH], aa[:], qq[:], init, rd, [r_hf[ci]])
        for ci in range(NCH - 1, -1, -1):
            b = k % 2
            k += 1
            t0 = ci * CH
            xc = xc_full[:, t0:t0 + CH]
            p.dma("sp", g_t[b][:], gt[blk, :, t0:t0 + CH], [], [r_gt[b]])
            p.copy("pool", xcb[b][:], xc, [r_xc[ci]], [r_xcb[b]])
            gates(blk, 1, xcb[b], r_xcb[b], xc, r_xc[ci])
            hrt, r_hrt = hr[b]
            hrp, r_hrp = hr[1 - b]
            init = 0.0 if ci == NCH - 1 else hrp[:, 0:1]
            rd = [r_aa, r_qq] + ([] if ci == NCH - 1 else [r_hrp])
            p.scan(hrt[:, ::-1], aa[:, ::-1], qq[:, ::-1], init, rd, [r_hrt])
            p.tt("pool", g2[:], g_t[b][:], g_t[b][:], ALU.mult, [r_gt[b]], [r_g2])
            p.ts("pool", g2[:], g2[:], 0.044715, 1.0, ALU.mult, ALU.add, [r_g2], [r_g2])
            p.tt("pool", g2[:], g2[:], g_t[b][:], ALU.mult, [r_g2, r_gt[b]], [r_g2])
            p.act(g2[:], g2[:], AF.Sigmoid, [r_g2], [r_g2], scale=1.5957691216057308)
            p.tt("pool", g2[:], g2[:], g_t[b][:], ALU.mult, [r_g2, r_gt[b]], [r_g2])
            p.tt("pool", rr[:], hrt[:], hf_full[:, t0:t0 + CH], ALU.add, [r_hrt, r_hf[ci]], [r_rr])
            ott, r_ott = ot[b]
            p.tt("dve", ott[:], rr[:], g2[:], ALU.mult, [r_rr, r_g2], [r_ott])
            p.dma("sp", out[blk, :, t0:t0 + CH], ott[:], [r_ott], [], track=r_ott)


def build_c(p, TOK, kind, tag):
    SBK = min(1024, TOK)
    KC, NK = (96, 8) if kind == "lru" else (128, 6)
    xT = p.dram_in("xT", [D, TOK], F32)
    umem = p.dram_in("u_mem", [2, 128, TOK], F32)
    tok = p.dram_in("tok", [NK, KC, TOK], BF16)
    memT = p.dram_in("memT", [D, 256], F32)
    gmem = p.dram_in("gmem", [128, 8], F32)
    wkv = p.dram_in("w_mem_kv", [D, 512], F32)
    gq = p.dram_in("gq", [128, 1], F32)
    gk = p.dram_in("gk", [128, 1], F32)
    w_out = p.dram_in("w_out", [D, D], F32)
    g2 = p.dram_in("g2", [128, 8], F32)
    w_gu = p.dram_in("w_gu", [D, 2 * DFF], F32)
    w_dn = p.dram_in("w_dn", [DFF, D], F32)
    xo = p.dram_out("xT_out", [D, TOK], F32)
    r_xo = p.reg("xo")
    c = setup_common(p, TOK, SBK)
    fs = FFN.alloc_shared(p, c)
    gmem_t, r_gmem = load_gain(p, "gmem", gmem, 8, 1.0)
    gq_t, r_gq = load_gain(p, "gq", gq, 1, 1.0)
    gk_t, r_gk = load_gain(p, "gk", gk, 1, 1.0)
    bd = p.sb("bd", [128, 128], BF16)
    r_bd = p.reg("bd")
    p.memset("dve", bd[:], 0.0, [r_bd])
    p.memset("dve", bd[0:64, 0:64], 1.0, [r_bd])
    p.memset("dve", bd[64:128, 64:128], 1.0, [r_bd])
    KmT = p.sb("KmT", [128, 2, 256], BF16)
    r_KmT = p.reg("KmT")
    Vm = p.sb("Vm", [128, 2, 4, 128], BF16)
    r_Vm = p.reg("Vm")
    p.memset("pool", Vm[:], 1.0, [r_Vm])
    for ch in range(8):
        p.dma("sp", c.x[:, ch, 0:256], memT[ch * 128:(ch + 1) * 128, :], [], [c.r_x[ch]])
    wkv_v = wkv.rearrange("(c p) n -> p c n", p=128)
    r_wkvK, r_wkvV = p.reg("wkvK"), p.reg("wkvV")
    p.dma("pool", fs.wg[0][:], wkv_v[:, :, 0:256], [], [r_wkvK])
    p.dma("pool", fs.wu[0][:], wkv_v[:, :, 256:512], [], [r_wkvV])
    for ch in range(8):
        p.act(c.sq[:, ch, 0:256], c.x[:, ch, 0:256], AF.Square, [c.r_x[ch]], [c.r_sq])
    for ch in range(8):
        p.mm(c.ps_ss[:, 0:256], c.ones[:], c.sq[:, ch, 0:256], ch == 0, ch == 7, [c.r_ones, c.r_sq], [c.r_ps_ss])
    rstd_from_ps(p, c, c.rstd[:, 0:256], c.ps_ss[:, 0:256], D, [c.r_ps_ss], [c.r_rstd])
    for ch in range(8):
        p.stt("dve", c.h[:, ch, 0:256], c.x[:, ch, 0:256], gmem_t[:, ch:ch + 1], c.rstd[:, 0:256], ALU.mult, ALU.mult,
              [c.r_x[ch], r_gmem, c.r_rstd], [c.r_h])
    for cc in range(2):
        psk = fs.ps_g[cc]
        for ch in range(8):
            p.mm(psk[:, 0:256], fs.wg[0][:, ch, cc * 128:(cc + 1) * 128], c.h[:, ch, 0:256], ch == 0, ch == 7,
                 [r_wkvK, c.r_h], [fs.r_psg[cc]])
        p.act(c.sq[:, cc, 0:256], psk[:, 0:256], AF.Square, [fs.r_psg[cc]], [c.r_sq])
        p.mm(c.ps_ss[:, 0:256], bd[:], c.sq[:, cc, 0:256], True, True, [r_bd, c.r_sq], [c.r_ps_ss])
        rstd_from_ps(p, c, c.rstd[:, 0:256], c.ps_ss[:, 0:256], 64, [c.r_ps_ss], [c.r_rstd])
        p.stt("dve", KmT[:, cc, :], psk[:, 0:256], gk_t[:, 0:1], c.rstd[:, 0:256], ALU.mult, ALU.mult,
              [fs.r_psg[cc], r_gk, c.r_rstd], [r_KmT])
    for kt in range(2):
        psv = fs.ps_u[kt]
        for ch in range(8):
            p.mm(psv[:, 0:256], c.h[:, ch, kt * 128:(kt + 1) * 128], fs.wu[0][:, ch, :], ch == 0, ch == 7,
                 [r_wkvV, c.r_h], [fs.r_psu[kt]])
        for h in range(4):
            off = 0 if h % 2 == 0 else 64
            p.copy("dve", Vm[:, kt, h, off:off + 64], psv[:, h * 64:(h + 1) * 64], [fs.r_psu[kt]], [r_Vm])
    ffn = FFN(p, c, "2" + tag, w_gu, w_dn, g2)
    wo_t, r_wo_t = load_w_resident(p, "wo_t", w_out[0:768, :], 768, D, kc=KC)
    wo_m, r_wo_m = load_w_resident(p, "wo_m", w_out[768:1024, :], 256, D, kc=128)
    um_t = fs.sg
    qn = p.sb("qn", [128, 2, 512], BF16)
    r_qn = p.reg("qn")
    mixm = p.sb("mixm", [128, 2, 512], BF16)
    r_mixm = p.reg("mixm")
    mixt = p.sb("mixt", [KC, NK, 512], BF16)
    r_mixt = p.reg("mixt")
    pT = [p.sb(f"pT{i}", [128, 512], BF16) for i in range(2)]
    r_pT = p.regs(2, "pT")
    rec, r_rec = c.rstd, c.r_rstd
    ks = 0
    for sbi in range(c.NSB):
        load_x(p, c, xT, sbi)
        for hf in range(c.NH):
            sl = slice(hf * 512, (hf + 1) * 512)
            tsl = slice(sbi * SBK + hf * 512, sbi * SBK + (hf + 1) * 512)
            for cc in range(2):
                p.dma("sp", um_t[cc][:], umem[cc, :, tsl], [], [fs.r_sg[cc]])
            for k in range(NK):
                p.dma("sp", mixt[:, k, :], tok[k, :, tsl], [], [r_mixt])
            for cc in range(2):
                p.act(c.sq[:, cc, :], um_t[cc][:], AF.Square, [fs.r_sg[cc]], [c.r_sq])
                p.mm(c.ps_ss[:], bd[:], c.sq[:, cc, :], True, True, [r_bd, c.r_sq], [c.r_ps_ss])
                rstd_from_ps(p, c, c.rstd[:], c.ps_ss[:], 64, [c.r_ps_ss], [c.r_rstd])
                p.stt("dve", qn[:, cc, :], um_t[cc][:], gq_t[:, 0:1], c.rstd[:], ALU.mult, ALU.mult,
                      [fs.r_sg[cc], r_gq, c.r_rstd], [r_qn])
            for h in range(4):
                cc, base = h // 2, (h % 2) * 64
                pso = fs.ps_u[h % 2]
                r_pso = fs.r_psu[h % 2]
                for kt in range(2):
                    k = ks % 2
                    ks += 1
                    p.mm(fs.ps_g[k][:], KmT[base:base + 64, cc, kt * 128:(kt + 1) * 128], qn[base:base + 64, cc, :],
                         True, True, [r_KmT, r_qn], [fs.r_psg[k]])
                    p.act(pT[k][:], fs.ps_g[k][:], AF.Exp, [fs.r_psg[k]], [r_pT[k]], scale=0.125)
                    p.mm(pso[:], Vm[:, kt, h, :], pT[k][:], kt == 0, kt == 1, [r_Vm, r_pT[k]], [r_pso])
                if h % 2 == 0:
                    p.recip(rec[64:128, :], pso[64:128, :], [r_pso], [r_rec])
                    p.tt("dve", mixm[0:64, cc, :], pso[0:64, :], rec[64:128, :], ALU.mult, [r_pso, r_rec], [r_mixm])
                else:
                    p.recip(rec[0:64, :], pso[0:64, :], [r_pso], [r_rec])
                    p.tt("dve", mixm[64:128, cc, :], pso[64:128, :], rec[0:64, :], ALU.mult, [r_pso, r_rec], [r_mixm])
            for o in range(8):
                k = fs.ky % 2
                fs.ky += 1
                osl = slice(o * 128, (o + 1) * 128)
                for kk in range(NK):
                    p.mm(fs.ps_y[k][:], wo_t[:, kk, osl], mixt[:, kk, :], kk == 0, False, [r_wo_t, r_mixt], [fs.r_psy[k]])
                for cc in range(2):
                    p.mm(fs.ps_y[k][:], wo_m[:, cc, osl], mixm[:, cc, :], False, cc == 1, [r_wo_m, r_mixm], [fs.r_psy[k]])
                p.tt("dve", c.x[:, o, sl], fs.ps_y[k][:], c.x[:, o, sl], ALU.add, [fs.r_psy[k], c.r_x[o]], [c.r_x[o]])
        ffn.run(fs, extra=([r_wkvK], [r_wkvV]) if sbi == 0 else None)
        store_x(p, c, xo, r_xo, sbi)


def mla_consts():
    invf = np.zeros((96, 1), np.float32)
    inv = (10000.0 ** (-np.arange(16, dtype=np.float32) * np.float32(2.0 / 32))).astype(np.float32)
    invf[64:80, 0] = inv
    invf[80:96, 0] = inv
    sgn = np.zeros((96, 1), np.float32)
    sgn[64:80] = -1.0
    sgn[80:96] = 1.0
    cm = np.zeros((96, 1), np.float32)
    cm[64:96] = 1.0
    cb = np.zeros((96, 1), np.float32)
    cb[0:64] = 1.0
    Pm = np.zeros((96, 96), np.float32)
    for j in range(16):
        Pm[80 + j, 64 + j] = 1.0
        Pm[64 + j, 80 + j] = 1.0
    return {"invf": invf, "sgn": sgn, "cm": cm, "cb": cb, "Pm": Pm}


def build_mla_q(p, TOK):
    NHALF = TOK // 512
    ut = p.dram_in("u_tok", [6, 128, TOK], F32)
    pos = p.dram_in("pos", [1, TOK], I32)
    gqa = p.dram_in("gqa", [128, 3], F32)
    w_uq = p.dram_in("w_uq", [384, 1152], F32)
    gkva = p.dram_in("gkva", [128, 2], F32)
    w_ukv = p.dram_in("w_ukv", [256, 1536], F32)
    gqn = p.dram_in("gqn", [96, 1], F32)
    gkn = p.dram_in("gkn", [96, 1], F32)
    invf = p.dram_in("invf", [96, 1], F32)
    sgn = p.dram_in("sgn", [96, 1], F32)
    cm = p.dram_in("cm", [96, 1], F32)
    cb = p.dram_in("cb", [96, 1], F32)
    Pm = p.dram_in("Pm", [96, 96], F32)
    qTo = p.dram_out("qT", [12, 96, TOK], BF16)
    KTo = p.dram_out("KT", [12, 96, TOK], BF16)
    Vo = p.dram_out("Vaug", [12, 128, TOK // 128, 128], BF16)
    r_qo, r_ko, r_vo = p.reg("qo"), p.reg("ko"), p.reg("vo")
    c = Ctx()
    c.ones = p.sb("ones", [128, 128], BF16)
    c.r_ones = p.reg("ones")
    p.memset("dve", c.ones[:], 1.0, [c.r_ones])
    c.eps = p.sb("eps", [128, 1], F32)
    c.r_eps = p.reg("eps")
    p.memset("dve", c.eps[:], EPS, [c.r_eps])
    c.rstd = p.sb("rstd_sb", [128, 512], F32)
    c.r_rstd = p.reg("rstd")
    c.ps_ss = p.ps("ps_ss", [128, 512])
    c.r_ps_ss = p.reg("ps_ss")

    def small(name, src, shape):
        t = p.sb(name, shape, F32)
        r = p.reg(name)
        p.dma("sp", t[:], src, [], [r])
        return t, r
    gqa_t, r_gqa = small("gqa", gqa, [128, 3])
    gkva_t, r_gkva = small("gkva", gkva, [128, 2])
    gqn_t, r_gqn = small("gqn", gqn, [96, 1])
    gkn_t, r_gkn = small("gkn", gkn, [96, 1])
    invf_t, r_invf = small("invf", invf, [96, 1])
    sgn_t, r_sgn = small("sgn", sgn, [96, 1])
    cm_t, r_cm = small("cm", cm, [96, 1])
    cb_t, r_cb = small("cb", cb, [96, 1])
    Pm_t, r_Pm = small("Pm", Pm, [96, 96])
    wuq, r_wuq = load_w_resident(p, "wuq", w_uq, 384, 1152)
    wukv, r_wukv = load_w_resident(p, "wukv", w_ukv, 256, 1536)
    wukv_h = wukv[:].rearrange("p c (h t) -> p c h t", t=128)

    def t32(name, rows=96, dt=F32):
        return p.sb(name, [rows, 512], dt), p.reg(name)
    cq = p.sb("cq", [128, 3, 512], F32); r_cq = p.reg("cq")
    ckv = p.sb("ckv", [128, 2, 512], F32); r_ckv = p.reg("ckv")
    sq = p.sb("sq", [128, 3, 512], BF16); r_sq = p.reg("sq")
    cqn = p.sb("cqn", [128, 3, 512], BF16); r_cqn = p.reg("cqn")
    ckvn = p.sb("ckvn", [128, 2, 512], BF16); r_ckvn = p.reg("ckvn")
    posi = p.sb("posi", [96, 512], I32); r_posi = p.reg("posi")
    ang, r_ang = t32("ang")
    kf, r_kf = t32("kf")
    ki = p.sb("ki", [96, 512], I32); r_ki = p.reg("ki")
    rc, r_rc = t32("rc")
    St, r_S = t32("S")
    Ct, r_C = t32("C")
    qn = [t32(f"qn{i}") for i in range(2)]
    t1, r_t1 = t32("t1")
    t2, r_t2 = t32("t2")
    sqh = [t32(f"sqh{i}", 96, BF16) for i in range(2)]
    rstdh = [t32(f"rstdh{i}") for i in range(2)]
    qst = [t32(f"qst{i}", 96, BF16) for i in range(2)]
    krw, r_krw = t32("krw")
    kq, r_kq = t32("kq")
    krr, r_krr = t32("krr")
    sqk, r_sqk = t32("sqk", 96, BF16)
    kst = [t32(f"kst{i}", 96, BF16) for i in range(2)]
    vst = p.sb("vst", [128, 12, 4, 128], BF16); r_vst = p.reg("vst")
    p.memset("pool", vst[:], 1.0, [r_vst])
    p.memset("pool", kq[:], 0.0, [r_kq])
    psA = [p.ps(f"psA{i}", [128, 512]) for i in range(2)]; r_psA = p.regs(2, "psA")
    psB = [p.ps(f"psB{i}", [128, 512]) for i in range(2)]; r_psB = p.regs(2, "psB")
    psC = p.ps("psC", [128, 512]); r_psC = p.reg("psC")
    psV = [p.ps(f"psV{i}", [128, 384]) for i in range(2)]; r_psV = p.regs(2, "psV")
    TWO_PI = 2.0 * PI

    def wrap(t, r_t):
        p.ts("dve", kf[:], t[:], PI, None, ALU.is_gt, None, [r_t], [r_kf])
        p.stt("dve", t[:], kf[:], -TWO_PI, t[:], ALU.mult, ALU.add, [r_kf, r_t], [r_t])
        p.ts("dve", kf[:], t[:], -PI, None, ALU.is_lt, None, [r_t], [r_kf])
        p.stt("dve", t[:], kf[:], TWO_PI, t[:], ALU.mult, ALU.add, [r_kf, r_t], [r_t])
        p.ts("dve", t[:], t[:], PI, -PI, ALU.min, ALU.max, [r_t], [r_t])

    def rope_apply(src, r_src, dst_ap, r_dst, rows):
        p.mm(psC[0:96, :], Pm_t[:], src[:], True, True, [r_Pm, r_src], [r_psC])
        p.tt("dve", t1[rows, :], psC[rows, :], St[rows, :], ALU.mult, [r_psC, r_S], [r_t1])
        p.tt("pool", t2[rows, :], src[rows, :], Ct[rows, :], ALU.mult, [r_src, r_C], [r_t2])
        p.tt("pool", dst_ap, t1[rows, :], t2[rows, :], ALU.add, [r_t1, r_t2], [r_dst])

    kA = 0
    for hi in range(NHALF):
        tsl = slice(hi * 512, (hi + 1) * 512)
        p.dma("sp", posi[:], pos[0:1, tsl].partition_broadcast(96), [], [r_posi])
        p.copy("dve", ang[:], posi[:], [r_posi], [r_ang])
        p.ts("dve", ang[:], ang[:], invf_t[:, 0:1], None, ALU.mult, None, [r_ang, r_invf], [r_ang])
        p.ts("dve", kf[:], ang[:], 1.0 / TWO_PI, None, ALU.mult, None, [r_ang], [r_kf])
        p.copy("dve", ki[:], kf[:], [r_kf], [r_ki])
        p.copy("dve", kf[:], ki[:], [r_ki], [r_kf])
        p.stt("dve", ang[:], kf[:], -TWO_PI, ang[:], ALU.mult, ALU.add, [r_kf, r_ang], [r_ang])
        wrap(ang, r_ang)
        p.ts("dve", rc[:], ang[:], PI / 2, None, ALU.add, None, [r_ang], [r_rc])
        wrap(rc, r_rc)
        p.act(St[:], ang[:], AF.Sin, [r_ang], [r_S])
        p.ts("dve", St[:], St[:], sgn_t[:, 0:1], None, ALU.mult, None, [r_S, r_sgn], [r_S])
        p.act(Ct[:], rc[:], AF.Sin, [r_rc], [r_C])
        p.ts("dve", Ct[:], Ct[:], cm_t[:, 0:1], cb_t[:, 0:1], ALU.mult, ALU.add, [r_C, r_cm, r_cb], [r_C])
        for cc in range(3):
            p.dma("sp", cq[:, cc, :], ut[cc, :, tsl], [], [r_cq])
        for cc in range(2):
            p.dma("sp", ckv[:, cc, :], ut[3 + cc, :, tsl], [], [r_ckv])
        p.dma("sp", krw[64:96, :], ut[5, 64:96, tsl], [], [r_krw])
        for (src, r_src, n, g_t, r_g, dst, r_dst, dim) in ((cq, r_cq, 3, gqa_t, r_gqa, cqn, r_cqn, 384),
                                                           (ckv, r_ckv, 2, gkva_t, r_gkva, ckvn, r_ckvn, 256)):
            for cc in range(n):
                p.act(sq[:, cc, :], src[:, cc, :], AF.Square, [r_src], [r_sq])
            for cc in range(n):
                p.mm(c.ps_ss[:], c.ones[:], sq[:, cc, :], cc == 0, cc == n - 1, [c.r_ones, r_sq], [c.r_ps_ss])
            rstd_from_ps(p, c, c.rstd[:], c.ps_ss[:], dim, [c.r_ps_ss], [c.r_rstd])
            for cc in range(n):
                p.stt("dve", dst[:, cc, :], src[:, cc, :], g_t[:, cc:cc + 1], c.rstd[:], ALU.mult, ALU.mult,
                      [r_src, r_g, c.r_rstd], [r_dst])
        for h in range(12):
            k = kA % 2
            kA += 1
            for cc in range(3):
                p.mm(psA[k][0:96, :], wuq[:, cc, h * 96:(h + 1) * 96], cqn[:, cc, :], cc == 0, cc == 2, [r_wuq, r_cqn], [r_psA[k]])
            sqh_t, r_sqh = sqh[k]
            rs_t, r_rs = rstdh[k]
            qn_t, r_qn = qn[k]
            qs_t, r_qs = qst[k]
            p.act(sqh_t[:], psA[k][0:96, :], AF.Square, [r_psA[k]], [r_sqh])
            p.mm(psB[k][0:96, :], c.ones[0:96, 0:96], sqh_t[:], True, True, [c.r_ones, r_sqh], [r_psB[k]])
            rstd_from_ps(p, c, rs_t[:], psB[k][0:96, :], 96, [r_psB[k]], [r_rs])
            p.stt("dve", qn_t[:], psA[k][0:96, :], gqn_t[:, 0:1], rs_t[:], ALU.mult, ALU.mult, [r_psA[k], r_gqn, r_rs], [r_qn])
            rope_apply(qn_t, r_qn, qs_t[:], r_qs, slice(0, 96))
            p.dma("sp", qTo[h, :, tsl], qs_t[:], [r_qs], [], track=r_qs)
        p.ts("dve", kq[64:96, :], krw[64:96, :], gkn_t[64:96, 0:1], None, ALU.mult, None, [r_krw, r_gkn], [r_kq])
        rope_apply(kq, r_kq, krr[64:96, :], r_krr, slice(64, 96))
        p.act(sqk[64:96, :], krw[64:96, :], AF.Square, [r_krw], [r_sqk])
        for h in range(12):
            k = kA % 2
            kA += 1
            for cc in range(2):
                p.mm(psA[k][0:64, :], wukv[:, cc, h * 128:h * 128 + 64], ckvn[:, cc, :], cc == 0, cc == 1, [r_wukv, r_ckvn], [r_psA[k]])
            rs_t, r_rs = rstdh[k]
            ks_t, r_ks = kst[k]
            p.act(sqk[0:64, :], psA[k][0:64, :], AF.Square, [r_psA[k]], [r_sqk])
            p.mm(psB[k][0:96, :], c.ones[0:96, 0:96], sqk[:], True, True, [c.r_ones, r_sqk], [r_psB[k]])
            rstd_from_ps(p, c, rs_t[:], psB[k][0:96, :], 96, [r_psB[k]], [r_rs])
            p.stt("dve", ks_t[0:64, :], psA[k][0:64, :], gkn_t[0:64, 0:1], rs_t[0:64, :], ALU.mult, ALU.mult,
                  [r_psA[k], r_gkn, r_rs], [r_ks])
            p.tt("dve", ks_t[64:96, :], krr[64:96, :], rs_t[64:96, :], ALU.mult, [r_krr, r_rs], [r_ks])
            p.dma("sp", KTo[h, :, tsl], ks_t[:], [r_ks], [], track=r_ks)
        for tt_ in range(4):
            for g in range(2):
                for cc in range(2):
                    p.mm(psV[g][:].rearrange("p (h t) -> p h t", t=64), ckvn[:, cc, tt_ * 128:(tt_ + 1) * 128],
                         wukv_h[:, cc, g * 6:(g + 1) * 6, 64:128], cc == 0, cc == 1, [r_wukv, r_ckvn], [r_psV[g]])
                pv3 = psV[g][:].rearrange("p (h t) -> p h t", t=64)
                p.copy("act", vst[:, g * 6:(g + 1) * 6:2, tt_, 0:64], pv3[:, 0:6:2, :], [r_psV[g]], [r_vst])
                p.copy("act", vst[:, g * 6 + 1:(g + 1) * 6:2, tt_, 64:128], pv3[:, 1:6:2, :], [r_psV[g]], [r_vst])
        for h in range(12):
            p.dma("sp", Vo[h, :, hi * 4:(hi + 1) * 4, :], vst[:, h, :, :], [r_vst], [], track=r_vst)


def build_attn(p, TOK, S):
    NQB = TOK // 512
    NKT = S // 128
    qT = p.dram_in("qT", [12, 96, TOK], BF16)
    KT = p.dram_in("KT", [12, 96, S], BF16)
    Va = p.dram_in("Va", [12, 128, NKT, 128], BF16)
    out = p.dram_out("attn", [6, 128, TOK], BF16)
    r_out = p.reg("out")
    Kb = [p.sb(f"Kb{i}", [96, S], BF16) for i in range(2)]
    Vb = [p.sb(f"Vb{i}", [128, NKT, 128], BF16) for i in range(2)]
    qb_ = [p.sb(f"qb{i}", [96, TOK], BF16) for i in range(2)]
    r_K, r_V, r_q = p.regs(2, "K"), p.regs(2, "V"), p.regs(2, "q")
    NS = 4
    ps_s = [p.ps(f"ps_s{i}", [128, 512]) for i in range(NS)]
    r_pss = p.regs(NS, "pss")
    ps_o = [p.ps(f"ps_o{i}", [128, 512]) for i in range(2)]
    r_pso = p.regs(2, "pso")
    NP = 3
    pT = [p.sb(f"pT{i}", [128, 512], BF16) for i in range(NP)]
    r_pT = p.regs(NP, "pT")
    rec = [p.sb(f"rec{i}", [128, 512], F32) for i in range(2)]
    r_rec = p.regs(2, "rec")
    ost = [p.sb(f"ost{i}", [128, 512], BF16) for i in range(2)]
    r_ost = p.regs(2, "ost")
    SCALE = float(96 ** -0.5)
    LA = 2

    def load(h):
        b = h % 2
        p.dma("sp", Kb[b][:], KT[h], [], [r_K[b]])
        p.dma("sp", Vb[b][:], Va[h], [], [r_V[b]])
        p.dma("sp", qb_[b][:], qT[h], [], [r_q[b]])

    its = [(h, qb, kt) for h in range(12) for qb in range(NQB) for kt in range(NKT)]
    N = len(its)

    def rec_S(i):
        h, qb, kt = its[i]
        b = h % 2
        p.mm(ps_s[i % NS][:], Kb[b][:, kt * 128:(kt + 1) * 128], qb_[b][:, qb * 512:(qb + 1) * 512], True, True,
             [r_K[b], r_q[b]], [r_pss[i % NS]])

    load(0)
    loaded = 0
    for i in range(min(LA, N)):
        rec_S(i)
    for i in range(N):
        h, qb, kt = its[i]
        b = h % 2
        if kt == 0 and qb == 0 and h + 1 < 12 and loaded < h + 1:
            load(h + 1)
            loaded = h + 1
        if i + LA < N:
            rec_S(i + LA)
        p.act(pT[i % NP][:], ps_s[i % NS][:], AF.Exp, [r_pss[i % NS]], [r_pT[i % NP]], scale=SCALE)
        o = (h * NQB + qb) % 2
        p.mm(ps_o[o][:], Vb[b][:, kt, :], pT[i % NP][:], kt == 0, kt == NKT - 1, [r_V[b], r_pT[i % NP]], [r_pso[o]])
        if kt == NKT - 1:
            base = (h % 2) * 64
            oth = 64 - base
            p.recip(rec[o][oth:oth + 64, :], ps_o[o][oth:oth + 64, :], [r_pso[o]], [r_rec[o]])
            p.tt("dve", ost[o][base:base + 64, :], ps_o[o][base:base + 64, :], rec[o][oth:oth + 64, :], ALU.mult,
                 [r_pso[o], r_rec[o]], [r_ost[o]])
            p.dma("sp", out[h // 2, base:base + 64, qb * 512:(qb + 1) * 512], ost[o][base:base + 64, :], [r_ost[o]], [],
                  track=r_ost[o])


def build_fused(S, depth=4, max_phases=None, only=None):
    p = Prog()
    nph = [0]

    def more():
        nph[0] += 1
        if only is not None:
            return nph[0] in only
        return max_phases is None or nph[0] <= max_phases
    nl, nm = (depth + 1) // 2, depth // 2

    def ein(name, shape, dt=F32):
        return p.ext_in(name, shape, dt)
    xT = ein("xT", [D, S])
    memT = ein("memT", [D, 256])
    pos = ein("pos", [1, S], I32)
    g1 = ein("ffn1_norm", [depth, 128, 8])
    gu1 = ein("ffn1_w_gate_up", [depth, D, 2 * DFF])
    dn1 = ein("ffn1_w_down", [depth, DFF, D])
    gmix = ein("mix_norm", [depth, 128, 8])
    gmem = ein("mem_norm", [depth, 128, 8])
    wkv = ein("w_mem_kv", [depth, D, 512])
    gq = ein("mem_q_norm", [depth, 128, 1])
    gk = ein("mem_k_norm", [depth, 128, 1])
    wo = ein("w_out", [depth, D, D])
    g2 = ein("ffn2_norm", [depth, 128, 8])
    gu2 = ein("ffn2_w_gate_up", [depth, D, 2 * DFF])
    dn2 = ein("ffn2_w_down", [depth, DFF, D])
    lwin = ein("lru_w_in", [nl, D, 1792])
    lcw = ein("lru_cw", [nl, 96, 8, 4])
    lcb = ein("lru_cb", [nl, 96, 8])
    lgw = ein("lru_gw", [nl, 96, 8, 4, 96])
    lgb = ein("lru_gb", [nl, 96, 8, 4])
    llam = ein("lru_lam", [nl, 96, 16])
    mwin = ein("mla_w_in", [nm, D, 928])
    mgqa = ein("mla_q_a_norm", [nm, 128, 3])
    mwuq = ein("mla_w_uq", [nm, 384, 1152])
    mgkva = ein("mla_kv_a_norm", [nm, 128, 2])
    mwukv = ein("mla_w_ukv", [nm, 256, 1536])
    mgqn = ein("mla_q_norm", [nm, 96, 1])
    mgkn = ein("mla_k_norm", [nm, 96, 1])
    cst = {k: ein(k, list(v.shape)) for k, v in mla_consts().items()}
    out = p.ext_out("xT_out", [D, S], F32)
    xs = p.dram_tmp("xs", [D, S], F32)
    utok_l = p.dram_tmp("utok_l", [16, 96, S], F32)
    utok_m = p.dram_tmp("utok_m", [6, 128, S], F32)
    umem = p.dram_tmp("umem", [2, 128, S], F32)
    tok_l = p.dram_tmp("tok_l", [8, 96, S], BF16)
    attn = p.dram_tmp("attn_s", [6, 128, S], BF16)
    qTs = p.dram_tmp("qT_s", [12, 96, S], BF16)
    KTs = p.dram_tmp("KT_s", [12, 96, S], BF16)
    Vas = p.dram_tmp("Va_s", [12, 128, S // 128, 128], BF16)
    for l in range(depth):
        j = l // 2
        kind = "lru" if l % 2 == 0 else "mla"
        xin = xT if l == 0 else xs
        xout = out if l == depth - 1 else xs
        if more():
            p.begin_phase()
            p.io = {"xT": xin, "g1": g1[l], "gm": gmix[l], "w_gu": gu1[l], "w_dn": dn1[l],
                    "w_in": lwin[j] if kind == "lru" else mwin[j], "xT_out": xs,
                    "u_tok": utok_l if kind == "lru" else utok_m, "u_mem": umem}
            build_a(p, S, kind, f"L{l}")
            p.end_phase()
        if kind == "lru":
            if more():
                p.begin_phase()
                p.io = {"xr": utok_l[8:16], "gate": utok_l[0:8], "cw": lcw[j], "cb": lcb[j], "gw": lgw[j], "gb": lgb[j],
                        "lam": llam[j], "tok": tok_l}
                build_lru_b(p, S, 8)
                p.end_phase()
            tk = tok_l
        else:
            if more():
                p.begin_phase()
                p.io = {"u_tok": utok_m, "pos": pos, "gqa": mgqa[j], "w_uq": mwuq[j], "gkva": mgkva[j], "w_ukv": mwukv[j],
                        "gqn": mgqn[j], "gkn": mgkn[j], "qT": qTs, "KT": KTs, "Vaug": Vas}
                p.io.update(cst)
                build_mla_q(p, S)
                p.end_phase()
            if more():
                p.begin_phase()
                p.io = {"qT": qTs, "KT": KTs, "Va": Vas, "attn": attn}
                build_attn(p, S, S)
                p.end_phase()
            tk = attn
        if more():
            p.begin_phase()
            p.io = {"xT": xs, "u_mem": umem, "tok": tk, "memT": memT, "gmem": gmem[l], "w_mem_kv": wkv[l], "gq": gq[l],
                    "gk": gk[l], "w_out": wo[l], "g2": g2[l], "w_gu": gu2[l], "w_dn": dn2[l], "xT_out": xout}
            build_c(p, S, kind, f"L{l}")
            p.end_phase()
    return p.finish()


_PROGS = {}


def _f(a):
    return np.ascontiguousarray(np.asarray(a, dtype=np.float32))


def _prep(x, mem, positions, ffn1_norm, ffn1_w_gate_up, ffn1_w_down, mix_norm, mem_norm,
          w_mem_kv, mem_q_norm, mem_k_norm, w_out, ffn2_norm, ffn2_w_gate_up, ffn2_w_down,
          lru_w_in, lru_conv_w, lru_conv_b, lru_gate_w, lru_gate_b, lru_lambda,
          mla_w_in, mla_q_a_norm, mla_w_uq, mla_kv_a_norm, mla_w_ukv, mla_q_norm, mla_k_norm):
    x = np.asarray(x, np.float32)
    mem = np.asarray(mem, np.float32)
    positions = np.asarray(positions, np.int32)
    B, S, _ = x.shape
    depth = np.asarray(ffn1_norm).shape[0]
    nl, nm = (depth + 1) // 2, depth // 2
    QPB = NCORES // B
    TOK = S // QPB

    def lay(g, n):
        g = np.asarray(g, np.float32)
        return np.ascontiguousarray(g.reshape(g.shape[0], n, 128).transpose(0, 2, 1))
    cw = np.asarray(lru_conv_w, np.float32).reshape(nl, 4, 8, 96).transpose(0, 3, 2, 1)
    cb = np.asarray(lru_conv_b, np.float32).reshape(nl, 8, 96).transpose(0, 2, 1)
    gw = np.asarray(lru_gate_w, np.float32).transpose(0, 4, 3, 1, 2, 5).reshape(nl, 96, 8, 4, 96)
    gb = np.asarray(lru_gate_b, np.float32).reshape(nl, 2, 2, 8, 96).transpose(0, 4, 3, 1, 2).reshape(nl, 96, 8, 4)
    lam = np.asarray(lru_lambda, np.float32).reshape(nl, 2, 8, 96).transpose(0, 3, 2, 1).reshape(nl, 96, 16)
    shared = {
        "ffn1_norm": lay(ffn1_norm, 8), "ffn1_w_gate_up": _f(ffn1_w_gate_up), "ffn1_w_down": _f(ffn1_w_down),
        "mix_norm": lay(mix_norm, 8), "mem_norm": lay(mem_norm, 8), "w_mem_kv": _f(w_mem_kv),
        "mem_q_norm": _f(np.tile(np.asarray(mem_q_norm, np.float32), (1, 2)).reshape(depth, 128, 1)),
        "mem_k_norm": _f(np.tile(np.asarray(mem_k_norm, np.float32), (1, 2)).reshape(depth, 128, 1)),
        "w_out": _f(w_out), "ffn2_norm": lay(ffn2_norm, 8), "ffn2_w_gate_up": _f(ffn2_w_gate_up),
        "ffn2_w_down": _f(ffn2_w_down), "lru_w_in": _f(lru_w_in), "lru_cw": _f(cw), "lru_cb": _f(cb), "lru_gw": _f(gw),
        "lru_gb": _f(gb), "lru_lam": _f(lam), "mla_w_in": _f(mla_w_in), "mla_q_a_norm": lay(mla_q_a_norm, 3),
        "mla_w_uq": _f(mla_w_uq), "mla_kv_a_norm": lay(mla_kv_a_norm, 2), "mla_w_ukv": _f(mla_w_ukv),
        "mla_q_norm": _f(np.asarray(mla_q_norm, np.float32).reshape(nm, 96, 1)),
        "mla_k_norm": _f(np.asarray(mla_k_norm, np.float32).reshape(nm, 96, 1)),
    }
    shared.update(mla_consts())
    xTb = [np.ascontiguousarray(x[b].T) for b in range(B)]
    memTb = [np.ascontiguousarray(mem[b].T) for b in range(B)]
    posb = [np.ascontiguousarray(positions[b].reshape(1, S)) for b in range(B)]
    in_maps = [dict(shared, xT=xTb[c // QPB], memT=memTb[c // QPB], pos=posb[c // QPB]) for c in range(NCORES)]
    return in_maps, B, S, depth, QPB, TOK


def kernel(**inputs):
    in_maps, B, S, depth, QPB, TOK = _prep(**inputs)
    key = (S, depth)
    if key not in _PROGS:
        _PROGS[key] = build_fused(S, depth)
    res = run_bass_kernel_spmd(_PROGS[key], in_maps, core_ids=list(range(NCORES))).results
    out = np.empty((B, S, D), np.float32)
    for c in range(NCORES):
        b, q = c // QPB, c % QPB
        out[b, q * TOK:(q + 1) * TOK, :] = res[c]["xT_out"][:, q * TOK:(q + 1) * TOK].T
    return out
```

```python
import numpy as np
import ml_dtypes
from contextlib import ExitStack
import concourse.bass as bass
import concourse.mybir as mybir
from concourse.bass_utils import run_bass_kernel_spmd

F32, BF16, I32 = mybir.dt.float32, mybir.dt.bfloat16, mybir.dt.int32
ALU = mybir.AluOpType
AF = mybir.ActivationFunctionType
NPBF = ml_dtypes.bfloat16

D = 1024
DFF = 2816
NJ = DFF // 128
EPS = 1e-6
NCORES = 8
PI = float(np.pi)


class Reg:
    __slots__ = ("name", "w", "rs", "guard", "sem")

    def __init__(self, name):
        self.name = name
        self.w = None
        self.rs = {}
        self.guard = []
        self.sem = None


class SemSlot:
    __slots__ = ("h", "count", "sw")

    def __init__(self, h, sw=False):
        self.h = h
        self.count = 0
        self.sw = sw


class Op:
    __slots__ = ("eng", "fn", "waits", "signal", "track", "idx", "ticket")


class Prog:
    ENGS = ["pe", "act", "dve", "pool", "sp"]

    def __init__(self):
        self.nc = bass.Bass("TRN2", target_bir_lowering=False)
        self.es = ExitStack()
        self.pes = None
        self.ops = {e: [] for e in self.ENGS}
        self.seen = {e: {} for e in self.ENGS}
        self.nreg = 0
        self.esem = {}
        self.all_regs = []
        self.free_slots = []
        self.phase_slots = []
        self.nslots = 0
        self.emitted = {e: 0 for e in self.ENGS}
        self.tbase = {e: 0 for e in self.ENGS}
        self.io = {}
        for e in ["pe", "act", "dve", "pool"]:
            self.esem[e] = self.es.enter_context(self.nc.semaphore(f"s_{e}"))

    def reg(self, name=None):
        self.nreg += 1
        r = Reg(name or f"r{self.nreg}")
        self.all_regs.append(r)
        return r

    def regs(self, n, name="r"):
        return [self.reg(f"{name}{i}") for i in range(n)]

    def sb(self, name, shape, dt):
        self.nreg += 1
        return self.pes.enter_context(self.nc.sbuf_tensor(f"s{self.nreg}_" + name, list(shape), dt))

    def ps(self, name, shape, dt=F32):
        self.nreg += 1
        return self.pes.enter_context(self.nc.psum_tensor(f"p{self.nreg}_" + name, list(shape), dt))

    def dram_in(self, name, shape, dt):
        if name in self.io:
            ap = self.io[name]
            assert list(ap.shape) == list(shape) and ap.dtype == dt, (name, ap.shape, shape)
            return ap
        return self.nc.dram_tensor(name, list(shape), dt, kind="ExternalInput").ap()

    def dram_out(self, name, shape, dt):
        if name in self.io:
            ap = self.io[name]
            assert list(ap.shape) == list(shape) and ap.dtype == dt, (name, ap.shape, shape)
            return ap
        return self.nc.dram_tensor(name, list(shape), dt, kind="ExternalOutput").ap()

    def ext_in(self, name, shape, dt):
        return self.nc.dram_tensor(name, list(shape), dt, kind="ExternalInput").ap()

    def ext_out(self, name, shape, dt):
        return self.nc.dram_tensor(name, list(shape), dt, kind="ExternalOutput").ap()

    def dram_tmp(self, name, shape, dt):
        if name in self.io:
            return self.io[name]
        return self.nc.dram_tensor(name, list(shape), dt).ap()

    def _slot(self, reg, q):
        if reg.sem is None:
            if q != "pool" and self.free_slots:
                reg.sem = self.free_slots.pop()
            else:
                self.nslots += 1
                reg.sem = SemSlot(self.es.enter_context(self.nc.semaphore(f"d{self.nslots}")), sw=(q == "pool"))
            self.phase_slots.append(reg.sem)
        assert q != "pool" or reg.sem.sw, f"software DMA onto a recycled semaphore ({reg.name})"
        return reg.sem

    def _add(self, eng, fn, reads, writes, track=None):
        op = Op()
        op.eng, op.fn, op.signal, op.track = eng, fn, False, track
        lst = self.ops[eng]
        op.idx = len(lst)
        waits = []
        for r in reads:
            if r.w is not None:
                waits.append(r.w)
        for r in writes:
            joining = (track is not None and r.w is not None and r.w[0] == "d"
                       and r.w[1] is track.sem and not r.rs)
            if joining:
                waits.extend(r.guard)
            else:
                g = ([r.w] if r.w is not None else []) + list(r.rs.values())
                r.guard = g
                waits.extend(g)
        if track is not None:
            slot = self._slot(track, eng)
            op.track = slot
            slot.count += 1
            tok = ("d", slot, slot.count)
            key = ("d", id(slot))
        else:
            tok = ("c", eng, op.idx)
            key = ("c", eng)
        for r in reads:
            r.rs[key] = tok
        for r in writes:
            r.w = tok
            r.rs = {}
        op.waits = self._filter_waits(eng, waits)
        lst.append(op)
        return op

    def _filter_waits(self, eng, waits):
        final = []
        seen = self.seen[eng]
        for t in waits:
            if t[0] == "c":
                if t[1] == eng and eng == "pe":
                    continue
                k = t[1]
            else:
                k = id(t[1])
            if seen.get(k, -1) >= t[2]:
                continue
            seen[k] = t[2]
            final.append(t)
            if t[0] == "c":
                self.ops[t[1]][t[2]].signal = True
        return final

    def dma(self, q, out, in_, reads, writes, track=None, **kw):
        tr = track if track is not None else writes[0]
        return self._add(q, lambda e: e.dma_start(out=out, in_=in_, **kw), reads, writes, track=tr)

    def mm(self, out, lhsT, rhs, start, stop, reads, writes):
        return self._add("pe", lambda e: e.matmul(out, lhsT=lhsT, rhs=rhs, start=start, stop=stop), reads, writes)

    def act(self, out, in_, func, reads, writes, bias=None, scale=None, eng="act"):
        kw = {}
        if bias is not None:
            kw["bias"] = bias
        if scale is not None:
            kw["scale"] = scale
        return self._add(eng, lambda e: e.activation(out=out, in_=in_, func=func, **kw), reads, writes)

    def tt(self, eng, out, in0, in1, op, reads, writes):
        return self._add(eng, lambda e: e.tensor_tensor(out=out, in0=in0, in1=in1, op=op), reads, writes)

    def ts(self, eng, out, in0, s1, s2, op0, op1, reads, writes):
        if op1 is None:
            return self._add(eng, lambda e: e.tensor_scalar(out=out, in0=in0, scalar1=s1, scalar2=None, op0=op0), reads, writes)
        return self._add(eng, lambda e: e.tensor_scalar(out=out, in0=in0, scalar1=s1, scalar2=s2, op0=op0, op1=op1), reads, writes)

    def stt(self, eng, out, in0, scalar, in1, op0, op1, reads, writes):
        return self._add(eng, lambda e: e.scalar_tensor_tensor(out=out, in0=in0, scalar=scalar, in1=in1, op0=op0, op1=op1), reads, writes)

    def copy(self, eng, out, in_, reads, writes):
        if eng == "act":
            return self._add(eng, lambda e: e.copy(out=out, in_=in_), reads, writes)
        return self._add(eng, lambda e: e.tensor_copy(out=out, in_=in_), reads, writes)

    def memset(self, eng, ap, val, writes):
        return self._add(eng, lambda e: e.memset(ap, val), [], writes)

    def scan(self, out, d0, d1, initial, reads, writes):
        return self._add("dve", lambda e: e.tensor_tensor_scan(out=out, data0=d0, data1=d1, initial=initial,
                                                               op0=ALU.mult, op1=ALU.add), reads, writes)

    def recip(self, out, in_, reads, writes):
        return self._add("dve", lambda e: e.reciprocal(out=out, in_=in_), reads, writes)

    def begin_phase(self):
        self.pes = ExitStack()

    def end_phase(self):
        toks = []
        for e in ["pe", "act", "dve", "pool"]:
            lst = self.ops[e]
            for i in range(len(lst) - 1, self.emitted[e] - 1, -1):
                if lst[i].track is None and lst[i].fn is not None:
                    toks.append(("c", e, i))
                    break
        for sl in self.phase_slots:
            toks.append(("d", sl, sl.count))
        for e in self.ENGS:
            op = Op()
            op.eng, op.fn, op.signal, op.track = e, None, False, None
            op.idx = len(self.ops[e])
            op.waits = self._filter_waits(e, [t for t in toks if not (t[0] == "c" and t[1] == e)])
            self.ops[e].append(op)
        self._emit_slice()
        self.pes.close()
        self.pes = None
        for r in self.all_regs:
            r.w, r.rs, r.guard = None, {}, []
            r.sem = None
        self.all_regs = [r for r in self.all_regs if getattr(r, "name", "").startswith("DR_")]
        self.free_slots.extend(sl for sl in self.phase_slots if not sl.sw)
        self.phase_slots = []

    def _emit_slice(self):
        nc = self.nc
        for e in ["pe", "act", "dve", "pool"]:
            t = self.tbase[e]
            for op in self.ops[e][self.emitted[e]:]:
                if op.signal:
                    t += 1
                op.ticket = t
            self.tbase[e] = t
        prog = self
        start = dict(self.emitted)

        def run(name, engine):
            for op in prog.ops[name][start[name]:]:
                for t in op.waits:
                    if t[0] == "c":
                        engine.wait_ge(prog.esem[t[1]], prog.ops[t[1]][t[2]].ticket)
                    else:
                        engine.wait_ge(t[1].h, 16 * t[2])
                if op.fn is None:
                    continue
                ins = op.fn(engine)
                if ins is None:
                    continue
                if op.track is not None:
                    ins.then_inc(op.track.h, 16)
                elif op.signal:
                    ins.then_inc(prog.esem[name], 1)

        with nc.Block() as block:
            @block.tensor
            def _(e):
                run("pe", e)

            @block.scalar
            def _(e):
                run("act", e)

            @block.vector
            def _(e):
                run("dve", e)

            @block.gpsimd
            def _(e):
                run("pool", e)

            @block.sync
            def _(e):
                run("sp", e)
        for e in self.ENGS:
            self.emitted[e] = len(self.ops[e])

    def finish(self):
        self.es.close()
        return self.nc


class Ctx:
    pass


def setup_common(p, TOK, SBK):
    c = Ctx()
    c.TOK, c.SBK, c.NSB = TOK, SBK, TOK // SBK
    c.NH = SBK // 512
    c.ones = p.sb("ones", [128, 128], BF16)
    c.r_ones = p.reg("ones")
    p.memset("dve", c.ones[:], 1.0, [c.r_ones])
    c.eps = p.sb("eps", [128, 1], F32)
    c.r_eps = p.reg("eps")
    p.memset("dve", c.eps[:], EPS, [c.r_eps])
    c.x = p.sb("x_sb", [128, 8, SBK], F32)
    c.r_x = p.regs(8, "x")
    c.h = p.sb("h_sb", [128, 8, SBK], BF16)
    c.r_h = p.reg("h")
    c.sq = p.sb("sq_sb", [128, 8, 512], BF16)
    c.r_sq = p.reg("sq")
    c.rstd = p.sb("rstd_sb", [128, 512], F32)
    c.r_rstd = p.reg("rstd")
    c.ps_ss = p.ps("ps_ss", [128, 512])
    c.r_ps_ss = p.reg("ps_ss")
    return c


def load_x(p, c, xT, sbi, q="sp"):
    SBK = c.SBK
    for ch in range(8):
        p.dma(q, c.x[:, ch, :], xT[ch * 128:(ch + 1) * 128, sbi * SBK:(sbi + 1) * SBK], [], [c.r_x[ch]])


def store_x(p, c, xTo, r_out, sbi, q="sp"):
    SBK = c.SBK
    for ch in range(8):
        p.dma(q, xTo[ch * 128:(ch + 1) * 128, sbi * SBK:(sbi + 1) * SBK], c.x[:, ch, :], [c.r_x[ch]], [], track=c.r_x[ch])


def rstd_from_ps(p, c, out, ps, dim, reads, writes):
    p.act(out, ps, AF.Sqrt, reads + [c.r_eps], writes, bias=c.eps[0:out.shape[0], :], scale=1.0 / dim)
    p.recip(out, out, writes, writes)


def rmsnorm_x(p, c, g_sb, r_g):
    for hf in range(c.NH):
        sl = slice(hf * 512, (hf + 1) * 512)
        for ch in range(8):
            p.act(c.sq[:, ch, :], c.x[:, ch, sl], AF.Square, [c.r_x[ch]], [c.r_sq])
        for ch in range(8):
            p.mm(c.ps_ss[:], c.ones[:], c.sq[:, ch, :], ch == 0, ch == 7, [c.r_ones, c.r_sq], [c.r_ps_ss])
        rstd_from_ps(p, c, c.rstd[:], c.ps_ss[:], D, [c.r_ps_ss], [c.r_rstd])
        for ch in range(8):
            eng = "dve"
            p.stt(eng, c.h[:, ch, sl], c.x[:, ch, sl], g_sb[:, ch:ch + 1], c.rstd[:], ALU.mult, ALU.mult,
                  [c.r_x[ch], r_g, c.r_rstd], [c.r_h])


def load_gain(p, name, src, ncol, mult):
    t = p.sb(name, [128, ncol], F32)
    r = p.reg(name)
    p.dma("sp", t[:], src, [], [r])
    if mult != 1.0:
        p.ts("dve", t[:], t[:], float(mult), None, ALU.mult, None, [r], [r])
    return t, r


class FFN:
    def __init__(self, p, c, tag, w_gu, w_down, g_dram):
        self.p, self.c = p, c
        SBK = c.SBK
        self.wg_s = p.dram_tmp(f"wg_s{tag}", [11, 128, 8, 256], BF16)
        self.wu_s = p.dram_tmp(f"wu_s{tag}", [11, 128, 8, 256], BF16)
        r_cast = p.reg(f"wcast{tag}")
        self.r_wgs = [r_cast] * 11
        self.r_wus = [r_cast] * 11
        wv = w_gu.rearrange("(c p) n -> p c n", p=128)
        for s in range(11):
            p.dma("pool", self.wg_s[s], wv[:, :, s * 256:(s + 1) * 256], [], [self.r_wgs[s]])
            p.dma("pool", self.wu_s[s], wv[:, :, DFF + s * 256:DFF + (s + 1) * 256], [], [self.r_wus[s]])
        self.g, self.r_g = load_gain(p, f"g{tag}", g_dram, 8, 1.0)
        self.wd = p.sb(f"wd{tag}", [128, NJ, D], BF16)
        self.r_wd = p.reg(f"wd{tag}")
        wdv = w_down.rearrange("(j p) n -> p j n", p=128)
        for j0 in range(0, NJ, 2):
            p.dma("pool", self.wd[:, j0:j0 + 2, :], wdv[:, j0:j0 + 2, :], [], [self.r_wd])

    @staticmethod
    def alloc_shared(p, c):
        s = Ctx()
        s.wg = [p.sb(f"wg{i}", [128, 8, 256], BF16) for i in range(2)]
        s.wu = [p.sb(f"wu{i}", [128, 8, 256], BF16) for i in range(2)]
        s.r_wg = p.regs(2, "wg")
        s.r_wu = p.regs(2, "wu")
        s.actT = p.sb("actT", [128, NJ, c.SBK], BF16)
        s.r_act = p.regs(NJ, "act")
        s.ps_g = [p.ps(f"ps_g{i}", [128, 512]) for i in range(2)]
        s.ps_u = [p.ps(f"ps_u{i}", [128, 512]) for i in range(2)]
        s.r_psg = p.regs(2, "psg")
        s.r_psu = p.regs(2, "psu")
        s.sg = [p.sb(f"sg{i}", [128, 512], F32) for i in range(2)]
        s.r_sg = p.regs(2, "sg")
        s.ps_y = [p.ps(f"ps_y{i}", [128, 512]) for i in range(2)]
        s.r_psy = p.regs(2, "psy")
        s.k = 0
        s.ky = 0
        return s

    def run(self, s, extra=None):
        p, c = self.p, self.c
        rmsnorm_x(p, c, self.g, self.r_g)
        for sl_i in range(11):
            b = sl_i % 2
            xg, xu = (extra if (extra is not None and sl_i == 0) else ([], []))
            p.dma("sp", s.wg[b][:], self.wg_s[sl_i], [self.r_wgs[sl_i]], [s.r_wg[b]] + xg)
            p.dma("sp", s.wu[b][:], self.wu_s[sl_i], [self.r_wus[sl_i]], [s.r_wu[b]] + xu)
            for jj in range(2):
                j = sl_i * 2 + jj
                for hf in range(c.NH):
                    sl = slice(hf * 512, (hf + 1) * 512)
                    k = s.k % 2
                    s.k += 1
                    for ch in range(8):
                        p.mm(s.ps_g[k][:], s.wg[b][:, ch, jj * 128:(jj + 1) * 128], c.h[:, ch, sl], ch == 0, ch == 7,
                             [s.r_wg[b], c.r_h], [s.r_psg[k]])
                    for ch in range(8):
                        p.mm(s.ps_u[k][:], s.wu[b][:, ch, jj * 128:(jj + 1) * 128], c.h[:, ch, sl], ch == 0, ch == 7,
                             [s.r_wu[b], c.r_h], [s.r_psu[k]])
                    p.act(s.sg[k][:], s.ps_g[k][:], AF.Silu, [s.r_psg[k]], [s.r_sg[k]])
                    p.tt("dve", s.actT[:, j, sl], s.sg[k][:], s.ps_u[k][:], ALU.mult, [s.r_sg[k], s.r_psu[k]], [s.r_act[j]])
        for o in range(8):
            for hf in range(c.NH):
                sl = slice(hf * 512, (hf + 1) * 512)
                k = s.ky % 2
                s.ky += 1
                for j in range(NJ):
                    p.mm(s.ps_y[k][:], self.wd[:, j, o * 128:(o + 1) * 128], s.actT[:, j, sl], j == 0, j == NJ - 1,
                         [self.r_wd, s.r_act[j]], [s.r_psy[k]])
                p.stt("dve", c.x[:, o, sl], s.ps_y[k][:], 0.5, c.x[:, o, sl], ALU.mult, ALU.add,
                      [s.r_psy[k], c.r_x[o]], [c.r_x[o]])


def load_w_resident(p, name, w, K, N, kc=128):
    nk = K // kc
    t = p.sb(name, [kc, nk, N], BF16)
    r = p.reg(name)
    wv = w.rearrange("(c p) n -> p c n", p=kc)
    step = max(1, 4096 // N)
    for c0 in range(0, nk, step):
        c1 = min(nk, c0 + step)
        p.dma("pool", t[:, c0:c1, :], wv[:, c0:c1, :], [], [r])
    return t, r


def build_a(p, TOK, kind, tag):
    SBK = min(1024, TOK)
    WIN = 1792 if kind == "lru" else 928
    NTC, TCP = (16, 96) if kind == "lru" else (6, 128)
    xT = p.dram_in("xT", [D, TOK], F32)
    g1 = p.dram_in("g1", [128, 8], F32)
    gm = p.dram_in("gm", [128, 8], F32)
    w_gu = p.dram_in("w_gu", [D, 2 * DFF], F32)
    w_dn = p.dram_in("w_dn", [DFF, D], F32)
    w_in = p.dram_in("w_in", [D, WIN], F32)
    xo = p.dram_out("xT_out", [D, TOK], F32)
    uo = p.dram_out("u_tok", [NTC, TCP, TOK], F32)
    umo = p.dram_out("u_mem", [2, 128, TOK], F32)
    r_xo, r_uo, r_umo = p.reg("xo"), p.reg("uo"), p.reg("umo")
    c = setup_common(p, TOK, SBK)
    ffn = FFN(p, c, "1" + tag, w_gu, w_dn, g1)
    fs = FFN.alloc_shared(p, c)
    gmt, r_gm = load_gain(p, "gm", gm, 8, 1.0)
    win, r_win = load_w_resident(p, "win", w_in, D, WIN)
    ps_p = [p.ps(f"ps_p{i}", [128, 512]) for i in range(1)]
    r_psp = p.regs(1, "psp")
    ust = [p.sb(f"ust{i}", [128, 512], F32) for i in range(2)]
    r_ust = p.regs(2, "ust")
    if kind == "lru":
        chunks = [(i * 96, 96) for i in range(16)] + [(1536, 128), (1664, 128)]
    else:
        chunks = [(i * 128, 128) for i in range(5)] + [(576, 96), (672, 128), (800, 128)]
    kk = 0
    for sbi in range(c.NSB):
        load_x(p, c, xT, sbi)
        ffn.run(fs)
        store_x(p, c, xo, r_xo, sbi)
        rmsnorm_x(p, c, gmt, r_gm)
        for ci, (cs, M) in enumerate(chunks):
            for hf in range(c.NH):
                ub = kk % 2
                kk += 1
                sl = slice(hf * 512, (hf + 1) * 512)
                for ch in range(8):
                    p.mm(ps_p[0][0:M, :], win[:, ch, cs:cs + M], c.h[:, ch, sl], ch == 0, ch == 7, [r_win, c.r_h], [r_psp[0]])
                p.copy("act", ust[ub][0:M, :], ps_p[0][0:M, :], [r_psp[0]], [r_ust[ub]])
                tsl = slice(sbi * SBK + hf * 512, sbi * SBK + (hf + 1) * 512)
                if ci < NTC:
                    p.dma("sp", uo[ci, 0:M, tsl], ust[ub][0:M, :], [r_ust[ub]], [], track=r_ust[ub])
                else:
                    p.dma("sp", umo[ci - NTC, :, tsl], ust[ub][0:128, :], [r_ust[ub]], [], track=r_ust[ub])


def build_lru_b(p, S, NB):
    CH = min(1024, S)
    NCH = S // CH
    NHF = CH // 512
    xr = p.dram_in("xr", [NB, 96, S], F32)
    gt = p.dram_in("gate", [NB, 96, S], F32)
    cw = p.dram_in("cw", [96, NB, 4], F32)
    cb = p.dram_in("cb", [96, NB], F32)
    gw = p.dram_in("gw", [96, NB, 4, 96], F32)
    gb = p.dram_in("gb", [96, NB, 4], F32)
    lam = p.dram_in("lam", [96, 2 * NB], F32)
    out = p.dram_out("tok", [NB, 96, S], BF16)
    r_out = p.reg("out")

    def small(name, src, shape, q="sp"):
        t = p.sb(name, shape, F32)
        r = p.reg(name)
        p.dma(q, t[:], src, [], [r])
        return t, r
    cw_t, r_cw = small("cw", cw, [96, NB, 4])
    cb_t, r_cb = small("cb", cb, [96, NB])
    gb_t, r_gb = small("gb", gb, [96, NB, 4])
    lam_t, r_lam = small("lam", lam, [96, 2 * NB])
    gw_t = p.sb("gw", [96, NB, 4, 96], BF16)
    r_gw = p.reg("gw")
    p.dma("pool", gw_t[:], gw, [], [r_gw])
    one = p.sb("one", [128, 1], F32)
    r_one = p.reg("one")
    p.memset("dve", one[:], 1.0, [r_one])
    m8 = p.sb("m8", [96, 2 * NB], F32)
    r_m8 = p.reg("m8")
    p.act(m8[:], lam_t[:], AF.Exp, [r_lam], [r_m8], scale=-1.0)
    p.act(m8[:], m8[:], AF.Ln, [r_m8, r_one], [r_m8], bias=one[0:96, :])
    p.ts("dve", m8[:], m8[:], -8.0, None, ALU.mult, None, [r_m8], [r_m8])

    xc_full = p.sb("xc_full", [96, S], F32)
    hf_full = p.sb("hf_full", [96, S], F32)
    r_xc = p.regs(NCH, "xc")
    r_hf = p.regs(NCH, "hf")
    xr_t = [p.sb(f"xr_t{i}", [96, CH + 3], F32) for i in range(2)]
    r_xr = p.regs(2, "xr_t")
    g_t = [p.sb(f"g_t{i}", [96, CH], F32) for i in range(2)]
    r_gt = p.regs(2, "g_t")
    xcb = [p.sb(f"xcb{i}", [96, CH], BF16) for i in range(2)]
    r_xcb = p.regs(2, "xcb")
    ps_r = [p.ps(f"ps_r{i}", [128, CH]) for i in range(1)]
    ps_i = [p.ps(f"ps_i{i}", [128, CH]) for i in range(1)]
    r_psr = p.regs(1, "psr")
    r_psi = p.regs(1, "psi")

    def wt(name, dt=F32):
        return p.sb(name, [96, CH], dt), p.reg(name)
    rr, r_rr = wt("rr")
    ii, r_ii = wt("ii")
    aa, r_aa = wt("aa")
    qq, r_qq = wt("qq")
    hr = [wt(f"hr{i}") for i in range(2)]
    g2, r_g2 = wt("g2")
    ot = [wt(f"ot{i}", BF16) for i in range(2)]

    def gates(blk, d, xcb_t, r_xcb_t, xc_ap, r_xc_c):
        for hf in range(NHF):
            sl = slice(hf * 512, (hf + 1) * 512)
            p.mm(ps_r[0][0:96, sl], gw_t[:, blk, d * 2 + 0, :], xcb_t[:, sl], True, True, [r_gw, r_xcb_t], [r_psr[0]])
            p.mm(ps_i[0][0:96, sl], gw_t[:, blk, d * 2 + 1, :], xcb_t[:, sl], True, True, [r_gw, r_xcb_t], [r_psi[0]])
        p.act(rr[:], ps_r[0][0:96, :], AF.Sigmoid, [r_psr[0], r_gb], [r_rr], bias=gb_t[:, blk, d * 2:d * 2 + 1])
        p.act(ii[:], ps_i[0][0:96, :], AF.Sigmoid, [r_psi[0], r_gb], [r_ii], bias=gb_t[:, blk, d * 2 + 1:d * 2 + 2])
        p.act(aa[:], rr[:], AF.Exp, [r_rr, r_m8], [r_aa], scale=m8[:, blk * 2 + d:blk * 2 + d + 1])
        p.act(qq[:], aa[:], AF.Square, [r_aa], [r_qq])
        p.act(qq[:], qq[:], AF.Sqrt, [r_qq, r_one], [r_qq], bias=one[0:96, :], scale=-1.0)
        p.tt("dve", ii[:], ii[:], xc_ap, ALU.mult, [r_ii, r_xc_c], [r_ii])
        p.tt("dve", qq[:], qq[:], ii[:], ALU.mult, [r_qq, r_ii], [r_qq])

    k = 0
    for blk in range(NB):
        for ci in range(NCH):
            b = k % 2
            k += 1
            t0 = ci * CH
            lo = max(0, t0 - 2)
            hi = min(S, t0 + CH + 1)
            if ci == 0 or ci == NCH - 1:
                p.memset("pool", xr_t[b][:], 0.0, [r_xr[b]])
            p.dma("sp", xr_t[b][:, lo - (t0 - 2):hi - (t0 - 2)], xr[blk, :, lo:hi], [], [r_xr[b]])
            xc = xc_full[:, t0:t0 + CH]
            p.ts("dve", xc, xr_t[b][:, 0:CH], cw_t[:, blk, 0:1], cb_t[:, blk:blk + 1], ALU.mult, ALU.add,
                 [r_xr[b], r_cw, r_cb], [r_xc[ci]])
            for kk in range(1, 4):
                p.stt("dve", xc, xr_t[b][:, kk:kk + CH], cw_t[:, blk, kk:kk + 1], xc, ALU.mult, ALU.add,
                      [r_xr[b], r_cw, r_xc[ci]], [r_xc[ci]])
            p.copy("pool", xcb[b][:], xc, [r_xc[ci]], [r_xcb[b]])
            gates(blk, 0, xcb[b], r_xcb[b], xc, r_xc[ci])
            init = 0.0 if ci == 0 else hf_full[:, t0 - 1:t0]
            rd = [r_aa, r_qq] + ([] if ci == 0 else [r_hf[ci - 1]])
            p.scan(hf_full[:, t0:t0 + CH], aa[:], qq[:], init, rd, [r_hf[ci]])
        for ci in range(NCH - 1, -1, -1):
            b = k % 2
            k += 1
            t0 = ci * CH
            xc = xc_full[:, t0:t0 + CH]
            p.dma("sp", g_t[b][:], gt[blk, :, t0:t0 + CH], [], [r_gt[b]])
            p.copy("pool", xcb[b][:], xc, [r_xc[ci]], [r_xcb[b]])
            gates(blk, 1, xcb[b], r_xcb[b], xc, r_xc[ci])
            hrt, r_hrt = hr[b]
            hrp, r_hrp = hr[1 - b]
            init = 0.0 if ci == NCH - 1 else hrp[:, 0:1]
            rd = [r_aa, r_qq] + ([] if ci == NCH - 1 else [r_hrp])
            p.scan(hrt[:, ::-1], aa[:, ::-1], qq[:, ::-1], init, rd, [r_hrt])
            p.tt("pool", g2[:], g_t[b][:], g_t[b][:], ALU.mult, [r_gt[b]], [r_g2])
            p.ts("pool", g2[:], g2[:], 0.044715, 1.0, ALU.mult, ALU.add, [r_g2], [r_g2])
            p.tt("pool", g2[:], g2[:], g_t[b][:], ALU.mult, [r_g2, r_gt[b]], [r_g2])
            p.act(g2[:], g2[:], AF.Sigmoid, [r_g2], [r_g2], scale=1.5957691216057308)
            p.tt("pool", g2[:], g2[:], g_t[b][:], ALU.mult, [r_g2, r_gt[b]], [r_g2])
            p.tt("pool", rr[:], hrt[:], hf_full[:, t0:t0 + CH], ALU.add, [r_hrt, r_hf[ci]], [r_rr])
            ott, r_ott = ot[b]
            p.tt("dve", ott[:], rr[:], g2[:], ALU.mult, [r_rr, r_g2], [r_ott])
            p.dma("sp", out[blk, :, t0:t0 + CH], ott[:], [r_ott], [], track=r_ott)


def build_c(p, TOK, kind, tag, own=None):
    SBK = min(1024, TOK) if own is None else 512
    KC, NK = (96, 8) if kind == "lru" else (128, 6)
    SF = TOK if own is None else own[0]
    xT = p.dram_in("xT", [D, SF], F32)
    umem = p.dram_in("u_mem", [2, 128, SF], F32)
    tok = p.dram_in("tok", [NK, KC, TOK], BF16)
    memT = p.dram_in("memT", [D, 256], F32)
    gmem = p.dram_in("gmem", [128, 8], F32)
    wkv = p.dram_in("w_mem_kv", [D, 512], F32)
    gq = p.dram_in("gq", [128, 1], F32)
    gk = p.dram_in("gk", [128, 1], F32)
    w_out = p.dram_in("w_out", [D, D], F32)
    g2 = p.dram_in("g2", [128, 8], F32)
    w_gu = p.dram_in("w_gu", [D, 2 * DFF], F32)
    w_dn = p.dram_in("w_dn", [DFF, D], F32)
    xo = p.dram_out("xT_out", [D, TOK], F32)
    r_xo = p.reg("xo")
    c = setup_common(p, TOK, SBK)
    fs = FFN.alloc_shared(p, c)
    gmem_t, r_gmem = load_gain(p, "gmem", gmem, 8, 1.0)
    gq_t, r_gq = load_gain(p, "gq", gq, 1, 1.0)
    gk_t, r_gk = load_gain(p, "gk", gk, 1, 1.0)
    bd = p.sb("bd", [128, 128], BF16)
    r_bd = p.reg("bd")
    p.memset("dve", bd[:], 0.0, [r_bd])
    p.memset("dve", bd[0:64, 0:64], 1.0, [r_bd])
    p.memset("dve", bd[64:128, 64:128], 1.0, [r_bd])
    KmT = p.sb("KmT", [128, 2, 256], BF16)
    r_KmT = p.reg("KmT")
    Vm = p.sb("Vm", [128, 2, 4, 128], BF16)
    r_Vm = p.reg("Vm")
    p.memset("pool", Vm[:], 1.0, [r_Vm])
    for ch in range(8):
        p.dma("sp", c.x[:, ch, 0:256], memT[ch * 128:(ch + 1) * 128, :], [], [c.r_x[ch]])
    wkv_v = wkv.rearrange("(c p) n -> p c n", p=128)
    r_wkvK, r_wkvV = p.reg("wkvK"), p.reg("wkvV")
    p.dma("pool", fs.wg[0][:], wkv_v[:, :, 0:256], [], [r_wkvK])
    p.dma("pool", fs.wu[0][:], wkv_v[:, :, 256:512], [], [r_wkvV])
    for ch in range(8):
        p.act(c.sq[:, ch, 0:256], c.x[:, ch, 0:256], AF.Square, [c.r_x[ch]], [c.r_sq])
    for ch in range(8):
        p.mm(c.ps_ss[:, 0:256], c.ones[:], c.sq[:, ch, 0:256], ch == 0, ch == 7, [c.r_ones, c.r_sq], [c.r_ps_ss])
    rstd_from_ps(p, c, c.rstd[:, 0:256], c.ps_ss[:, 0:256], D, [c.r_ps_ss], [c.r_rstd])
    for ch in range(8):
        p.stt("dve", c.h[:, ch, 0:256], c.x[:, ch, 0:256], gmem_t[:, ch:ch + 1], c.rstd[:, 0:256], ALU.mult, ALU.mult,
              [c.r_x[ch], r_gmem, c.r_rstd], [c.r_h])
    for cc in range(2):
        psk = fs.ps_g[cc]
        for ch in range(8):
            p.mm(psk[:, 0:256], fs.wg[0][:, ch, cc * 128:(cc + 1) * 128], c.h[:, ch, 0:256], ch == 0, ch == 7,
                 [r_wkvK, c.r_h], [fs.r_psg[cc]])
        p.act(c.sq[:, cc, 0:256], psk[:, 0:256], AF.Square, [fs.r_psg[cc]], [c.r_sq])
        p.mm(c.ps_ss[:, 0:256], bd[:], c.sq[:, cc, 0:256], True, True, [r_bd, c.r_sq], [c.r_ps_ss])
        rstd_from_ps(p, c, c.rstd[:, 0:256], c.ps_ss[:, 0:256], 64, [c.r_ps_ss], [c.r_rstd])
        p.stt("dve", KmT[:, cc, :], psk[:, 0:256], gk_t[:, 0:1], c.rstd[:, 0:256], ALU.mult, ALU.mult,
              [fs.r_psg[cc], r_gk, c.r_rstd], [r_KmT])
    for kt in range(2):
        psv = fs.ps_u[kt]
        for ch in range(8):
            p.mm(psv[:, 0:256], c.h[:, ch, kt * 128:(kt + 1) * 128], fs.wu[0][:, ch, :], ch == 0, ch == 7,
                 [r_wkvV, c.r_h], [fs.r_psu[kt]])
        for h in range(4):
            off = 0 if h % 2 == 0 else 64
            p.copy("dve", Vm[:, kt, h, off:off + 64], psv[:, h * 64:(h + 1) * 64], [fs.r_psu[kt]], [r_Vm])
    ffn = FFN(p, c, "2" + tag, w_gu, w_dn, g2)
    wo_t, r_wo_t = load_w_resident(p, "wo_t", w_out[0:768, :], 768, D, kc=KC)
    wo_m, r_wo_m = load_w_resident(p, "wo_m", w_out[768:1024, :], 256, D, kc=128)
    um_t = fs.sg
    qn = p.sb("qn", [128, 2, 512], BF16)
    r_qn = p.reg("qn")
    mixm = p.sb("mixm", [128, 2, 512], BF16)
    r_mixm = p.reg("mixm")
    mixt = p.sb("mixt", [KC, NK, 512], BF16)
    r_mixt = p.reg("mixt")
    pT = [p.sb(f"pT{i}", [128, 512], BF16) for i in range(2)]
    r_pT = p.regs(2, "pT")
    rec, r_rec = c.rstd, c.r_rstd
    selk = [0]
    if own is not None:
        NQ = own[1]
        qmask = p.dram_in("qmask", [128, NQ], F32)
        qm_t, r_qm = load_gain(p, "qmask", qmask, NQ, 1.0)
        selt = [(p.sb(f"selt{i}", [128, 512], F32), p.reg(f"selt{i}")) for i in range(2)]

    def sel_load(dst, r_dst, src_fn):
        for qi in range(NQ):
            tmp, r_tmp = selt[selk[0] % 2]
            selk[0] += 1
            p.dma("sp", tmp[:], src_fn(qi), [], [r_tmp])
            if qi == 0:
                p.ts("dve", dst, tmp[:], qm_t[:, 0:1], None, ALU.mult, None, [r_tmp, r_qm], [r_dst])
            else:
                p.stt("dve", dst, tmp[:], qm_t[:, qi:qi + 1], dst, ALU.mult, ALU.add, [r_tmp, r_qm, r_dst], [r_dst])
    ks = 0
    for sbi in range(c.NSB):
        if own is None:
            load_x(p, c, xT, sbi)
        else:
            for ch in range(8):
                sel_load(c.x[:, ch, :], c.r_x[ch],
                         lambda qi, ch=ch: xT[ch * 128:(ch + 1) * 128, qi * TOK + sbi * SBK:qi * TOK + (sbi + 1) * SBK])
        for hf in range(c.NH):
            sl = slice(hf * 512, (hf + 1) * 512)
            tsl = slice(sbi * SBK + hf * 512, sbi * SBK + (hf + 1) * 512)
            for cc in range(2):
                if own is None:
                    p.dma("sp", um_t[cc][:], umem[cc, :, tsl], [], [fs.r_sg[cc]])
                else:
                    sel_load(um_t[cc][:], fs.r_sg[cc],
                             lambda qi, cc=cc: umem[cc, :, qi * TOK + tsl.start:qi * TOK + tsl.stop])
            for k in range(NK):
                p.dma("sp", mixt[:, k, :], tok[k, :, tsl], [], [r_mixt])
            for cc in range(2):
                p.act(c.sq[:, cc, :], um_t[cc][:], AF.Square, [fs.r_sg[cc]], [c.r_sq])
                p.mm(c.ps_ss[:], bd[:], c.sq[:, cc, :], True, True, [r_bd, c.r_sq], [c.r_ps_ss])
                rstd_from_ps(p, c, c.rstd[:], c.ps_ss[:], 64, [c.r_ps_ss], [c.r_rstd])
                p.stt("dve", qn[:, cc, :], um_t[cc][:], gq_t[:, 0:1], c.rstd[:], ALU.mult, ALU.mult,
                      [fs.r_sg[cc], r_gq, c.r_rstd], [r_qn])
            for h in range(4):
                cc, base = h // 2, (h % 2) * 64
                pso = fs.ps_u[h % 2]
                r_pso = fs.r_psu[h % 2]
                for kt in range(2):
                    k = ks % 2
                    ks += 1
                    p.mm(fs.ps_g[k][:], KmT[base:base + 64, cc, kt * 128:(kt + 1) * 128], qn[base:base + 64, cc, :],
                         True, True, [r_KmT, r_qn], [fs.r_psg[k]])
                    p.act(pT[k][:], fs.ps_g[k][:], AF.Exp, [fs.r_psg[k]], [r_pT[k]], scale=0.125)
                    p.mm(pso[:], Vm[:, kt, h, :], pT[k][:], kt == 0, kt == 1, [r_Vm, r_pT[k]], [r_pso])
                if h % 2 == 0:
                    p.recip(rec[64:128, :], pso[64:128, :], [r_pso], [r_rec])
                    p.tt("dve", mixm[0:64, cc, :], pso[0:64, :], rec[64:128, :], ALU.mult, [r_pso, r_rec], [r_mixm])
                else:
                    p.recip(rec[0:64, :], pso[0:64, :], [r_pso], [r_rec])
                    p.tt("dve", mixm[64:128, cc, :], pso[64:128, :], rec[0:64, :], ALU.mult, [r_pso, r_rec], [r_mixm])
            for o in range(8):
                k = fs.ky % 2
                fs.ky += 1
                osl = slice(o * 128, (o + 1) * 128)
                for kk in range(NK):
                    p.mm(fs.ps_y[k][:], wo_t[:, kk, osl], mixt[:, kk, :], kk == 0, False, [r_wo_t, r_mixt], [fs.r_psy[k]])
                for cc in range(2):
                    p.mm(fs.ps_y[k][:], wo_m[:, cc, osl], mixm[:, cc, :], False, cc == 1, [r_wo_m, r_mixm], [fs.r_psy[k]])
                p.tt("dve", c.x[:, o, sl], fs.ps_y[k][:], c.x[:, o, sl], ALU.add, [fs.r_psy[k], c.r_x[o]], [c.r_x[o]])
        ffn.run(fs, extra=([r_wkvK], [r_wkvV]) if sbi == 0 else None)
        store_x(p, c, xo, r_xo, sbi)


def mla_consts():
    invf = np.zeros((96, 1), np.float32)
    inv = (10000.0 ** (-np.arange(16, dtype=np.float32) * np.float32(2.0 / 32))).astype(np.float32)
    invf[64:80, 0] = inv
    invf[80:96, 0] = inv
    sgn = np.zeros((96, 1), np.float32)
    sgn[64:80] = -1.0
    sgn[80:96] = 1.0
    cm = np.zeros((96, 1), np.float32)
    cm[64:96] = 1.0
    cb = np.zeros((96, 1), np.float32)
    cb[0:64] = 1.0
    Pm = np.zeros((96, 96), np.float32)
    for j in range(16):
        Pm[80 + j, 64 + j] = 1.0
        Pm[64 + j, 80 + j] = 1.0
    return {"invf": invf, "sgn": sgn, "cm": cm, "cb": cb, "Pm": Pm}


def build_mla_q(p, TOK, own=None):
    NHALF = TOK // 512
    ut = p.dram_in("u_tok", [6, 128, TOK], F32)
    pos = p.dram_in("pos", [1, TOK], I32)
    gqa = p.dram_in("gqa", [128, 3], F32)
    w_uq = p.dram_in("w_uq", [384, 1152], F32)
    gkva = p.dram_in("gkva", [128, 2], F32)
    w_ukv = p.dram_in("w_ukv", [256, 1536], F32)
    gqn = p.dram_in("gqn", [96, 1], F32)
    gkn = p.dram_in("gkn", [96, 1], F32)
    invf = p.dram_in("invf", [96, 1], F32)
    sgn = p.dram_in("sgn", [96, 1], F32)
    cm = p.dram_in("cm", [96, 1], F32)
    cb = p.dram_in("cb", [96, 1], F32)
    Pm = p.dram_in("Pm", [96, 96], F32)
    qTo = p.dram_out("qT", [12, 96, TOK], BF16)
    KTo = p.dram_out("KT", [12, 96, TOK], BF16)
    Vo = p.dram_out("Vaug", [12, 128, TOK // 128, 128], BF16)
    r_qo, r_ko, r_vo = p.reg("qo"), p.reg("ko"), p.reg("vo")
    c = Ctx()
    c.ones = p.sb("ones", [128, 128], BF16)
    c.r_ones = p.reg("ones")
    p.memset("dve", c.ones[:], 1.0, [c.r_ones])
    c.eps = p.sb("eps", [128, 1], F32)
    c.r_eps = p.reg("eps")
    p.memset("dve", c.eps[:], EPS, [c.r_eps])
    c.rstd = p.sb("rstd_sb", [128, 512], F32)
    c.r_rstd = p.reg("rstd")
    c.ps_ss = p.ps("ps_ss", [128, 512])
    c.r_ps_ss = p.reg("ps_ss")

    def small(name, src, shape):
        t = p.sb(name, shape, F32)
        r = p.reg(name)
        p.dma("sp", t[:], src, [], [r])
        return t, r
    gqa_t, r_gqa = small("gqa", gqa, [128, 3])
    gkva_t, r_gkva = small("gkva", gkva, [128, 2])
    gqn_t, r_gqn = small("gqn", gqn, [96, 1])
    gkn_t, r_gkn = small("gkn", gkn, [96, 1])
    invf_t, r_invf = small("invf", invf, [96, 1])
    sgn_t, r_sgn = small("sgn", sgn, [96, 1])
    cm_t, r_cm = small("cm", cm, [96, 1])
    cb_t, r_cb = small("cb", cb, [96, 1])
    Pm_t, r_Pm = small("Pm", Pm, [96, 96])
    wuq, r_wuq = load_w_resident(p, "wuq", w_uq, 384, 1152)
    wukv, r_wukv = load_w_resident(p, "wukv", w_ukv, 256, 1536)
    wukv_h = wukv[:].rearrange("p c (h t) -> p c h t", t=128)

    def t32(name, rows=96, dt=F32):
        return p.sb(name, [rows, 512], dt), p.reg(name)
    cq = p.sb("cq", [128, 3, 512], F32); r_cq = p.reg("cq")
    ckv = p.sb("ckv", [128, 2, 512], F32); r_ckv = p.reg("ckv")
    sq = p.sb("sq", [128, 3, 512], BF16); r_sq = p.reg("sq")
    cqn = p.sb("cqn", [128, 3, 512], BF16); r_cqn = p.reg("cqn")
    ckvn = p.sb("ckvn", [128, 2, 512], BF16); r_ckvn = p.reg("ckvn")
    posi = p.sb("posi", [96, 512], I32); r_posi = p.reg("posi")
    ang, r_ang = t32("ang")
    kf, r_kf = t32("kf")
    ki = p.sb("ki", [96, 512], I32); r_ki = p.reg("ki")
    rc, r_rc = t32("rc")
    St, r_S = t32("S")
    Ct, r_C = t32("C")
    qn = [t32(f"qn{i}") for i in range(2)]
    t1, r_t1 = t32("t1")
    t2, r_t2 = t32("t2")
    sqh = [t32(f"sqh{i}", 96, BF16) for i in range(2)]
    rstdh = [t32(f"rstdh{i}") for i in range(2)]
    qst = [t32(f"qst{i}", 96, BF16) for i in range(2)]
    krw, r_krw = t32("krw")
    kq, r_kq = t32("kq")
    krr, r_krr = t32("krr")
    sqk, r_sqk = t32("sqk", 96, BF16)
    kst = [t32(f"kst{i}", 96, BF16) for i in range(2)]
    vst = p.sb("vst", [128, 12, 4, 128], BF16); r_vst = p.reg("vst")
    p.memset("pool", vst[:], 1.0, [r_vst])
    p.memset("pool", kq[:], 0.0, [r_kq])
    psA = [p.ps(f"psA{i}", [128, 512]) for i in range(2)]; r_psA = p.regs(2, "psA")
    psB = [p.ps(f"psB{i}", [128, 512]) for i in range(2)]; r_psB = p.regs(2, "psB")
    psC = p.ps("psC", [128, 512]); r_psC = p.reg("psC")
    psV = [p.ps(f"psV{i}", [128, 384]) for i in range(2)]; r_psV = p.regs(2, "psV")
    TWO_PI = 2.0 * PI

    def wrap(t, r_t):
        p.ts("dve", kf[:], t[:], PI, None, ALU.is_gt, None, [r_t], [r_kf])
        p.stt("dve", t[:], kf[:], -TWO_PI, t[:], ALU.mult, ALU.add, [r_kf, r_t], [r_t])
        p.ts("dve", kf[:], t[:], -PI, None, ALU.is_lt, None, [r_t], [r_kf])
        p.stt("dve", t[:], kf[:], TWO_PI, t[:], ALU.mult, ALU.add, [r_kf, r_t], [r_t])
        p.ts("dve", t[:], t[:], PI, -PI, ALU.min, ALU.max, [r_t], [r_t])

    def rope_apply(src, r_src, dst_ap, r_dst, rows):
        p.mm(psC[0:96, :], Pm_t[:], src[:], True, True, [r_Pm, r_src], [r_psC])
        p.tt("dve", t1[rows, :], psC[rows, :], St[rows, :], ALU.mult, [r_psC, r_S], [r_t1])
        p.tt("pool", t2[rows, :], src[rows, :], Ct[rows, :], ALU.mult, [r_src, r_C], [r_t2])
        p.tt("pool", dst_ap, t1[rows, :], t2[rows, :], ALU.add, [r_t1, r_t2], [r_dst])

    kA = [0]
    if own is not None:
        TOKQ, NQ = own
        qmask = p.dram_in("qmask", [128, NQ], F32)
        qm_t, r_qm = small("qmask", qmask, [128, NQ])
        selt = [(p.sb(f"selt{i}", [128, 512], F32), p.reg(f"selt{i}")) for i in range(2)]
    selk = [0]

    def tables(cands):
        if len(cands) == 1:
            p.dma("sp", posi[:], pos[0:1, cands[0]].partition_broadcast(96), [], [r_posi])
            p.copy("dve", ang[:], posi[:], [r_posi], [r_ang])
        else:
            for qi, ts_ in enumerate(cands):
                p.dma("sp", posi[:], pos[0:1, ts_].partition_broadcast(96), [], [r_posi])
                p.copy("dve", kf[:], posi[:], [r_posi], [r_kf])
                if qi == 0:
                    p.ts("dve", ang[:], kf[:], qm_t[0:96, 0:1], None, ALU.mult, None, [r_kf, r_qm], [r_ang])
                else:
                    p.stt("dve", ang[:], kf[:], qm_t[0:96, qi:qi + 1], ang[:], ALU.mult, ALU.add, [r_kf, r_qm, r_ang], [r_ang])
        p.ts("dve", ang[:], ang[:], invf_t[:, 0:1], None, ALU.mult, None, [r_ang, r_invf], [r_ang])
        p.ts("dve", kf[:], ang[:], 1.0 / TWO_PI, None, ALU.mult, None, [r_ang], [r_kf])
        p.copy("dve", ki[:], kf[:], [r_kf], [r_ki])
        p.copy("dve", kf[:], ki[:], [r_ki], [r_kf])
        p.stt("dve", ang[:], kf[:], -TWO_PI, ang[:], ALU.mult, ALU.add, [r_kf, r_ang], [r_ang])
        wrap(ang, r_ang)
        p.ts("dve", rc[:], ang[:], PI / 2, None, ALU.add, None, [r_ang], [r_rc])
        wrap(rc, r_rc)
        p.act(St[:], ang[:], AF.Sin, [r_ang], [r_S])
        p.ts("dve", St[:], St[:], sgn_t[:, 0:1], None, ALU.mult, None, [r_S, r_sgn], [r_S])
        p.act(Ct[:], rc[:], AF.Sin, [r_rc], [r_C])
        p.ts("dve", Ct[:], Ct[:], cm_t[:, 0:1], cb_t[:, 0:1], ALU.mult, ALU.add, [r_C, r_cm, r_cb], [r_C])

    def lat_norm(src, r_src, n, g_t, r_g, dst, r_dst, dim):
        for cc in range(n):
            p.act(sq[:, cc, :], src[:, cc, :], AF.Square, [r_src], [r_sq])
        for cc in range(n):
            p.mm(c.ps_ss[:], c.ones[:], sq[:, cc, :], cc == 0, cc == n - 1, [c.r_ones, r_sq], [c.r_ps_ss])
        rstd_from_ps(p, c, c.rstd[:], c.ps_ss[:], dim, [c.r_ps_ss], [c.r_rstd])
        for cc in range(n):
            p.stt("dve", dst[:, cc, :], src[:, cc, :], g_t[:, cc:cc + 1], c.rstd[:], ALU.mult, ALU.mult,
                  [r_src, r_g, c.r_rstd], [r_dst])

    def q_part(cands, osl):
        for cc in range(3):
            if len(cands) == 1:
                p.dma("sp", cq[:, cc, :], ut[cc, :, cands[0]], [], [r_cq])
            else:
                for qi, ts_ in enumerate(cands):
                    tmp, r_tmp = selt[selk[0] % 2]
                    selk[0] += 1
                    p.dma("sp", tmp[:], ut[cc, :, ts_], [], [r_tmp])
                    if qi == 0:
                        p.ts("dve", cq[:, cc, :], tmp[:], qm_t[:, 0:1], None, ALU.mult, None, [r_tmp, r_qm], [r_cq])
                    else:
                        p.stt("dve", cq[:, cc, :], tmp[:], qm_t[:, qi:qi + 1], cq[:, cc, :], ALU.mult, ALU.add,
                              [r_tmp, r_qm, r_cq], [r_cq])
        lat_norm(cq, r_cq, 3, gqa_t, r_gqa, cqn, r_cqn, 384)
        for h in range(12):
            k = kA[0] % 2
            kA[0] += 1
            for cc in range(3):
                p.mm(psA[k][0:96, :], wuq[:, cc, h * 96:(h + 1) * 96], cqn[:, cc, :], cc == 0, cc == 2, [r_wuq, r_cqn], [r_psA[k]])
            sqh_t, r_sqh = sqh[k]
            rs_t, r_rs = rstdh[k]
            qn_t, r_qn = qn[k]
            qs_t, r_qs = qst[k]
            p.act(sqh_t[:], psA[k][0:96, :], AF.Square, [r_psA[k]], [r_sqh])
            p.mm(psB[k][0:96, :], c.ones[0:96, 0:96], sqh_t[:], True, True, [c.r_ones, r_sqh], [r_psB[k]])
            rstd_from_ps(p, c, rs_t[:], psB[k][0:96, :], 96, [r_psB[k]], [r_rs])
            p.stt("dve", qn_t[:], psA[k][0:96, :], gqn_t[:, 0:1], rs_t[:], ALU.mult, ALU.mult, [r_psA[k], r_gqn, r_rs], [r_qn])
            rope_apply(qn_t, r_qn, qs_t[:], r_qs, slice(0, 96))
            p.dma("sp", qTo[h, :, osl], qs_t[:], [r_qs], [], track=r_qs)

    def kv_part(tsl, hi):
        for cc in range(2):
            p.dma("sp", ckv[:, cc, :], ut[3 + cc, :, tsl], [], [r_ckv])
        p.dma("sp", krw[64:96, :], ut[5, 64:96, tsl], [], [r_krw])
        lat_norm(ckv, r_ckv, 2, gkva_t, r_gkva, ckvn, r_ckvn, 256)
        p.ts("dve", kq[64:96, :], krw[64:96, :], gkn_t[64:96, 0:1], None, ALU.mult, None, [r_krw, r_gkn], [r_kq])
        rope_apply(kq, r_kq, krr[64:96, :], r_krr, slice(64, 96))
        p.act(sqk[64:96, :], krw[64:96, :], AF.Square, [r_krw], [r_sqk])
        for h in range(12):
            k = kA[0] % 2
            kA[0] += 1
            for cc in range(2):
                p.mm(psA[k][0:64, :], wukv[:, cc, h * 128:h * 128 + 64], ckvn[:, cc, :], cc == 0, cc == 1, [r_wukv, r_ckvn], [r_psA[k]])
            rs_t, r_rs = rstdh[k]
            ks_t, r_ks = kst[k]
            p.act(sqk[0:64, :], psA[k][0:64, :], AF.Square, [r_psA[k]], [r_sqk])
            p.mm(psB[k][0:96, :], c.ones[0:96, 0:96], sqk[:], True, True, [c.r_ones, r_sqk], [r_psB[k]])
            rstd_from_ps(p, c, rs_t[:], psB[k][0:96, :], 96, [r_psB[k]], [r_rs])
            p.stt("dve", ks_t[0:64, :], psA[k][0:64, :], gkn_t[0:64, 0:1], rs_t[0:64, :], ALU.mult, ALU.mult,
                  [r_psA[k], r_gkn, r_rs], [r_ks])
            p.tt("dve", ks_t[64:96, :], krr[64:96, :], rs_t[64:96, :], ALU.mult, [r_krr, r_rs], [r_ks])
            p.dma("sp", KTo[h, :, tsl], ks_t[:], [r_ks], [], track=r_ks)
        for tt_ in range(4):
            for g in range(2):
                for cc in range(2):
                    p.mm(psV[g][:].rearrange("p (h t) -> p h t", t=64), ckvn[:, cc, tt_ * 128:(tt_ + 1) * 128],
                         wukv_h[:, cc, g * 6:(g + 1) * 6, 64:128], cc == 0, cc == 1, [r_wukv, r_ckvn], [r_psV[g]])
                pv3 = psV[g][:].rearrange("p (h t) -> p h t", t=64)
                p.copy("act", vst[:, g * 6:(g + 1) * 6:2, tt_, 0:64], pv3[:, 0:6:2, :], [r_psV[g]], [r_vst])
                p.copy("act", vst[:, g * 6 + 1:(g + 1) * 6:2, tt_, 64:128], pv3[:, 1:6:2, :], [r_psV[g]], [r_vst])
        for h in range(12):
            p.dma("sp", Vo[h, :, hi * 4:(hi + 1) * 4, :], vst[:, h, :, :], [r_vst], [], track=r_vst)

    for hi in range(NHALF):
        tsl = slice(hi * 512, (hi + 1) * 512)
        tables([tsl])
        if own is None:
            q_part([tsl], tsl)
        kv_part(tsl, hi)
    if own is not None:
        for jq in range(TOKQ // 512):
            cands = [slice(qi * TOKQ + jq * 512, qi * TOKQ + (jq + 1) * 512) for qi in range(NQ)]
            tables(cands)
            q_part(cands, slice(jq * 512, (jq + 1) * 512))


def build_attn(p, TOK, S):
    NQB = TOK // 512
    NKT = S // 128
    qT = p.dram_in("qT", [12, 96, TOK], BF16)
    KT = p.dram_in("KT", [12, 96, S], BF16)
    Va = p.dram_in("Va", [12, 128, NKT, 128], BF16)
    out = p.dram_out("attn", [6, 128, TOK], BF16)
    r_out = p.reg("out")
    Kb = [p.sb(f"Kb{i}", [96, S], BF16) for i in range(2)]
    Vb = [p.sb(f"Vb{i}", [128, NKT, 128], BF16) for i in range(2)]
    qb_ = [p.sb(f"qb{i}", [96, TOK], BF16) for i in range(2)]
    r_K, r_V, r_q = p.regs(2, "K"), p.regs(2, "V"), p.regs(2, "q")
    NS = 4
    ps_s = [p.ps(f"ps_s{i}", [128, 512]) for i in range(NS)]
    r_pss = p.regs(NS, "pss")
    ps_o = [p.ps(f"ps_o{i}", [128, 512]) for i in range(2)]
    r_pso = p.regs(2, "pso")
    NP = 3
    pT = [p.sb(f"pT{i}", [128, 512], BF16) for i in range(NP)]
    r_pT = p.regs(NP, "pT")
    rec = [p.sb(f"rec{i}", [128, 512], F32) for i in range(2)]
    r_rec = p.regs(2, "rec")
    ost = [p.sb(f"ost{i}", [128, 512], BF16) for i in range(2)]
    r_ost = p.regs(2, "ost")
    SCALE = float(96 ** -0.5)
    LA = 2

    def load(h):
        b = h % 2
        p.dma("sp", Kb[b][:], KT[h], [], [r_K[b]])
        p.dma("sp", Vb[b][:], Va[h], [], [r_V[b]])
        p.dma("sp", qb_[b][:], qT[h], [], [r_q[b]])

    its = [(h, qb, kt) for h in range(12) for qb in range(NQB) for kt in range(NKT)]
    N = len(its)

    def rec_S(i):
        h, qb, kt = its[i]
        b = h % 2
        p.mm(ps_s[i % NS][:], Kb[b][:, kt * 128:(kt + 1) * 128], qb_[b][:, qb * 512:(qb + 1) * 512], True, True,
             [r_K[b], r_q[b]], [r_pss[i % NS]])

    load(0)
    loaded = 0
    for i in range(min(LA, N)):
        rec_S(i)
    for i in range(N):
        h, qb, kt = its[i]
        b = h % 2
        if kt == 0 and qb == 0 and h + 1 < 12 and loaded < h + 1:
            load(h + 1)
            loaded = h + 1
        if i + LA < N:
            rec_S(i + LA)
        p.act(pT[i % NP][:], ps_s[i % NS][:], AF.Exp, [r_pss[i % NS]], [r_pT[i % NP]], scale=SCALE)
        o = (h * NQB + qb) % 2
        p.mm(ps_o[o][:], Vb[b][:, kt, :], pT[i % NP][:], kt == 0, kt == NKT - 1, [r_V[b], r_pT[i % NP]], [r_pso[o]])
        if kt == NKT - 1:
            base = (h % 2) * 64
            oth = 64 - base
            p.recip(rec[o][oth:oth + 64, :], ps_o[o][oth:oth + 64, :], [r_pso[o]], [r_rec[o]])
            p.tt("dve", ost[o][base:base + 64, :], ps_o[o][base:base + 64, :], rec[o][oth:oth + 64, :], ALU.mult,
                 [r_pso[o], r_rec[o]], [r_ost[o]])
            p.dma("sp", out[h // 2, base:base + 64, qb * 512:(qb + 1) * 512], ost[o][base:base + 64, :], [r_ost[o]], [],
                  track=r_ost[o])


def build_fused(S, depth=4, max_phases=None, only=None, NQ=4):
    p = Prog()
    nph = [0]

    def more():
        nph[0] += 1
        if only is not None:
            return nph[0] in only
        return max_phases is None or nph[0] <= max_phases
    nl, nm = (depth + 1) // 2, depth // 2

    def ein(name, shape, dt=F32):
        return p.ext_in(name, shape, dt)
    xT = ein("xT", [D, S])
    memT = ein("memT", [D, 256])
    pos = ein("pos", [1, S], I32)
    g1 = ein("ffn1_norm", [depth, 128, 8])
    gu1 = ein("ffn1_w_gate_up", [depth, D, 2 * DFF])
    dn1 = ein("ffn1_w_down", [depth, DFF, D])
    gmix = ein("mix_norm", [depth, 128, 8])
    gmem = ein("mem_norm", [depth, 128, 8])
    wkv = ein("w_mem_kv", [depth, D, 512])
    gq = ein("mem_q_norm", [depth, 128, 1])
    gk = ein("mem_k_norm", [depth, 128, 1])
    wo = ein("w_out", [depth, D, D])
    g2 = ein("ffn2_norm", [depth, 128, 8])
    gu2 = ein("ffn2_w_gate_up", [depth, D, 2 * DFF])
    dn2 = ein("ffn2_w_down", [depth, DFF, D])
    lwin = ein("lru_w_in", [nl, D, 1792])
    lcw = ein("lru_cw", [nl, 96, 8, 4])
    lcb = ein("lru_cb", [nl, 96, 8])
    lgw = ein("lru_gw", [nl, 96, 8, 4, 96])
    lgb = ein("lru_gb", [nl, 96, 8, 4])
    llam = ein("lru_lam", [nl, 96, 16])
    mwin = ein("mla_w_in", [nm, D, 928])
    mgqa = ein("mla_q_a_norm", [nm, 128, 3])
    mwuq = ein("mla_w_uq", [nm, 384, 1152])
    mgkva = ein("mla_kv_a_norm", [nm, 128, 2])
    mwukv = ein("mla_w_ukv", [nm, 256, 1536])
    mgqn = ein("mla_q_norm", [nm, 96, 1])
    mgkn = ein("mla_k_norm", [nm, 96, 1])
    cst = {k: ein(k, list(v.shape)) for k, v in mla_consts().items()}
    last_own = (depth % 2 == 0)
    TOKQ = S // NQ
    out = p.ext_out("xT_out", [D, TOKQ if last_own else S], F32)
    qmask = ein("qmask", [128, NQ])
    xs = p.dram_tmp("xs", [D, S], F32)
    utok_l = p.dram_tmp("utok_l", [16, 96, S], F32)
    utok_m = p.dram_tmp("utok_m", [6, 128, S], F32)
    umem = p.dram_tmp("umem", [2, 128, S], F32)
    tok_l = p.dram_tmp("tok_l", [8, 96, S], BF16)
    attn = p.dram_tmp("attn_s", [6, 128, S], BF16)
    qTs = p.dram_tmp("qT_s", [12, 96, S], BF16)
    KTs = p.dram_tmp("KT_s", [12, 96, S], BF16)
    Vas = p.dram_tmp("Va_s", [12, 128, S // 128, 128], BF16)
    for l in range(depth):
        j = l // 2
        kind = "lru" if l % 2 == 0 else "mla"
        xin = xT if l == 0 else xs
        xout = out if l == depth - 1 else xs
        if more():
            p.begin_phase()
            p.io = {"xT": xin, "g1": g1[l], "gm": gmix[l], "w_gu": gu1[l], "w_dn": dn1[l],
                    "w_in": lwin[j] if kind == "lru" else mwin[j], "xT_out": xs,
                    "u_tok": utok_l if kind == "lru" else utok_m, "u_mem": umem}
            build_a(p, S, kind, f"L{l}")
            p.end_phase()
        if kind == "lru":
            if more():
                p.begin_phase()
                p.io = {"xr": utok_l[8:16], "gate": utok_l[0:8], "cw": lcw[j], "cb": lcb[j], "gw": lgw[j], "gb": lgb[j],
                        "lam": llam[j], "tok": tok_l}
                build_lru_b(p, S, 8)
                p.end_phase()
            tk = tok_l
        else:
            if more():
                p.begin_phase()
                p.io = {"u_tok": utok_m, "pos": pos, "gqa": mgqa[j], "w_uq": mwuq[j], "gkva": mgkva[j], "w_ukv": mwukv[j],
                        "gqn": mgqn[j], "gkn": mgkn[j], "qT": qTs, "KT": KTs, "Vaug": Vas}
                p.io.update(cst)
                ownl = last_own and l == depth - 1
                if ownl:
                    p.io["qmask"] = qmask
                build_mla_q(p, S, own=(TOKQ, NQ) if ownl else None)
                p.end_phase()
            if more():
                p.begin_phase()
                if ownl:
                    p.io = {"qT": qTs[:, :, 0:TOKQ], "KT": KTs, "Va": Vas, "attn": attn[:, :, 0:TOKQ]}
                    build_attn(p, TOKQ, S)
                else:
                    p.io = {"qT": qTs, "KT": KTs, "Va": Vas, "attn": attn}
                    build_attn(p, S, S)
                p.end_phase()
            tk = attn
        if more():
            p.begin_phase()
            p.io = {"xT": xs, "u_mem": umem, "tok": tk, "memT": memT, "gmem": gmem[l], "w_mem_kv": wkv[l], "gq": gq[l],
                    "gk": gk[l], "w_out": wo[l], "g2": g2[l], "w_gu": gu2[l], "w_dn": dn2[l], "xT_out": xout}
            if last_own and l == depth - 1:
                p.io["tok"] = tk[:, :, 0:TOKQ]
                p.io["qmask"] = qmask
                build_c(p, TOKQ, kind, f"L{l}", own=(S, NQ))
            else:
                build_c(p, S, kind, f"L{l}")
            p.end_phase()
    return p.finish()


_PROGS = {}


def _f(a):
    return np.ascontiguousarray(np.asarray(a, dtype=np.float32))


def _prep(x, mem, positions, ffn1_norm, ffn1_w_gate_up, ffn1_w_down, mix_norm, mem_norm,
          w_mem_kv, mem_q_norm, mem_k_norm, w_out, ffn2_norm, ffn2_w_gate_up, ffn2_w_down,
          lru_w_in, lru_conv_w, lru_conv_b, lru_gate_w, lru_gate_b, lru_lambda,
          mla_w_in, mla_q_a_norm, mla_w_uq, mla_kv_a_norm, mla_w_ukv, mla_q_norm, mla_k_norm):
    x = np.asarray(x, np.float32)
    mem = np.asarray(mem, np.float32)
    positions = np.asarray(positions, np.int32)
    B, S, _ = x.shape
    depth = np.asarray(ffn1_norm).shape[0]
    nl, nm = (depth + 1) // 2, depth // 2
    QPB = NCORES // B
    TOK = S // QPB

    def lay(g, n):
        g = np.asarray(g, np.float32)
        return np.ascontiguousarray(g.reshape(g.shape[0], n, 128).transpose(0, 2, 1))
    cw = np.asarray(lru_conv_w, np.float32).reshape(nl, 4, 8, 96).transpose(0, 3, 2, 1)
    cb = np.asarray(lru_conv_b, np.float32).reshape(nl, 8, 96).transpose(0, 2, 1)
    gw = np.asarray(lru_gate_w, np.float32).transpose(0, 4, 3, 1, 2, 5).reshape(nl, 96, 8, 4, 96)
    gb = np.asarray(lru_gate_b, np.float32).reshape(nl, 2, 2, 8, 96).transpose(0, 4, 3, 1, 2).reshape(nl, 96, 8, 4)
    lam = np.asarray(lru_lambda, np.float32).reshape(nl, 2, 8, 96).transpose(0, 3, 2, 1).reshape(nl, 96, 16)
    shared = {
        "ffn1_norm": lay(ffn1_norm, 8), "ffn1_w_gate_up": _f(ffn1_w_gate_up), "ffn1_w_down": _f(ffn1_w_down),
        "mix_norm": lay(mix_norm, 8), "mem_norm": lay(mem_norm, 8), "w_mem_kv": _f(w_mem_kv),
        "mem_q_norm": _f(np.tile(np.asarray(mem_q_norm, np.float32), (1, 2)).reshape(depth, 128, 1)),
        "mem_k_norm": _f(np.tile(np.asarray(mem_k_norm, np.float32), (1, 2)).reshape(depth, 128, 1)),
        "w_out": _f(w_out), "ffn2_norm": lay(ffn2_norm, 8), "ffn2_w_gate_up": _f(ffn2_w_gate_up),
        "ffn2_w_down": _f(ffn2_w_down), "lru_w_in": _f(lru_w_in), "lru_cw": _f(cw), "lru_cb": _f(cb), "lru_gw": _f(gw),
        "lru_gb": _f(gb), "lru_lam": _f(lam), "mla_w_in": _f(mla_w_in), "mla_q_a_norm": lay(mla_q_a_norm, 3),
        "mla_w_uq": _f(mla_w_uq), "mla_kv_a_norm": lay(mla_kv_a_norm, 2), "mla_w_ukv": _f(mla_w_ukv),
        "mla_q_norm": _f(np.asarray(mla_q_norm, np.float32).reshape(nm, 96, 1)),
        "mla_k_norm": _f(np.asarray(mla_k_norm, np.float32).reshape(nm, 96, 1)),
    }
    shared.update(mla_consts())
    xTb = [np.ascontiguousarray(x[b].T) for b in range(B)]
    memTb = [np.ascontiguousarray(mem[b].T) for b in range(B)]
    posb = [np.ascontiguousarray(positions[b].reshape(1, S)) for b in range(B)]
    qm = [np.ascontiguousarray(np.tile(np.eye(QPB, dtype=np.float32)[q], (128, 1))) for q in range(QPB)]
    in_maps = [dict(shared, xT=xTb[c // QPB], memT=memTb[c // QPB], pos=posb[c // QPB], qmask=qm[c % QPB])
               for c in range(NCORES)]
    return in_maps, B, S, depth, QPB, TOK


def kernel(**inputs):
    in_maps, B, S, depth, QPB, TOK = _prep(**inputs)
    key = (S, depth)
    if key not in _PROGS:
        _PROGS[key] = build_fused(S, depth)
    res = run_bass_kernel_spmd(_PROGS[key], in_maps, core_ids=list(range(NCORES))).results
    out = np.empty((B, S, D), np.float32)
    for c in range(NCORES):
        b, q = c // QPB, c % QPB
        o = res[c]["xT_out"]
        out[b, q * TOK:(q + 1) * TOK, :] = (o if o.shape[1] == TOK else o[:, q * TOK:(q + 1) * TOK]).T
    return out
```

```python
import numpy as np
import ml_dtypes
from contextlib import ExitStack
import concourse.bass as bass
import concourse.mybir as mybir
from concourse.bass_utils import run_bass_kernel_spmd

F32, BF16, I32 = mybir.dt.float32, mybir.dt.bfloat16, mybir.dt.int32
ALU = mybir.AluOpType
AF = mybir.ActivationFunctionType
NPBF = ml_dtypes.bfloat16

D = 1024
DFF = 2816
NJ = DFF // 128
EPS = 1e-6
NCORES = 8
PI = float(np.pi)


class Reg:
    __slots__ = ("name", "w", "rs", "guard", "sem")

    def __init__(self, name):
        self.name = name
        self.w = None
        self.rs = {}
        self.guard = []
        self.sem = None


class SemSlot:
    __slots__ = ("h", "count", "sw")

    def __init__(self, h, sw=False):
        self.h = h
        self.count = 0
        self.sw = sw


class Op:
    __slots__ = ("eng", "fn", "waits", "signal", "track", "idx", "ticket")


class Prog:
    ENGS = ["pe", "act", "dve", "pool", "sp"]

    def __init__(self):
        self.nc = bass.Bass("TRN2", target_bir_lowering=False)
        self.es = ExitStack()
        self.pes = None
        self.ops = {e: [] for e in self.ENGS}
        self.seen = {e: {} for e in self.ENGS}
        self.nreg = 0
        self.esem = {}
        self.all_regs = []
        self.free_slots = []
        self.phase_slots = []
        self.nslots = 0
        self.emitted = {e: 0 for e in self.ENGS}
        self.tbase = {e: 0 for e in self.ENGS}
        self.io = {}
        for e in ["pe", "act", "dve", "pool"]:
            self.esem[e] = self.es.enter_context(self.nc.semaphore(f"s_{e}"))

    def reg(self, name=None):
        self.nreg += 1
        r = Reg(name or f"r{self.nreg}")
        self.all_regs.append(r)
        return r

    def regs(self, n, name="r"):
        return [self.reg(f"{name}{i}") for i in range(n)]

    def sb(self, name, shape, dt):
        self.nreg += 1
        return self.pes.enter_context(self.nc.sbuf_tensor(f"s{self.nreg}_" + name, list(shape), dt))

    def ps(self, name, shape, dt=F32):
        self.nreg += 1
        return self.pes.enter_context(self.nc.psum_tensor(f"p{self.nreg}_" + name, list(shape), dt))

    def dram_in(self, name, shape, dt):
        if name in self.io:
            ap = self.io[name]
            assert list(ap.shape) == list(shape) and ap.dtype == dt, (name, ap.shape, shape)
            return ap
        return self.nc.dram_tensor(name, list(shape), dt, kind="ExternalInput").ap()

    def dram_out(self, name, shape, dt):
        if name in self.io:
            ap = self.io[name]
            assert list(ap.shape) == list(shape) and ap.dtype == dt, (name, ap.shape, shape)
            return ap
        return self.nc.dram_tensor(name, list(shape), dt, kind="ExternalOutput").ap()

    def ext_in(self, name, shape, dt):
        return self.nc.dram_tensor(name, list(shape), dt, kind="ExternalInput").ap()

    def ext_out(self, name, shape, dt):
        return self.nc.dram_tensor(name, list(shape), dt, kind="ExternalOutput").ap()

    def dram_tmp(self, name, shape, dt):
        if name in self.io:
            return self.io[name]
        return self.nc.dram_tensor(name, list(shape), dt).ap()

    def _slot(self, reg, q):
        if reg.sem is None:
            if q != "pool" and self.free_slots:
                reg.sem = self.free_slots.pop()
            else:
                self.nslots += 1
                reg.sem = SemSlot(self.es.enter_context(self.nc.semaphore(f"d{self.nslots}")), sw=(q == "pool"))
            self.phase_slots.append(reg.sem)
        assert q != "pool" or reg.sem.sw, f"software DMA onto a recycled semaphore ({reg.name})"
        return reg.sem

    def _add(self, eng, fn, reads, writes, track=None):
        op = Op()
        op.eng, op.fn, op.signal, op.track = eng, fn, False, track
        lst = self.ops[eng]
        op.idx = len(lst)
        waits = []
        for r in reads:
            if r.w is not None:
                waits.append(r.w)
        for r in writes:
            joining = (track is not None and r.w is not None and r.w[0] == "d"
                       and r.w[1] is track.sem and not r.rs)
            if joining:
                waits.extend(r.guard)
            else:
                g = ([r.w] if r.w is not None else []) + list(r.rs.values())
                r.guard = g
                waits.extend(g)
        if track is not None:
            slot = self._slot(track, eng)
            op.track = slot
            slot.count += 1
            tok = ("d", slot, slot.count)
            key = ("d", id(slot))
        else:
            tok = ("c", eng, op.idx)
            key = ("c", eng)
        for r in reads:
            r.rs[key] = tok
        for r in writes:
            r.w = tok
            r.rs = {}
        op.waits = self._filter_waits(eng, waits)
        lst.append(op)
        return op

    def _filter_waits(self, eng, waits):
        final = []
        seen = self.seen[eng]
        for t in waits:
            if t[0] == "c":
                if t[1] == eng and eng == "pe":
                    continue
                k = t[1]
            else:
                k = id(t[1])
            if seen.get(k, -1) >= t[2]:
                continue
            seen[k] = t[2]
            final.append(t)
            if t[0] == "c":
                self.ops[t[1]][t[2]].signal = True
        return final

    def dma(self, q, out, in_, reads, writes, track=None, **kw):
        tr = track if track is not None else writes[0]
        return self._add(q, lambda e: e.dma_start(out=out, in_=in_, **kw), reads, writes, track=tr)

    def mm(self, out, lhsT, rhs, start, stop, reads, writes):
        return self._add("pe", lambda e: e.matmul(out, lhsT=lhsT, rhs=rhs, start=start, stop=stop), reads, writes)

    def act(self, out, in_, func, reads, writes, bias=None, scale=None, eng="act"):
        kw = {}
        if bias is not None:
            kw["bias"] = bias
        if scale is not None:
            kw["scale"] = scale
        return self._add(eng, lambda e: e.activation(out=out, in_=in_, func=func, **kw), reads, writes)

    def tt(self, eng, out, in0, in1, op, reads, writes):
        return self._add(eng, lambda e: e.tensor_tensor(out=out, in0=in0, in1=in1, op=op), reads, writes)

    def ts(self, eng, out, in0, s1, s2, op0, op1, reads, writes):
        if op1 is None:
            return self._add(eng, lambda e: e.tensor_scalar(out=out, in0=in0, scalar1=s1, scalar2=None, op0=op0), reads, writes)
        return self._add(eng, lambda e: e.tensor_scalar(out=out, in0=in0, scalar1=s1, scalar2=s2, op0=op0, op1=op1), reads, writes)

    def stt(self, eng, out, in0, scalar, in1, op0, op1, reads, writes):
        return self._add(eng, lambda e: e.scalar_tensor_tensor(out=out, in0=in0, scalar=scalar, in1=in1, op0=op0, op1=op1), reads, writes)

    def copy(self, eng, out, in_, reads, writes):
        if eng == "act":
            return self._add(eng, lambda e: e.copy(out=out, in_=in_), reads, writes)
        return self._add(eng, lambda e: e.tensor_copy(out=out, in_=in_), reads, writes)

    def memset(self, eng, ap, val, writes):
        return self._add(eng, lambda e: e.memset(ap, val), [], writes)

    def scan(self, out, d0, d1, initial, reads, writes):
        return self._add("dve", lambda e: e.tensor_tensor_scan(out=out, data0=d0, data1=d1, initial=initial,
                                                               op0=ALU.mult, op1=ALU.add), reads, writes)

    def recip(self, out, in_, reads, writes):
        return self._add("dve", lambda e: e.reciprocal(out=out, in_=in_), reads, writes)

    def begin_phase(self):
        self.pes = ExitStack()

    def end_phase(self):
        toks = []
        for e in ["pe", "act", "dve", "pool"]:
            lst = self.ops[e]
            for i in range(len(lst) - 1, self.emitted[e] - 1, -1):
                if lst[i].track is None and lst[i].fn is not None:
                    toks.append(("c", e, i))
                    break
        for sl in self.phase_slots:
            toks.append(("d", sl, sl.count))
        for e in self.ENGS:
            op = Op()
            op.eng, op.fn, op.signal, op.track = e, None, False, None
            op.idx = len(self.ops[e])
            op.waits = self._filter_waits(e, [t for t in toks if not (t[0] == "c" and t[1] == e)])
            self.ops[e].append(op)
        self._emit_slice()
        self.pes.close()
        self.pes = None
        for r in self.all_regs:
            r.w, r.rs, r.guard = None, {}, []
            r.sem = None
        self.all_regs = [r for r in self.all_regs if getattr(r, "name", "").startswith("DR_")]
        self.free_slots.extend(sl for sl in self.phase_slots if not sl.sw)
        self.phase_slots = []

    def _emit_slice(self):
        nc = self.nc
        for e in ["pe", "act", "dve", "pool"]:
            t = self.tbase[e]
            for op in self.ops[e][self.emitted[e]:]:
                if op.signal:
                    t += 1
                op.ticket = t
            self.tbase[e] = t
        prog = self
        start = dict(self.emitted)

        def run(name, engine):
            for op in prog.ops[name][start[name]:]:
                for t in op.waits:
                    if t[0] == "c":
                        engine.wait_ge(prog.esem[t[1]], prog.ops[t[1]][t[2]].ticket)
                    else:
                        engine.wait_ge(t[1].h, 16 * t[2])
                if op.fn is None:
                    continue
                ins = op.fn(engine)
                if ins is None:
                    continue
                if op.track is not None:
                    ins.then_inc(op.track.h, 16)
                elif op.signal:
                    ins.then_inc(prog.esem[name], 1)

        with nc.Block() as block:
            @block.tensor
            def _(e):
                run("pe", e)

            @block.scalar
            def _(e):
                run("act", e)

            @block.vector
            def _(e):
                run("dve", e)

            @block.gpsimd
            def _(e):
                run("pool", e)

            @block.sync
            def _(e):
                run("sp", e)
        for e in self.ENGS:
            self.emitted[e] = len(self.ops[e])

    def finish(self):
        self.es.close()
        return self.nc


class Ctx:
    pass


def setup_common(p, TOK, SBK):
    c = Ctx()
    c.TOK, c.SBK, c.NSB = TOK, SBK, TOK // SBK
    c.NH = SBK // 512
    c.ones = p.sb("ones", [128, 128], BF16)
    c.r_ones = p.reg("ones")
    p.memset("dve", c.ones[:], 1.0, [c.r_ones])
    c.eps = p.sb("eps", [128, 1], F32)
    c.r_eps = p.reg("eps")
    p.memset("dve", c.eps[:], EPS, [c.r_eps])
    c.x = p.sb("x_sb", [128, 8, SBK], F32)
    c.r_x = p.regs(8, "x")
    c.h = p.sb("h_sb", [128, 8, SBK], BF16)
    c.r_h = p.reg("h")
    c.sq = p.sb("sq_sb", [128, 8, 512], BF16)
    c.r_sq = p.reg("sq")
    c.rstd = p.sb("rstd_sb", [128, 512], F32)
    c.r_rstd = p.reg("rstd")
    c.ps_ss = p.ps("ps_ss", [128, 512])
    c.r_ps_ss = p.reg("ps_ss")
    return c


def load_x(p, c, xT, sbi, q="sp"):
    SBK = c.SBK
    for ch in range(8):
        p.dma(q, c.x[:, ch, :], xT[ch * 128:(ch + 1) * 128, sbi * SBK:(sbi + 1) * SBK], [], [c.r_x[ch]])


def store_x(p, c, xTo, r_out, sbi, q="sp"):
    SBK = c.SBK
    for ch in range(8):
        p.dma(q, xTo[ch * 128:(ch + 1) * 128, sbi * SBK:(sbi + 1) * SBK], c.x[:, ch, :], [c.r_x[ch]], [], track=c.r_x[ch])


def rstd_from_ps(p, c, out, ps, dim, reads, writes):
    p.act(out, ps, AF.Sqrt, reads + [c.r_eps], writes, bias=c.eps[0:out.shape[0], :], scale=1.0 / dim)
    p.recip(out, out, writes, writes)


def rmsnorm_x(p, c, g_sb, r_g):
    for hf in range(c.NH):
        sl = slice(hf * 512, (hf + 1) * 512)
        for ch in range(8):
            p.act(c.sq[:, ch, :], c.x[:, ch, sl], AF.Square, [c.r_x[ch]], [c.r_sq])
        for ch in range(8):
            p.mm(c.ps_ss[:], c.ones[:], c.sq[:, ch, :], ch == 0, ch == 7, [c.r_ones, c.r_sq], [c.r_ps_ss])
        rstd_from_ps(p, c, c.rstd[:], c.ps_ss[:], D, [c.r_ps_ss], [c.r_rstd])
        for ch in range(8):
            eng = "dve"
            p.stt(eng, c.h[:, ch, sl], c.x[:, ch, sl], g_sb[:, ch:ch + 1], c.rstd[:], ALU.mult, ALU.mult,
                  [c.r_x[ch], r_g, c.r_rstd], [c.r_h])


def load_gain(p, name, src, ncol, mult):
    t = p.sb(name, [128, ncol], F32)
    r = p.reg(name)
    p.dma("sp", t[:], src, [], [r])
    if mult != 1.0:
        p.ts("dve", t[:], t[:], float(mult), None, ALU.mult, None, [r], [r])
    return t, r


class FFN:
    def __init__(self, p, c, tag, w_gu, w_down, g_dram):
        self.p, self.c = p, c
        SBK = c.SBK
        self.wg_s = p.dram_tmp(f"wg_s{tag}", [11, 128, 8, 256], BF16)
        self.wu_s = p.dram_tmp(f"wu_s{tag}", [11, 128, 8, 256], BF16)
        r_cast = p.reg(f"wcast{tag}")
        self.r_wgs = [r_cast] * 11
        self.r_wus = [r_cast] * 11
        wv = w_gu.rearrange("(c p) n -> p c n", p=128)
        for s in range(11):
            p.dma("pool", self.wg_s[s], wv[:, :, s * 256:(s + 1) * 256], [], [self.r_wgs[s]])
            p.dma("pool", self.wu_s[s], wv[:, :, DFF + s * 256:DFF + (s + 1) * 256], [], [self.r_wus[s]])
        self.g, self.r_g = load_gain(p, f"g{tag}", g_dram, 8, 1.0)
        self.wd = p.sb(f"wd{tag}", [128, NJ, D], BF16)
        self.r_wd = p.reg(f"wd{tag}")
        wdv = w_down.rearrange("(j p) n -> p j n", p=128)
        for j0 in range(0, NJ, 2):
            p.dma("pool", self.wd[:, j0:j0 + 2, :], wdv[:, j0:j0 + 2, :], [], [self.r_wd])

    @staticmethod
    def alloc_shared(p, c):
        s = Ctx()
        s.wg = [p.sb(f"wg{i}", [128, 8, 256], BF16) for i in range(2)]
        s.wu = [p.sb(f"wu{i}", [128, 8, 256], BF16) for i in range(2)]
        s.r_wg = p.regs(2, "wg")
        s.r_wu = p.regs(2, "wu")
        s.actT = p.sb("actT", [128, NJ, c.SBK], BF16)
        s.r_act = p.regs(NJ, "act")
        s.ps_g = [p.ps(f"ps_g{i}", [128, 512]) for i in range(2)]
        s.ps_u = [p.ps(f"ps_u{i}", [128, 512]) for i in range(2)]
        s.r_psg = p.regs(2, "psg")
        s.r_psu = p.regs(2, "psu")
        s.sg = [p.sb(f"sg{i}", [128, 512], F32) for i in range(2)]
        s.r_sg = p.regs(2, "sg")
        s.ps_y = [p.ps(f"ps_y{i}", [128, 512]) for i in range(2)]
        s.r_psy = p.regs(2, "psy")
        s.k = 0
        s.ky = 0
        return s

    def run(self, s, extra=None):
        p, c = self.p, self.c
        rmsnorm_x(p, c, self.g, self.r_g)
        for sl_i in range(11):
            b = sl_i % 2
            xg, xu = (extra if (extra is not None and sl_i == 0) else ([], []))
            p.dma("sp", s.wg[b][:], self.wg_s[sl_i], [self.r_wgs[sl_i]], [s.r_wg[b]] + xg)
            p.dma("sp", s.wu[b][:], self.wu_s[sl_i], [self.r_wus[sl_i]], [s.r_wu[b]] + xu)
            for jj in range(2):
                j = sl_i * 2 + jj
                for hf in range(c.NH):
                    sl = slice(hf * 512, (hf + 1) * 512)
                    k = s.k % 2
                    s.k += 1
                    for ch in range(8):
                        p.mm(s.ps_g[k][:], s.wg[b][:, ch, jj * 128:(jj + 1) * 128], c.h[:, ch, sl], ch == 0, ch == 7,
                             [s.r_wg[b], c.r_h], [s.r_psg[k]])
                    for ch in range(8):
                        p.mm(s.ps_u[k][:], s.wu[b][:, ch, jj * 128:(jj + 1) * 128], c.h[:, ch, sl], ch == 0, ch == 7,
                             [s.r_wu[b], c.r_h], [s.r_psu[k]])
                    p.act(s.sg[k][:], s.ps_g[k][:], AF.Silu, [s.r_psg[k]], [s.r_sg[k]])
                    p.tt("dve", s.actT[:, j, sl], s.sg[k][:], s.ps_u[k][:], ALU.mult, [s.r_sg[k], s.r_psu[k]], [s.r_act[j]])
        for o in range(8):
            for hf in range(c.NH):
                sl = slice(hf * 512, (hf + 1) * 512)
                k = s.ky % 2
                s.ky += 1
                for j in range(NJ):
                    p.mm(s.ps_y[k][:], self.wd[:, j, o * 128:(o + 1) * 128], s.actT[:, j, sl], j == 0, j == NJ - 1,
                         [self.r_wd, s.r_act[j]], [s.r_psy[k]])
                p.stt("dve", c.x[:, o, sl], s.ps_y[k][:], 0.5, c.x[:, o, sl], ALU.mult, ALU.add,
                      [s.r_psy[k], c.r_x[o]], [c.r_x[o]])


def load_w_resident(p, name, w, K, N, kc=128):
    nk = K // kc
    t = p.sb(name, [kc, nk, N], BF16)
    r = p.reg(name)
    wv = w.rearrange("(c p) n -> p c n", p=kc)
    step = max(1, 4096 // N)
    for c0 in range(0, nk, step):
        c1 = min(nk, c0 + step)
        p.dma("pool", t[:, c0:c1, :], wv[:, c0:c1, :], [], [r])
    return t, r


def build_a(p, TOK, kind, tag):
    SBK = min(1024, TOK)
    WIN = 1792 if kind == "lru" else 928
    NTC, TCP = (16, 96) if kind == "lru" else (6, 128)
    xT = p.dram_in("xT", [D, TOK], F32)
    g1 = p.dram_in("g1", [128, 8], F32)
    gm = p.dram_in("gm", [128, 8], F32)
    w_gu = p.dram_in("w_gu", [D, 2 * DFF], F32)
    w_dn = p.dram_in("w_dn", [DFF, D], F32)
    w_in = p.dram_in("w_in", [D, WIN], F32)
    xo = p.dram_out("xT_out", [D, TOK], F32)
    uo = p.dram_out("u_tok", [NTC, TCP, TOK], F32)
    umo = p.dram_out("u_mem", [2, 128, TOK], F32)
    r_xo, r_uo, r_umo = p.reg("xo"), p.reg("uo"), p.reg("umo")
    c = setup_common(p, TOK, SBK)
    ffn = FFN(p, c, "1" + tag, w_gu, w_dn, g1)
    fs = FFN.alloc_shared(p, c)
    gmt, r_gm = load_gain(p, "gm", gm, 8, 1.0)
    win, r_win = load_w_resident(p, "win", w_in, D, WIN)
    ps_p = [p.ps(f"ps_p{i}", [128, 512]) for i in range(1)]
    r_psp = p.regs(1, "psp")
    ust = [p.sb(f"ust{i}", [128, 512], F32) for i in range(2)]
    r_ust = p.regs(2, "ust")
    if kind == "lru":
        chunks = [(i * 96, 96) for i in range(16)] + [(1536, 128), (1664, 128)]
    else:
        chunks = [(i * 128, 128) for i in range(5)] + [(576, 96), (672, 128), (800, 128)]
    kk = 0
    for sbi in range(c.NSB):
        load_x(p, c, xT, sbi)
        ffn.run(fs)
        store_x(p, c, xo, r_xo, sbi)
        rmsnorm_x(p, c, gmt, r_gm)
        for ci, (cs, M) in enumerate(chunks):
            for hf in range(c.NH):
                ub = kk % 2
                kk += 1
                sl = slice(hf * 512, (hf + 1) * 512)
                for ch in range(8):
                    p.mm(ps_p[0][0:M, :], win[:, ch, cs:cs + M], c.h[:, ch, sl], ch == 0, ch == 7, [r_win, c.r_h], [r_psp[0]])
                p.copy("act", ust[ub][0:M, :], ps_p[0][0:M, :], [r_psp[0]], [r_ust[ub]])
                tsl = slice(sbi * SBK + hf * 512, sbi * SBK + (hf + 1) * 512)
                if ci < NTC:
                    p.dma("sp", uo[ci, 0:M, tsl], ust[ub][0:M, :], [r_ust[ub]], [], track=r_ust[ub])
                else:
                    p.dma("sp", umo[ci - NTC, :, tsl], ust[ub][0:128, :], [r_ust[ub]], [], track=r_ust[ub])


def build_lru_b(p, S, NB):
    CH = min(1024, S)
    NCH = S // CH
    NHF = CH // 512
    xr = p.dram_in("xr", [NB, 96, S], F32)
    gt = p.dram_in("gate", [NB, 96, S], F32)
    cw = p.dram_in("cw", [96, NB, 4], F32)
    cb = p.dram_in("cb", [96, NB], F32)
    gw = p.dram_in("gw", [96, NB, 4, 96], F32)
    gb = p.dram_in("gb", [96, NB, 4], F32)
    lam = p.dram_in("lam", [96, 2 * NB], F32)
    out = p.dram_out("tok", [NB, 96, S], BF16)
    r_out = p.reg("out")

    def small(name, src, shape, q="sp"):
        t = p.sb(name, shape, F32)
        r = p.reg(name)
        p.dma(q, t[:], src, [], [r])
        return t, r
    cw_t, r_cw = small("cw", cw, [96, NB, 4])
    cb_t, r_cb = small("cb", cb, [96, NB])
    gb_t, r_gb = small("gb", gb, [96, NB, 4])
    lam_t, r_lam = small("lam", lam, [96, 2 * NB])
    gw_t = p.sb("gw", [96, NB, 4, 96], BF16)
    r_gw = p.reg("gw")
    p.dma("pool", gw_t[:], gw, [], [r_gw])
    one = p.sb("one", [128, 1], F32)
    r_one = p.reg("one")
    p.memset("dve", one[:], 1.0, [r_one])
    m8 = p.sb("m8", [96, 2 * NB], F32)
    r_m8 = p.reg("m8")
    p.act(m8[:], lam_t[:], AF.Exp, [r_lam], [r_m8], scale=-1.0)
    p.act(m8[:], m8[:], AF.Ln, [r_m8, r_one], [r_m8], bias=one[0:96, :])
    p.ts("dve", m8[:], m8[:], -8.0, None, ALU.mult, None, [r_m8], [r_m8])

    xc_full = p.sb("xc_full", [96, S], F32)
    hf_full = p.sb("hf_full", [96, S], F32)
    r_xc = p.regs(NCH, "xc")
    r_hf = p.regs(NCH, "hf")
    xr_t = [p.sb(f"xr_t{i}", [96, CH + 3], F32) for i in range(2)]
    r_xr = p.regs(2, "xr_t")
    g_t = [p.sb(f"g_t{i}", [96, CH], F32) for i in range(2)]
    r_gt = p.regs(2, "g_t")
    xcb = [p.sb(f"xcb{i}", [96, CH], BF16) for i in range(2)]
    r_xcb = p.regs(2, "xcb")
    ps_r = [p.ps(f"ps_r{i}", [128, CH]) for i in range(2)]
    ps_i = [p.ps(f"ps_i{i}", [128, CH]) for i in range(2)]
    r_psr = p.regs(2, "psr")
    r_psi = p.regs(2, "psi")

    def wt(name, dt=F32):
        return p.sb(name, [96, CH], dt), p.reg(name)
    rr = [wt(f"rr{i}") for i in range(2)]
    ii = [wt(f"ii{i}") for i in range(2)]
    aa = [wt(f"aa{i}") for i in range(2)]
    qq = [wt(f"qq{i}") for i in range(2)]
    hr = [wt(f"hr{i}") for i in range(2)]
    g2 = [wt(f"g2{i}") for i in range(2)]
    ot = [wt(f"ot{i}", BF16) for i in range(2)]

    def gates1(blk, d, st):
        for hf in range(NHF):
            sl = slice(hf * 512, (hf + 1) * 512)
            p.mm(ps_r[st][0:96, sl], gw_t[:, blk, d * 2 + 0, :], xcb[st][:, sl], True, True, [r_gw, r_xcb[st]], [r_psr[st]])
            p.mm(ps_i[st][0:96, sl], gw_t[:, blk, d * 2 + 1, :], xcb[st][:, sl], True, True, [r_gw, r_xcb[st]], [r_psi[st]])
        p.act(rr[st][0][:], ps_r[st][0:96, :], AF.Sigmoid, [r_psr[st], r_gb], [rr[st][1]], bias=gb_t[:, blk, d * 2:d * 2 + 1])
        p.act(ii[st][0][:], ps_i[st][0:96, :], AF.Sigmoid, [r_psi[st], r_gb], [ii[st][1]], bias=gb_t[:, blk, d * 2 + 1:d * 2 + 2])
        p.act(aa[st][0][:], rr[st][0][:], AF.Exp, [rr[st][1], r_m8], [aa[st][1]], scale=m8[:, blk * 2 + d:blk * 2 + d + 1])
        p.act(qq[st][0][:], aa[st][0][:], AF.Square, [aa[st][1]], [qq[st][1]])
        p.act(qq[st][0][:], qq[st][0][:], AF.Sqrt, [qq[st][1], r_one], [qq[st][1]], bias=one[0:96, :], scale=-1.0)

    def gates2(st, xc_ap, r_xc_c):
        p.tt("dve", ii[st][0][:], ii[st][0][:], xc_ap, ALU.mult, [ii[st][1], r_xc_c], [ii[st][1]])
        p.tt("dve", qq[st][0][:], qq[st][0][:], ii[st][0][:], ALU.mult, [qq[st][1], ii[st][1]], [qq[st][1]])

    for blk in range(NB):
        def f1(ci, st):
            t0 = ci * CH
            lo = max(0, t0 - 2)
            hi = min(S, t0 + CH + 1)
            if ci == 0 or ci == NCH - 1:
                p.memset("pool", xr_t[st][:], 0.0, [r_xr[st]])
            p.dma("sp", xr_t[st][:, lo - (t0 - 2):hi - (t0 - 2)], xr[blk, :, lo:hi], [], [r_xr[st]])
            xc = xc_full[:, t0:t0 + CH]
            p.ts("dve", xc, xr_t[st][:, 0:CH], cw_t[:, blk, 0:1], cb_t[:, blk:blk + 1], ALU.mult, ALU.add,
                 [r_xr[st], r_cw, r_cb], [r_xc[ci]])
            for kk in range(1, 4):
                p.stt("dve", xc, xr_t[st][:, kk:kk + CH], cw_t[:, blk, kk:kk + 1], xc, ALU.mult, ALU.add,
                      [r_xr[st], r_cw, r_xc[ci]], [r_xc[ci]])
            p.copy("pool", xcb[st][:], xc, [r_xc[ci]], [r_xcb[st]])
            gates1(blk, 0, st)

        def f2(ci, st):
            t0 = ci * CH
            gates2(st, xc_full[:, t0:t0 + CH], r_xc[ci])
            init = 0.0 if ci == 0 else hf_full[:, t0 - 1:t0]
            rd = [aa[st][1], qq[st][1]] + ([] if ci == 0 else [r_hf[ci - 1]])
            p.scan(hf_full[:, t0:t0 + CH], aa[st][0][:], qq[st][0][:], init, rd, [r_hf[ci]])

        def r1(ci, st):
            t0 = ci * CH
            p.dma("sp", g_t[st][:], gt[blk, :, t0:t0 + CH], [], [r_gt[st]])
            p.copy("pool", xcb[st][:], xc_full[:, t0:t0 + CH], [r_xc[ci]], [r_xcb[st]])
            gates1(blk, 1, st)
            g2t, r_g2 = g2[st]
            p.tt("pool", g2t[:], g_t[st][:], g_t[st][:], ALU.mult, [r_gt[st]], [r_g2])
            p.ts("pool", g2t[:], g2t[:], 0.044715, 1.0, ALU.mult, ALU.add, [r_g2], [r_g2])
            p.tt("pool", g2t[:], g2t[:], g_t[st][:], ALU.mult, [r_g2, r_gt[st]], [r_g2])
            p.act(g2t[:], g2t[:], AF.Sigmoid, [r_g2], [r_g2], scale=1.5957691216057308)
            p.tt("pool", g2t[:], g2t[:], g_t[st][:], ALU.mult, [r_g2, r_gt[st]], [r_g2])

        def r2(ci, st):
            t0 = ci * CH
            gates2(st, xc_full[:, t0:t0 + CH], r_xc[ci])
            hrt, r_hrt = hr[st]
            hrp, r_hrp = hr[1 - st]
            init = 0.0 if ci == NCH - 1 else hrp[:, 0:1]
            rd = [aa[st][1], qq[st][1]] + ([] if ci == NCH - 1 else [r_hrp])
            p.scan(hrt[:, ::-1], aa[st][0][:, ::-1], qq[st][0][:, ::-1], init, rd, [r_hrt])
            p.tt("pool", rr[st][0][:], hrt[:], hf_full[:, t0:t0 + CH], ALU.add, [r_hrt, r_hf[ci], rr[st][1]], [rr[st][1]])
            ott, r_ott = ot[st]
            p.tt("dve", ott[:], rr[st][0][:], g2[st][0][:], ALU.mult, [rr[st][1], g2[st][1]], [r_ott])
            p.dma("sp", out[blk, :, t0:t0 + CH], ott[:], [r_ott], [], track=r_ott)

        f1(0, 0)
        for ci in range(NCH):
            if ci + 1 < NCH:
                f1(ci + 1, (ci + 1) % 2)
            f2(ci, ci % 2)
        order = list(range(NCH - 1, -1, -1))
        r1(order[0], 0)
        for n, ci in enumerate(order):
            if n + 1 < NCH:
                r1(order[n + 1], (n + 1) % 2)
            r2(ci, n % 2)


def build_c(p, TOK, kind, tag, own=None):
    SBK = min(1024, TOK) if own is None else 512
    KC, NK = (96, 8) if kind == "lru" else (128, 6)
    SF = TOK if own is None else own[0]
    xT = p.dram_in("xT", [D, SF], F32)
    umem = p.dram_in("u_mem", [2, 128, SF], F32)
    tok = p.dram_in("tok", [NK, KC, TOK], BF16)
    memT = p.dram_in("memT", [D, 256], F32)
    gmem = p.dram_in("gmem", [128, 8], F32)
    wkv = p.dram_in("w_mem_kv", [D, 512], F32)
    gq = p.dram_in("gq", [128, 1], F32)
    gk = p.dram_in("gk", [128, 1], F32)
    w_out = p.dram_in("w_out", [D, D], F32)
    g2 = p.dram_in("g2", [128, 8], F32)
    w_gu = p.dram_in("w_gu", [D, 2 * DFF], F32)
    w_dn = p.dram_in("w_dn", [DFF, D], F32)
    xo = p.dram_out("xT_out", [D, TOK], F32)
    r_xo = p.reg("xo")
    c = setup_common(p, TOK, SBK)
    fs = FFN.alloc_shared(p, c)
    gmem_t, r_gmem = load_gain(p, "gmem", gmem, 8, 1.0)
    gq_t, r_gq = load_gain(p, "gq", gq, 1, 1.0)
    gk_t, r_gk = load_gain(p, "gk", gk, 1, 1.0)
    bd = p.sb("bd", [128, 128], BF16)
    r_bd = p.reg("bd")
    p.memset("dve", bd[:], 0.0, [r_bd])
    p.memset("dve", bd[0:64, 0:64], 1.0, [r_bd])
    p.memset("dve", bd[64:128, 64:128], 1.0, [r_bd])
    KmT = p.sb("KmT", [128, 2, 256], BF16)
    r_KmT = p.reg("KmT")
    Vm = p.sb("Vm", [128, 2, 4, 128], BF16)
    r_Vm = p.reg("Vm")
    p.memset("pool", Vm[:], 1.0, [r_Vm])
    for ch in range(8):
        p.dma("sp", c.x[:, ch, 0:256], memT[ch * 128:(ch + 1) * 128, :], [], [c.r_x[ch]])
    wkv_v = wkv.rearrange("(c p) n -> p c n", p=128)
    r_wkvK, r_wkvV = p.reg("wkvK"), p.reg("wkvV")
    p.dma("pool", fs.wg[0][:], wkv_v[:, :, 0:256], [], [r_wkvK])
    p.dma("pool", fs.wu[0][:], wkv_v[:, :, 256:512], [], [r_wkvV])
    for ch in range(8):
        p.act(c.sq[:, ch, 0:256], c.x[:, ch, 0:256], AF.Square, [c.r_x[ch]], [c.r_sq])
    for ch in range(8):
        p.mm(c.ps_ss[:, 0:256], c.ones[:], c.sq[:, ch, 0:256], ch == 0, ch == 7, [c.r_ones, c.r_sq], [c.r_ps_ss])
    rstd_from_ps(p, c, c.rstd[:, 0:256], c.ps_ss[:, 0:256], D, [c.r_ps_ss], [c.r_rstd])
    for ch in range(8):
        p.stt("dve", c.h[:, ch, 0:256], c.x[:, ch, 0:256], gmem_t[:, ch:ch + 1], c.rstd[:, 0:256], ALU.mult, ALU.mult,
              [c.r_x[ch], r_gmem, c.r_rstd], [c.r_h])
    for cc in range(2):
        psk = fs.ps_g[cc]
        for ch in range(8):
            p.mm(psk[:, 0:256], fs.wg[0][:, ch, cc * 128:(cc + 1) * 128], c.h[:, ch, 0:256], ch == 0, ch == 7,
                 [r_wkvK, c.r_h], [fs.r_psg[cc]])
        p.act(c.sq[:, cc, 0:256], psk[:, 0:256], AF.Square, [fs.r_psg[cc]], [c.r_sq])
        p.mm(c.ps_ss[:, 0:256], bd[:], c.sq[:, cc, 0:256], True, True, [r_bd, c.r_sq], [c.r_ps_ss])
        rstd_from_ps(p, c, c.rstd[:, 0:256], c.ps_ss[:, 0:256], 64, [c.r_ps_ss], [c.r_rstd])
        p.stt("dve", KmT[:, cc, :], psk[:, 0:256], gk_t[:, 0:1], c.rstd[:, 0:256], ALU.mult, ALU.mult,
              [fs.r_psg[cc], r_gk, c.r_rstd], [r_KmT])
    for kt in range(2):
        psv = fs.ps_u[kt]
        for ch in range(8):
            p.mm(psv[:, 0:256], c.h[:, ch, kt * 128:(kt + 1) * 128], fs.wu[0][:, ch, :], ch == 0, ch == 7,
                 [r_wkvV, c.r_h], [fs.r_psu[kt]])
        for h in range(4):
            off = 0 if h % 2 == 0 else 64
            p.copy("dve", Vm[:, kt, h, off:off + 64], psv[:, h * 64:(h + 1) * 64], [fs.r_psu[kt]], [r_Vm])
    ffn = FFN(p, c, "2" + tag, w_gu, w_dn, g2)
    wo_t, r_wo_t = load_w_resident(p, "wo_t", w_out[0:768, :], 768, D, kc=KC)
    wo_m, r_wo_m = load_w_resident(p, "wo_m", w_out[768:1024, :], 256, D, kc=128)
    um_t = fs.sg
    qn = p.sb("qn", [128, 2, 512], BF16)
    r_qn = p.reg("qn")
    mixm = p.sb("mixm", [128, 2, 512], BF16)
    r_mixm = p.reg("mixm")
    mixt = p.sb("mixt", [KC, NK, 512], BF16)
    r_mixt = p.reg("mixt")
    pT = [p.sb(f"pT{i}", [128, 512], BF16) for i in range(2)]
    r_pT = p.regs(2, "pT")
    rec, r_rec = c.rstd, c.r_rstd
    selk = [0]
    if own is not None:
        NQ = own[1]
        qmask = p.dram_in("qmask", [128, NQ], F32)
        qm_t, r_qm = load_gain(p, "qmask", qmask, NQ, 1.0)
        selt = [(p.sb(f"selt{i}", [128, 512], F32), p.reg(f"selt{i}")) for i in range(2)]

    def sel_load(dst, r_dst, src_fn):
        for qi in range(NQ):
            tmp, r_tmp = selt[selk[0] % 2]
            selk[0] += 1
            p.dma("sp", tmp[:], src_fn(qi), [], [r_tmp])
            if qi == 0:
                p.ts("dve", dst, tmp[:], qm_t[:, 0:1], None, ALU.mult, None, [r_tmp, r_qm], [r_dst])
            else:
                p.stt("dve", dst, tmp[:], qm_t[:, qi:qi + 1], dst, ALU.mult, ALU.add, [r_tmp, r_qm, r_dst], [r_dst])
    ks = 0
    for sbi in range(c.NSB):
        if own is None:
            load_x(p, c, xT, sbi)
        else:
            for ch in range(8):
                sel_load(c.x[:, ch, :], c.r_x[ch],
                         lambda qi, ch=ch: xT[ch * 128:(ch + 1) * 128, qi * TOK + sbi * SBK:qi * TOK + (sbi + 1) * SBK])
        for hf in range(c.NH):
            sl = slice(hf * 512, (hf + 1) * 512)
            tsl = slice(sbi * SBK + hf * 512, sbi * SBK + (hf + 1) * 512)
            for cc in range(2):
                if own is None:
                    p.dma("sp", um_t[cc][:], umem[cc, :, tsl], [], [fs.r_sg[cc]])
                else:
                    sel_load(um_t[cc][:], fs.r_sg[cc],
                             lambda qi, cc=cc: umem[cc, :, qi * TOK + tsl.start:qi * TOK + tsl.stop])
            for k in range(NK):
                p.dma("sp", mixt[:, k, :], tok[k, :, tsl], [], [r_mixt])
            for cc in range(2):
                p.act(c.sq[:, cc, :], um_t[cc][:], AF.Square, [fs.r_sg[cc]], [c.r_sq])
                p.mm(c.ps_ss[:], bd[:], c.sq[:, cc, :], True, True, [r_bd, c.r_sq], [c.r_ps_ss])
                rstd_from_ps(p, c, c.rstd[:], c.ps_ss[:], 64, [c.r_ps_ss], [c.r_rstd])
                p.stt("dve", qn[:, cc, :], um_t[cc][:], gq_t[:, 0:1], c.rstd[:], ALU.mult, ALU.mult,
                      [fs.r_sg[cc], r_gq, c.r_rstd], [r_qn])
            for h in range(4):
                cc, base = h // 2, (h % 2) * 64
                pso = fs.ps_u[h % 2]
                r_pso = fs.r_psu[h % 2]
                for kt in range(2):
                    k = ks % 2
                    ks += 1
                    p.mm(fs.ps_g[k][:], KmT[base:base + 64, cc, kt * 128:(kt + 1) * 128], qn[base:base + 64, cc, :],
                         True, True, [r_KmT, r_qn], [fs.r_psg[k]])
                    p.act(pT[k][:], fs.ps_g[k][:], AF.Exp, [fs.r_psg[k]], [r_pT[k]], scale=0.125)
                    p.mm(pso[:], Vm[:, kt, h, :], pT[k][:], kt == 0, kt == 1, [r_Vm, r_pT[k]], [r_pso])
                if h % 2 == 0:
                    p.recip(rec[64:128, :], pso[64:128, :], [r_pso], [r_rec])
                    p.tt("dve", mixm[0:64, cc, :], pso[0:64, :], rec[64:128, :], ALU.mult, [r_pso, r_rec], [r_mixm])
                else:
                    p.recip(rec[0:64, :], pso[0:64, :], [r_pso], [r_rec])
                    p.tt("dve", mixm[64:128, cc, :], pso[64:128, :], rec[0:64, :], ALU.mult, [r_pso, r_rec], [r_mixm])
            for o in range(8):
                k = fs.ky % 2
                fs.ky += 1
                osl = slice(o * 128, (o + 1) * 128)
                for kk in range(NK):
                    p.mm(fs.ps_y[k][:], wo_t[:, kk, osl], mixt[:, kk, :], kk == 0, False, [r_wo_t, r_mixt], [fs.r_psy[k]])
                for cc in range(2):
                    p.mm(fs.ps_y[k][:], wo_m[:, cc, osl], mixm[:, cc, :], False, cc == 1, [r_wo_m, r_mixm], [fs.r_psy[k]])
                p.tt("dve", c.x[:, o, sl], fs.ps_y[k][:], c.x[:, o, sl], ALU.add, [fs.r_psy[k], c.r_x[o]], [c.r_x[o]])
        ffn.run(fs, extra=([r_wkvK], [r_wkvV]) if sbi == 0 else None)
        store_x(p, c, xo, r_xo, sbi)


def mla_consts():
    invf = np.zeros((96, 1), np.float32)
    inv = (10000.0 ** (-np.arange(16, dtype=np.float32) * np.float32(2.0 / 32))).astype(np.float32)
    invf[64:80, 0] = inv
    invf[80:96, 0] = inv
    sgn = np.zeros((96, 1), np.float32)
    sgn[64:80] = -1.0
    sgn[80:96] = 1.0
    cm = np.zeros((96, 1), np.float32)
    cm[64:96] = 1.0
    cb = np.zeros((96, 1), np.float32)
    cb[0:64] = 1.0
    Pm = np.zeros((96, 96), np.float32)
    for j in range(16):
        Pm[80 + j, 64 + j] = 1.0
        Pm[64 + j, 80 + j] = 1.0
    return {"invf": invf, "sgn": sgn, "cm": cm, "cb": cb, "Pm": Pm}


def build_mla_q(p, TOK, own=None):
    NHALF = TOK // 512
    ut = p.dram_in("u_tok", [6, 128, TOK], F32)
    pos = p.dram_in("pos", [1, TOK], I32)
    gqa = p.dram_in("gqa", [128, 3], F32)
    w_uq = p.dram_in("w_uq", [384, 1152], F32)
    gkva = p.dram_in("gkva", [128, 2], F32)
    w_ukv = p.dram_in("w_ukv", [256, 1536], F32)
    gqn = p.dram_in("gqn", [96, 1], F32)
    gkn = p.dram_in("gkn", [96, 1], F32)
    invf = p.dram_in("invf", [96, 1], F32)
    sgn = p.dram_in("sgn", [96, 1], F32)
    cm = p.dram_in("cm", [96, 1], F32)
    cb = p.dram_in("cb", [96, 1], F32)
    Pm = p.dram_in("Pm", [96, 96], F32)
    qTo = p.dram_out("qT", [12, 96, TOK], BF16)
    KTo = p.dram_out("KT", [12, 96, TOK], BF16)
    Vo = p.dram_out("Vaug", [12, 128, TOK // 128, 128], BF16)
    r_qo, r_ko, r_vo = p.reg("qo"), p.reg("ko"), p.reg("vo")
    c = Ctx()
    c.ones = p.sb("ones", [128, 128], BF16)
    c.r_ones = p.reg("ones")
    p.memset("dve", c.ones[:], 1.0, [c.r_ones])
    c.eps = p.sb("eps", [128, 1], F32)
    c.r_eps = p.reg("eps")
    p.memset("dve", c.eps[:], EPS, [c.r_eps])
    c.rstd = p.sb("rstd_sb", [128, 512], F32)
    c.r_rstd = p.reg("rstd")
    c.ps_ss = p.ps("ps_ss", [128, 512])
    c.r_ps_ss = p.reg("ps_ss")

    def small(name, src, shape):
        t = p.sb(name, shape, F32)
        r = p.reg(name)
        p.dma("sp", t[:], src, [], [r])
        return t, r
    gqa_t, r_gqa = small("gqa", gqa, [128, 3])
    gkva_t, r_gkva = small("gkva", gkva, [128, 2])
    gqn_t, r_gqn = small("gqn", gqn, [96, 1])
    gkn_t, r_gkn = small("gkn", gkn, [96, 1])
    invf_t, r_invf = small("invf", invf, [96, 1])
    sgn_t, r_sgn = small("sgn", sgn, [96, 1])
    cm_t, r_cm = small("cm", cm, [96, 1])
    cb_t, r_cb = small("cb", cb, [96, 1])
    Pm_t, r_Pm = small("Pm", Pm, [96, 96])
    wuq, r_wuq = load_w_resident(p, "wuq", w_uq, 384, 1152)
    wukv, r_wukv = load_w_resident(p, "wukv", w_ukv, 256, 1536)
    wukv_h = wukv[:].rearrange("p c (h t) -> p c h t", t=128)

    def t32(name, rows=96, dt=F32):
        return p.sb(name, [rows, 512], dt), p.reg(name)
    cq = p.sb("cq", [128, 3, 512], F32); r_cq = p.reg("cq")
    ckv = p.sb("ckv", [128, 2, 512], F32); r_ckv = p.reg("ckv")
    sq = p.sb("sq", [128, 3, 512], BF16); r_sq = p.reg("sq")
    cqn = p.sb("cqn", [128, 3, 512], BF16); r_cqn = p.reg("cqn")
    ckvn = p.sb("ckvn", [128, 2, 512], BF16); r_ckvn = p.reg("ckvn")
    posi = p.sb("posi", [96, 512], I32); r_posi = p.reg("posi")
    ang, r_ang = t32("ang")
    kf, r_kf = t32("kf")
    ki = p.sb("ki", [96, 512], I32); r_ki = p.reg("ki")
    rc, r_rc = t32("rc")
    St, r_S = t32("S")
    Ct, r_C = t32("C")
    G = 6
    raw = [t32(f"raw{i}") for i in range(G)]
    sqs = [t32(f"sqs{i}", 96, BF16) for i in range(G)]
    rsl = [t32(f"rsl{i}") for i in range(G)]
    qn = [t32(f"qn{i}") for i in range(G)]
    t1s = [t32(f"t1s{i}") for i in range(G)]
    t2s = [t32(f"t2s{i}") for i in range(G)]
    outs = [t32(f"outs{i}", 96, BF16) for i in range(G)]
    krw, r_krw = t32("krw")
    kq, r_kq = t32("kq")
    krr, r_krr = t32("krr")
    vst = p.sb("vst", [128, 12, 4, 128], BF16); r_vst = p.reg("vst")
    p.memset("pool", vst[:], 1.0, [r_vst])
    p.memset("pool", kq[:], 0.0, [r_kq])
    psA = [p.ps(f"psA{i}", [128, 512]) for i in range(2)]; r_psA = p.regs(2, "psA")
    psB = [c.ps_ss, p.ps("psB1", [128, 512])]; r_psB = [c.r_ps_ss, p.reg("psB1")]
    psC = [p.ps(f"psC{i}", [128, 512]) for i in range(2)]; r_psC = p.regs(2, "psC")
    kC = [0]
    kB = [0]
    psV = [p.ps(f"psV{i}", [128, 384]) for i in range(2)]; r_psV = p.regs(2, "psV")
    TWO_PI = 2.0 * PI

    def wrap(t, r_t):
        p.ts("dve", kf[:], t[:], PI, None, ALU.is_gt, None, [r_t], [r_kf])
        p.stt("dve", t[:], kf[:], -TWO_PI, t[:], ALU.mult, ALU.add, [r_kf, r_t], [r_t])
        p.ts("dve", kf[:], t[:], -PI, None, ALU.is_lt, None, [r_t], [r_kf])
        p.stt("dve", t[:], kf[:], TWO_PI, t[:], ALU.mult, ALU.add, [r_kf, r_t], [r_t])
        p.ts("dve", t[:], t[:], PI, -PI, ALU.min, ALU.max, [r_t], [r_t])

    def rope_apply(src, r_src, dst_ap, r_dst, rows, sl=0):
        kc = kC[0] % 2
        kC[0] += 1
        t1, r_t1 = t1s[sl]
        t2, r_t2 = t2s[sl]
        p.mm(psC[kc][0:96, :], Pm_t[:], src[:], True, True, [r_Pm, r_src], [r_psC[kc]])
        p.tt("dve", t1[rows, :], psC[kc][rows, :], St[rows, :], ALU.mult, [r_psC[kc], r_S], [r_t1])
        p.tt("pool", t2[rows, :], src[rows, :], Ct[rows, :], ALU.mult, [r_src, r_C], [r_t2])
        p.tt("pool", dst_ap, t1[rows, :], t2[rows, :], ALU.add, [r_t1, r_t2], [r_dst])

    def head_norm(sl):
        kb = kB[0] % 2
        kB[0] += 1
        p.mm(psB[kb][0:96, :], c.ones[0:96, 0:96], sqs[sl][0][:], True, True, [c.r_ones, sqs[sl][1]], [r_psB[kb]])
        rstd_from_ps(p, c, rsl[sl][0][:], psB[kb][0:96, :], 96, [r_psB[kb]], [rsl[sl][1]])

    kA = [0]
    if own is not None:
        TOKQ, NQ = own
        qmask = p.dram_in("qmask", [128, NQ], F32)
        qm_t, r_qm = small("qmask", qmask, [128, NQ])
        selt = [(p.sb(f"selt{i}", [128, 512], F32), p.reg(f"selt{i}")) for i in range(2)]
    selk = [0]

    def tables(cands):
        if len(cands) == 1:
            p.dma("sp", posi[:], pos[0:1, cands[0]].partition_broadcast(96), [], [r_posi])
            p.copy("dve", ang[:], posi[:], [r_posi], [r_ang])
        else:
            for qi, ts_ in enumerate(cands):
                p.dma("sp", posi[:], pos[0:1, ts_].partition_broadcast(96), [], [r_posi])
                p.copy("dve", kf[:], posi[:], [r_posi], [r_kf])
                if qi == 0:
                    p.ts("dve", ang[:], kf[:], qm_t[0:96, 0:1], None, ALU.mult, None, [r_kf, r_qm], [r_ang])
                else:
                    p.stt("dve", ang[:], kf[:], qm_t[0:96, qi:qi + 1], ang[:], ALU.mult, ALU.add, [r_kf, r_qm, r_ang], [r_ang])
        p.ts("dve", ang[:], ang[:], invf_t[:, 0:1], None, ALU.mult, None, [r_ang, r_invf], [r_ang])
        p.ts("dve", kf[:], ang[:], 1.0 / TWO_PI, None, ALU.mult, None, [r_ang], [r_kf])
        p.copy("dve", ki[:], kf[:], [r_kf], [r_ki])
        p.copy("dve", kf[:], ki[:], [r_ki], [r_kf])
        p.stt("dve", ang[:], kf[:], -TWO_PI, ang[:], ALU.mult, ALU.add, [r_kf, r_ang], [r_ang])
        wrap(ang, r_ang)
        p.ts("dve", rc[:], ang[:], PI / 2, None, ALU.add, None, [r_ang], [r_rc])
        wrap(rc, r_rc)
        p.act(St[:], ang[:], AF.Sin, [r_ang], [r_S])
        p.ts("dve", St[:], St[:], sgn_t[:, 0:1], None, ALU.mult, None, [r_S, r_sgn], [r_S])
        p.act(Ct[:], rc[:], AF.Sin, [r_rc], [r_C])
        p.ts("dve", Ct[:], Ct[:], cm_t[:, 0:1], cb_t[:, 0:1], ALU.mult, ALU.add, [r_C, r_cm, r_cb], [r_C])

    def lat_norm(src, r_src, n, g_t, r_g, dst, r_dst, dim):
        for cc in range(n):
            p.act(sq[:, cc, :], src[:, cc, :], AF.Square, [r_src], [r_sq])
        for cc in range(n):
            p.mm(c.ps_ss[:], c.ones[:], sq[:, cc, :], cc == 0, cc == n - 1, [c.r_ones, r_sq], [c.r_ps_ss])
        rstd_from_ps(p, c, c.rstd[:], c.ps_ss[:], dim, [c.r_ps_ss], [c.r_rstd])
        for cc in range(n):
            p.stt("dve", dst[:, cc, :], src[:, cc, :], g_t[:, cc:cc + 1], c.rstd[:], ALU.mult, ALU.mult,
                  [r_src, r_g, c.r_rstd], [r_dst])

    def q_part(cands, osl):
        for cc in range(3):
            if len(cands) == 1:
                p.dma("sp", cq[:, cc, :], ut[cc, :, cands[0]], [], [r_cq])
            else:
                for qi, ts_ in enumerate(cands):
                    tmp, r_tmp = selt[selk[0] % 2]
                    selk[0] += 1
                    p.dma("sp", tmp[:], ut[cc, :, ts_], [], [r_tmp])
                    if qi == 0:
                        p.ts("dve", cq[:, cc, :], tmp[:], qm_t[:, 0:1], None, ALU.mult, None, [r_tmp, r_qm], [r_cq])
                    else:
                        p.stt("dve", cq[:, cc, :], tmp[:], qm_t[:, qi:qi + 1], cq[:, cc, :], ALU.mult, ALU.add,
                              [r_tmp, r_qm, r_cq], [r_cq])
        lat_norm(cq, r_cq, 3, gqa_t, r_gqa, cqn, r_cqn, 384)
        for g0 in range(0, 12, G):
            for sl in range(G):
                h = g0 + sl
                k = kA[0] % 2
                kA[0] += 1
                for cc in range(3):
                    p.mm(psA[k][0:96, :], wuq[:, cc, h * 96:(h + 1) * 96], cqn[:, cc, :], cc == 0, cc == 2, [r_wuq, r_cqn], [r_psA[k]])
                p.act(sqs[sl][0][:], psA[k][0:96, :], AF.Square, [r_psA[k]], [sqs[sl][1]])
                p.copy("act", raw[sl][0][:], psA[k][0:96, :], [r_psA[k]], [raw[sl][1]])
            for sl in range(G):
                head_norm(sl)
            for sl in range(G):
                p.stt("dve", qn[sl][0][:], raw[sl][0][:], gqn_t[:, 0:1], rsl[sl][0][:], ALU.mult, ALU.mult,
                      [raw[sl][1], r_gqn, rsl[sl][1]], [qn[sl][1]])
            for sl in range(G):
                rope_apply(qn[sl][0], qn[sl][1], outs[sl][0][:], outs[sl][1], slice(0, 96), sl)
                p.dma("sp", qTo[g0 + sl, :, osl], outs[sl][0][:], [outs[sl][1]], [], track=outs[sl][1])

    def kv_part(tsl, hi):
        for cc in range(2):
            p.dma("sp", ckv[:, cc, :], ut[3 + cc, :, tsl], [], [r_ckv])
        p.dma("sp", krw[64:96, :], ut[5, 64:96, tsl], [], [r_krw])
        lat_norm(ckv, r_ckv, 2, gkva_t, r_gkva, ckvn, r_ckvn, 256)
        p.ts("dve", kq[64:96, :], krw[64:96, :], gkn_t[64:96, 0:1], None, ALU.mult, None, [r_krw, r_gkn], [r_kq])
        rope_apply(kq, r_kq, krr[64:96, :], r_krr, slice(64, 96))
        for sl in range(G):
            p.act(sqs[sl][0][64:96, :], krw[64:96, :], AF.Square, [r_krw], [sqs[sl][1]])
        for g0 in range(0, 12, G):
            for sl in range(G):
                h = g0 + sl
                k = kA[0] % 2
                kA[0] += 1
                for cc in range(2):
                    p.mm(psA[k][0:64, :], wukv[:, cc, h * 128:h * 128 + 64], ckvn[:, cc, :], cc == 0, cc == 1, [r_wukv, r_ckvn], [r_psA[k]])
                p.act(sqs[sl][0][0:64, :], psA[k][0:64, :], AF.Square, [r_psA[k]], [sqs[sl][1]])
                p.copy("act", raw[sl][0][0:64, :], psA[k][0:64, :], [r_psA[k]], [raw[sl][1]])
            for sl in range(G):
                head_norm(sl)
            for sl in range(G):
                ks_t, r_ks = outs[sl]
                rs_t, r_rs = rsl[sl]
                p.stt("dve", ks_t[0:64, :], raw[sl][0][0:64, :], gkn_t[0:64, 0:1], rs_t[0:64, :], ALU.mult, ALU.mult,
                      [raw[sl][1], r_gkn, r_rs], [r_ks])
                p.tt("dve", ks_t[64:96, :], krr[64:96, :], rs_t[64:96, :], ALU.mult, [r_krr, r_rs], [r_ks])
                p.dma("sp", KTo[g0 + sl, :, tsl], ks_t[:], [r_ks], [], track=r_ks)
        for tt_ in range(4):
            for g in range(2):
                for cc in range(2):
                    p.mm(psV[g][:].rearrange("p (h t) -> p h t", t=64), ckvn[:, cc, tt_ * 128:(tt_ + 1) * 128],
                         wukv_h[:, cc, g * 6:(g + 1) * 6, 64:128], cc == 0, cc == 1, [r_wukv, r_ckvn], [r_psV[g]])
                pv3 = psV[g][:].rearrange("p (h t) -> p h t", t=64)
                p.copy("act", vst[:, g * 6:(g + 1) * 6:2, tt_, 0:64], pv3[:, 0:6:2, :], [r_psV[g]], [r_vst])
                p.copy("act", vst[:, g * 6 + 1:(g + 1) * 6:2, tt_, 64:128], pv3[:, 1:6:2, :], [r_psV[g]], [r_vst])
        for h in range(12):
            p.dma("sp", Vo[h, :, hi * 4:(hi + 1) * 4, :], vst[:, h, :, :], [r_vst], [], track=r_vst)

    for hi in range(NHALF):
        tsl = slice(hi * 512, (hi + 1) * 512)
        tables([tsl])
        if own is None:
            q_part([tsl], tsl)
        kv_part(tsl, hi)
    if own is not None:
        for jq in range(TOKQ // 512):
            cands = [slice(qi * TOKQ + jq * 512, qi * TOKQ + (jq + 1) * 512) for qi in range(NQ)]
            tables(cands)
            q_part(cands, slice(jq * 512, (jq + 1) * 512))


def build_attn(p, TOK, S):
    NQB = TOK // 512
    NKT = S // 128
    qT = p.dram_in("qT", [12, 96, TOK], BF16)
    KT = p.dram_in("KT", [12, 96, S], BF16)
    Va = p.dram_in("Va", [12, 128, NKT, 128], BF16)
    out = p.dram_out("attn", [6, 128, TOK], BF16)
    r_out = p.reg("out")
    Kb = [p.sb(f"Kb{i}", [96, S], BF16) for i in range(2)]
    Vb = [p.sb(f"Vb{i}", [128, NKT, 128], BF16) for i in range(2)]
    qb_ = [p.sb(f"qb{i}", [96, TOK], BF16) for i in range(2)]
    r_K, r_V, r_q = p.regs(2, "K"), p.regs(2, "V"), p.regs(2, "q")
    NS = 4
    ps_s = [p.ps(f"ps_s{i}", [128, 512]) for i in range(NS)]
    r_pss = p.regs(NS, "pss")
    ps_o = [p.ps(f"ps_o{i}", [128, 512]) for i in range(2)]
    r_pso = p.regs(2, "pso")
    NP = 3
    pT = [p.sb(f"pT{i}", [128, 512], BF16) for i in range(NP)]
    r_pT = p.regs(NP, "pT")
    rec = [p.sb(f"rec{i}", [128, 512], F32) for i in range(2)]
    r_rec = p.regs(2, "rec")
    ost = [p.sb(f"ost{i}", [128, 512], BF16) for i in range(2)]
    r_ost = p.regs(2, "ost")
    SCALE = float(96 ** -0.5)
    LA = 2

    def load(h):
        b = h % 2
        p.dma("sp", Kb[b][:], KT[h], [], [r_K[b]])
        p.dma("sp", Vb[b][:], Va[h], [], [r_V[b]])
        p.dma("sp", qb_[b][:], qT[h], [], [r_q[b]])

    its = [(h, qb, kt) for h in range(12) for qb in range(NQB) for kt in range(NKT)]
    N = len(its)

    def rec_S(i):
        h, qb, kt = its[i]
        b = h % 2
        p.mm(ps_s[i % NS][:], Kb[b][:, kt * 128:(kt + 1) * 128], qb_[b][:, qb * 512:(qb + 1) * 512], True, True,
             [r_K[b], r_q[b]], [r_pss[i % NS]])

    load(0)
    loaded = 0
    for i in range(min(LA, N)):
        rec_S(i)
    for i in range(N):
        h, qb, kt = its[i]
        b = h % 2
        if kt == 0 and qb == 0 and h + 1 < 12 and loaded < h + 1:
            load(h + 1)
            loaded = h + 1
        if i + LA < N:
            rec_S(i + LA)
        p.act(pT[i % NP][:], ps_s[i % NS][:], AF.Exp, [r_pss[i % NS]], [r_pT[i % NP]], scale=SCALE)
        o = (h * NQB + qb) % 2
        p.mm(ps_o[o][:], Vb[b][:, kt, :], pT[i % NP][:], kt == 0, kt == NKT - 1, [r_V[b], r_pT[i % NP]], [r_pso[o]])
        if kt == NKT - 1:
            base = (h % 2) * 64
            oth = 64 - base
            p.recip(rec[o][oth:oth + 64, :], ps_o[o][oth:oth + 64, :], [r_pso[o]], [r_rec[o]])
            p.tt("dve", ost[o][base:base + 64, :], ps_o[o][base:base + 64, :], rec[o][oth:oth + 64, :], ALU.mult,
                 [r_pso[o], r_rec[o]], [r_ost[o]])
            p.dma("sp", out[h // 2, base:base + 64, qb * 512:(qb + 1) * 512], ost[o][base:base + 64, :], [r_ost[o]], [],
                  track=r_ost[o])


def build_fused(S, depth=4, max_phases=None, only=None, NQ=4):
    p = Prog()
    nph = [0]

    def more():
        nph[0] += 1
        if only is not None:
            return nph[0] in only
        return max_phases is None or nph[0] <= max_phases
    nl, nm = (depth + 1) // 2, depth // 2

    def ein(name, shape, dt=F32):
        return p.ext_in(name, shape, dt)
    xT = ein("xT", [D, S])
    memT = ein("memT", [D, 256])
    pos = ein("pos", [1, S], I32)
    g1 = ein("ffn1_norm", [depth, 128, 8])
    gu1 = ein("ffn1_w_gate_up", [depth, D, 2 * DFF])
    dn1 = ein("ffn1_w_down", [depth, DFF, D])
    gmix = ein("mix_norm", [depth, 128, 8])
    gmem = ein("mem_norm", [depth, 128, 8])
    wkv = ein("w_mem_kv", [depth, D, 512])
    gq = ein("mem_q_norm", [depth, 128, 1])
    gk = ein("mem_k_norm", [depth, 128, 1])
    wo = ein("w_out", [depth, D, D])
    g2 = ein("ffn2_norm", [depth, 128, 8])
    gu2 = ein("ffn2_w_gate_up", [depth, D, 2 * DFF])
    dn2 = ein("ffn2_w_down", [depth, DFF, D])
    lwin = ein("lru_w_in", [nl, D, 1792])
    lcw = ein("lru_cw", [nl, 96, 8, 4])
    lcb = ein("lru_cb", [nl, 96, 8])
    lgw = ein("lru_gw", [nl, 96, 8, 4, 96])
    lgb = ein("lru_gb", [nl, 96, 8, 4])
    llam = ein("lru_lam", [nl, 96, 16])
    mwin = ein("mla_w_in", [nm, D, 928])
    mgqa = ein("mla_q_a_norm", [nm, 128, 3])
    mwuq = ein("mla_w_uq", [nm, 384, 1152])
    mgkva = ein("mla_kv_a_norm", [nm, 128, 2])
    mwukv = ein("mla_w_ukv", [nm, 256, 1536])
    mgqn = ein("mla_q_norm", [nm, 96, 1])
    mgkn = ein("mla_k_norm", [nm, 96, 1])
    cst = {k: ein(k, list(v.shape)) for k, v in mla_consts().items()}
    last_own = (depth % 2 == 0)
    TOKQ = S // NQ
    out = p.ext_out("xT_out", [D, TOKQ if last_own else S], F32)
    qmask = ein("qmask", [128, NQ])
    xs = p.dram_tmp("xs", [D, S], F32)
    utok_l = p.dram_tmp("utok_l", [16, 96, S], F32)
    utok_m = p.dram_tmp("utok_m", [6, 128, S], F32)
    umem = p.dram_tmp("umem", [2, 128, S], F32)
    tok_l = p.dram_tmp("tok_l", [8, 96, S], BF16)
    attn = p.dram_tmp("attn_s", [6, 128, S], BF16)
    qTs = p.dram_tmp("qT_s", [12, 96, S], BF16)
    KTs = p.dram_tmp("KT_s", [12, 96, S], BF16)
    Vas = p.dram_tmp("Va_s", [12, 128, S // 128, 128], BF16)
    for l in range(depth):
        j = l // 2
        kind = "lru" if l % 2 == 0 else "mla"
        xin = xT if l == 0 else xs
        xout = out if l == depth - 1 else xs
        if more():
            p.begin_phase()
            p.io = {"xT": xin, "g1": g1[l], "gm": gmix[l], "w_gu": gu1[l], "w_dn": dn1[l],
                    "w_in": lwin[j] if kind == "lru" else mwin[j], "xT_out": xs,
                    "u_tok": utok_l if kind == "lru" else utok_m, "u_mem": umem}
            build_a(p, S, kind, f"L{l}")
            p.end_phase()
        if kind == "lru":
            if more():
                p.begin_phase()
                p.io = {"xr": utok_l[8:16], "gate": utok_l[0:8], "cw": lcw[j], "cb": lcb[j], "gw": lgw[j], "gb": lgb[j],
                        "lam": llam[j], "tok": tok_l}
                build_lru_b(p, S, 8)
                p.end_phase()
            tk = tok_l
        else:
            ownl = last_own and l == depth - 1
            if more():
                p.begin_phase()
                p.io = {"u_tok": utok_m, "pos": pos, "gqa": mgqa[j], "w_uq": mwuq[j], "gkva": mgkva[j], "w_ukv": mwukv[j],
                        "gqn": mgqn[j], "gkn": mgkn[j], "qT": qTs, "KT": KTs, "Vaug": Vas}
                p.io.update(cst)
                if ownl:
                    p.io["qmask"] = qmask
                build_mla_q(p, S, own=(TOKQ, NQ) if ownl else None)
                p.end_phase()
            if more():
                p.begin_phase()
                if ownl:
                    p.io = {"qT": qTs[:, :, 0:TOKQ], "KT": KTs, "Va": Vas, "attn": attn[:, :, 0:TOKQ]}
                    build_attn(p, TOKQ, S)
                else:
                    p.io = {"qT": qTs, "KT": KTs, "Va": Vas, "attn": attn}
                    build_attn(p, S, S)
                p.end_phase()
            tk = attn
        if more():
            p.begin_phase()
            p.io = {"xT": xs, "u_mem": umem, "tok": tk, "memT": memT, "gmem": gmem[l], "w_mem_kv": wkv[l], "gq": gq[l],
                    "gk": gk[l], "w_out": wo[l], "g2": g2[l], "w_gu": gu2[l], "w_dn": dn2[l], "xT_out": xout}
            if last_own and l == depth - 1:
                p.io["tok"] = tk[:, :, 0:TOKQ]
                p.io["qmask"] = qmask
                build_c(p, TOKQ, kind, f"L{l}", own=(S, NQ))
            else:
                build_c(p, S, kind, f"L{l}")
            p.end_phase()
    return p.finish()


_PROGS = {}


def _f(a):
    return np.ascontiguousarray(np.asarray(a, dtype=np.float32))


def _prep(x, mem, positions, ffn1_norm, ffn1_w_gate_up, ffn1_w_down, mix_norm, mem_norm,
          w_mem_kv, mem_q_norm, mem_k_norm, w_out, ffn2_norm, ffn2_w_gate_up, ffn2_w_down,
          lru_w_in, lru_conv_w, lru_conv_b, lru_gate_w, lru_gate_b, lru_lambda,
          mla_w_in, mla_q_a_norm, mla_w_uq, mla_kv_a_norm, mla_w_ukv, mla_q_norm, mla_k_norm):
    x = np.asarray(x, np.float32)
    mem = np.asarray(mem, np.float32)
    positions = np.asarray(positions, np.int32)
    B, S, _ = x.shape
    depth = np.asarray(ffn1_norm).shape[0]
    nl, nm = (depth + 1) // 2, depth // 2
    QPB = NCORES // B
    TOK = S // QPB

    def lay(g, n):
        g = np.asarray(g, np.float32)
        return np.ascontiguousarray(g.reshape(g.shape[0], n, 128).transpose(0, 2, 1))
    cw = np.asarray(lru_conv_w, np.float32).reshape(nl, 4, 8, 96).transpose(0, 3, 2, 1)
    cb = np.asarray(lru_conv_b, np.float32).reshape(nl, 8, 96).transpose(0, 2, 1)
    gw = np.asarray(lru_gate_w, np.float32).transpose(0, 4, 3, 1, 2, 5).reshape(nl, 96, 8, 4, 96)
    gb = np.asarray(lru_gate_b, np.float32).reshape(nl, 2, 2, 8, 96).transpose(0, 4, 3, 1, 2).reshape(nl, 96, 8, 4)
    lam = np.asarray(lru_lambda, np.float32).reshape(nl, 2, 8, 96).transpose(0, 3, 2, 1).reshape(nl, 96, 16)
    shared = {
        "ffn1_norm": lay(ffn1_norm, 8), "ffn1_w_gate_up": _f(ffn1_w_gate_up), "ffn1_w_down": _f(ffn1_w_down),
        "mix_norm": lay(mix_norm, 8), "mem_norm": lay(mem_norm, 8), "w_mem_kv": _f(w_mem_kv),
        "mem_q_norm": _f(np.tile(np.asarray(mem_q_norm, np.float32), (1, 2)).reshape(depth, 128, 1)),
        "mem_k_norm": _f(np.tile(np.asarray(mem_k_norm, np.float32), (1, 2)).reshape(depth, 128, 1)),
        "w_out": _f(w_out), "ffn2_norm": lay(ffn2_norm, 8), "ffn2_w_gate_up": _f(ffn2_w_gate_up),
        "ffn2_w_down": _f(ffn2_w_down), "lru_w_in": _f(lru_w_in), "lru_cw": _f(cw), "lru_cb": _f(cb), "lru_gw": _f(gw),
        "lru_gb": _f(gb), "lru_lam": _f(lam), "mla_w_in": _f(mla_w_in), "mla_q_a_norm": lay(mla_q_a_norm, 3),
        "mla_w_uq": _f(mla_w_uq), "mla_kv_a_norm": lay(mla_kv_a_norm, 2), "mla_w_ukv": _f(mla_w_ukv),
        "mla_q_norm": _f(np.asarray(mla_q_norm, np.float32).reshape(nm, 96, 1)),
        "mla_k_norm": _f(np.asarray(mla_k_norm, np.float32).reshape(nm, 96, 1)),
    }
    shared.update(mla_consts())
    xTb = [np.ascontiguousarray(x[b].T) for b in range(B)]
    memTb = [np.ascontiguousarray(mem[b].T) for b in range(B)]
    posb = [np.ascontiguousarray(positions[b].reshape(1, S)) for b in range(B)]
    qm = [np.ascontiguousarray(np.tile(np.eye(QPB, dtype=np.float32)[q], (128, 1))) for q in range(QPB)]
    in_maps = [dict(shared, xT=xTb[c // QPB], memT=memTb[c // QPB], pos=posb[c // QPB], qmask=qm[c % QPB])
               for c in range(NCORES)]
    return in_maps, B, S, depth, QPB, TOK


def kernel(**inputs):
    in_maps, B, S, depth, QPB, TOK = _prep(**inputs)
    key = (S, depth)
    if key not in _PROGS:
        _PROGS[key] = build_fused(S, depth)
    res = run_bass_kernel_spmd(_PROGS[key], in_maps, core_ids=list(range(NCORES))).results
    out = np.empty((B, S, D), np.float32)
    for c in range(NCORES):
        b, q = c // QPB, c % QPB
        o = res[c]["xT_out"]
        out[b, q * TOK:(q + 1) * TOK, :] = (o if o.shape[1] == TOK else o[:, q * TOK:(q + 1) * TOK]).T
    return out
```
